# Optimizing a Trainium2 kernel written in Bass

```python
import math
import jax, jax.numpy as jnp
from jax import lax
import numpy as np

D_MODEL = 2048
BATCH = 2
SEQ = 8192
DEPTH = 2

HEAD_DIM = 128
N_MOBA_HEADS = 8
N_FOX_HEADS = 8
MOBA_BLOCK = 256
MOBA_TOPK = 3
MOBA_Q_CHUNK = 64
Q_CHUNK = 128
N_DSA_HEADS = 16
DSA_Q_RANK = 512
DSA_KV_RANK = 512
IDX_HEADS = 16
IDX_DIM = 64
DSA_TOPK_MAX = 256
REL_BUCKETS = 32
REL_MAX_DIST = 128
N_BIAS_HEADS = max(N_MOBA_HEADS, N_DSA_HEADS)
D_FF_DENSE = 5632
N_EXPERTS = 8
MOE_TOPK = 2
D_FF_EXPERT = 7168
EXPERT_ROWS = 256
FORGET_BIAS_CENTER = 2.0
LN_EPS = 1e-5
RMS_EPS = 1e-6
N_EVEN = (DEPTH + 1) // 2
N_ODD = DEPTH // 2
DEEPNORM_ALPHA = (2 * DEPTH) ** 0.25
DEEPNORM_BETA = (8 * DEPTH) ** -0.25
F32 = jnp.float32

kernel_name = 'hybrid_moba_fox_dsa_moe_deepnorm'


def layer_norm(x, g, b):
    xf = x.astype(F32)
    mu = jnp.mean(xf, axis=-1, keepdims=True)
    var = jnp.mean(jnp.square(xf - mu), axis=-1, keepdims=True)
    return ((xf - mu) * lax.rsqrt(var + LN_EPS) * g.astype(F32) + b.astype(F32)).astype(x.dtype)


def rms_norm(x, g):
    xf = x.astype(F32)
    return (xf * lax.rsqrt(jnp.mean(xf * xf, axis=-1, keepdims=True) + RMS_EPS) * g.astype(F32)).astype(x.dtype)


def rel_bucket(dist):
    n = jnp.maximum(dist, 0)
    exact = REL_BUCKETS // 2
    nf = jnp.maximum(n, 1).astype(F32)
    large = exact + (jnp.log(nf / exact) / math.log(REL_MAX_DIST / exact) * (REL_BUCKETS - exact)).astype(jnp.int32)
    large = jnp.minimum(large, REL_BUCKETS - 1)
    return jnp.where(n < exact, n, large)


def rel_bias_per_head(table, bucket):
    n_heads = bucket.shape[1]
    tab = table[:, :n_heads].T.astype(F32)
    hid = jnp.arange(n_heads).reshape((1, n_heads) + (1,) * (bucket.ndim - 2))
    return tab[hid, bucket]


def rel_bias_shared(table, bucket, n_heads):
    return jnp.moveaxis(table[bucket, :n_heads].astype(F32), -1, 0)


def split_heads(z, n_heads):
    b, t, _ = z.shape
    return z.reshape(b, t, n_heads, HEAD_DIM).transpose(0, 2, 1, 3)


def moba_attention(q, k, v, rel_table):
    B, H, T, Dh = q.shape
    nb = -(-T // MOBA_BLOCK)
    tp = nb * MOBA_BLOCK
    pad = ((0, 0), (0, 0), (0, tp - T), (0, 0))
    kp = jnp.pad(k, pad)
    vp = jnp.pad(v, pad)
    kb = kp.reshape(B, H, nb, MOBA_BLOCK, Dh)
    vb = vp.reshape(B, H, nb, MOBA_BLOCK, Dh)
    k_mean = jnp.mean(kb.astype(F32), axis=3)
    n_sel = min(MOBA_TOPK, nb)
    scale = Dh ** -0.5
    blk_ids = jnp.arange(nb)
    offs = jnp.arange(MOBA_BLOCK)
    gather_blocks = jax.vmap(jax.vmap(lambda blocks, idx: blocks[idx]))

    def chunk(ci):
        start = ci * MOBA_Q_CHUNK
        qc = lax.dynamic_slice_in_dim(q, start, MOBA_Q_CHUNK, axis=2)
        t = start + jnp.arange(MOBA_Q_CHUNK)
        own = start // MOBA_BLOCK
        gate = jnp.einsum('bhqd,bhnd->bhqn', qc.astype(F32), k_mean)
        gate = jnp.where(blk_ids < own, gate, -jnp.inf)
        _, sel = lax.top_k(gate, n_sel)
        valid = sel < own
        kg = gather_blocks(kb, sel)
        vg = gather_blocks(vb, sel)
        s_sel = jnp.einsum('bhqd,bhqnld->bhqnl', qc, kg).astype(F32) * scale
        pos = sel[..., None] * MOBA_BLOCK + offs
        s_sel = s_sel + rel_bias_per_head(rel_table, rel_bucket(t[:, None, None] - pos))
        s_sel = jnp.where(valid[..., None], s_sel, -jnp.inf)
        own_start = own * MOBA_BLOCK
        ko = lax.dynamic_slice_in_dim(kp, own_start, MOBA_BLOCK, axis=2)
        vo = lax.dynamic_slice_in_dim(vp, own_start, MOBA_BLOCK, axis=2)
        pos_o = own_start + offs
        s_own = jnp.einsum('bhqd,bhld->bhql', qc, ko).astype(F32) * scale + rel_bias_shared(rel_table, rel_bucket(t[:, None] - pos_o[None, :]), H)
        s_own = jnp.where(pos_o[None, :] <= t[:, None], s_own, -jnp.inf)
        logits = jnp.concatenate([s_sel.reshape(B, H, MOBA_Q_CHUNK, n_sel * MOBA_BLOCK), s_own], axis=-1)
        p = jax.nn.softmax(logits, axis=-1).astype(v.dtype)
        p_sel = p[..., :n_sel * MOBA_BLOCK].reshape(B, H, MOBA_Q_CHUNK, n_sel, MOBA_BLOCK)
        p_own = p[..., n_sel * MOBA_BLOCK:]
        return jnp.einsum('bhqnl,bhqnld->bhqd', p_sel, vg) + jnp.einsum('bhql,bhld->bhqd', p_own, vo)

    outs = lax.map(chunk, jnp.arange(T // MOBA_Q_CHUNK))
    return outs.transpose(1, 0, 3, 2, 4).reshape(B, T, H * Dh)


def forgetting_attention(q, k, v, log_f):
    B, H, T, Dh = q.shape
    csum = jnp.cumsum(log_f, axis=-1)
    scale = Dh ** -0.5
    s_pos = jnp.arange(T)

    def chunk(ci):
        start = ci * Q_CHUNK
        qc = lax.dynamic_slice_in_dim(q, start, Q_CHUNK, axis=2)
        cq = lax.dynamic_slice_in_dim(csum, start, Q_CHUNK, axis=2)
        t = start + jnp.arange(Q_CHUNK)
        s = jnp.einsum('bhqd,bhsd->bhqs', qc, k).astype(F32) * scale + (cq[..., :, None] - csum[:, :, None, :])
        s = jnp.where(s_pos[None, :] <= t[:, None], s, -jnp.inf)
        p = jax.nn.softmax(s, axis=-1).astype(v.dtype)
        return jnp.einsum('bhqs,bhsd->bhqd', p, v)

    outs = lax.map(chunk, jnp.arange(T // Q_CHUNK))
    return outs.transpose(1, 0, 3, 2, 4).reshape(B, T, H * Dh)


def moba_fox_mixer(x, w_in, b_forget, w_out, rel_table):
    wa = N_MOBA_HEADS * HEAD_DIM
    wb = N_FOX_HEADS * HEAD_DIM
    proj = x @ w_in
    cuts = [wa, 2 * wa, 3 * wa, 3 * wa + wb, 3 * wa + 2 * wb, 3 * wa + 3 * wb]
    qa, ka, va, qb, kb, vb, fb = jnp.split(proj, cuts, axis=-1)
    log_f = jax.nn.log_sigmoid(fb.astype(F32) + b_forget.astype(F32)).transpose(0, 2, 1)
    oa = moba_attention(split_heads(qa, N_MOBA_HEADS), split_heads(ka, N_MOBA_HEADS), split_heads(va, N_MOBA_HEADS), rel_table)
    ob = forgetting_attention(split_heads(qb, N_FOX_HEADS), split_heads(kb, N_FOX_HEADS), split_heads(vb, N_FOX_HEADS), log_f)
    return jnp.concatenate([oa, ob], axis=-1) @ w_out


def dsa_mixer(x, w_in, q_norm_g, kv_norm_g, w_uq, w_qidx, w_uk, w_uv, w_out, rel_table):
    B, T, _ = x.shape
    proj = x @ w_in
    c_q, c_kv, k_idx, w_idx = jnp.split(proj, [DSA_Q_RANK, DSA_Q_RANK + DSA_KV_RANK, DSA_Q_RANK + DSA_KV_RANK + IDX_DIM], axis=-1)
    c_q = rms_norm(c_q, q_norm_g)
    c_kv = rms_norm(c_kv, kv_norm_g)
    q_idx = (c_q @ w_qidx).reshape(B, T, IDX_HEADS, IDX_DIM)
    w_idx = w_idx.astype(F32) * IDX_HEADS ** -0.5
    k_sel = min(DSA_TOPK_MAX, T // 4)
    gather_rows = jax.vmap(lambda c, i: c[i])
    s_pos = jnp.arange(T)
    scale = HEAD_DIM ** -0.5

    def chunk(ci):
        start = ci * Q_CHUNK
        t = start + jnp.arange(Q_CHUNK)
        qi = lax.dynamic_slice_in_dim(q_idx, start, Q_CHUNK, axis=1)
        wi = lax.dynamic_slice_in_dim(w_idx, start, Q_CHUNK, axis=1)
        dots = jnp.einsum('bqhd,bsd->bqhs', qi, k_idx).astype(F32) * IDX_DIM ** -0.5
        score = jnp.einsum('bqhs,bqh->bqs', jax.nn.relu(dots), wi)
        score = jnp.where(s_pos[None, None, :] <= t[None, :, None], score, -jnp.inf)
        _, sel = lax.top_k(score, k_sel)
        valid = sel <= t[None, :, None]
        ckv_g = gather_rows(c_kv, sel)
        cq = lax.dynamic_slice_in_dim(c_q, start, Q_CHUNK, axis=1)
        q = (cq @ w_uq).reshape(B, Q_CHUNK, N_DSA_HEADS, HEAD_DIM)
        q_lat = jnp.einsum('bqhd,hrd->bqhr', q, w_uk)
        s = jnp.einsum('bqhr,bqkr->bhqk', q_lat, ckv_g).astype(F32) * scale
        bias = rel_table[rel_bucket(t[None, :, None] - sel), :N_DSA_HEADS].astype(F32)
        s = s + bias.transpose(0, 3, 1, 2)
        s = jnp.where(valid[:, None], s, -jnp.inf)
        p = jax.nn.softmax(s, axis=-1).astype(ckv_g.dtype)
        o_lat = jnp.einsum('bhqk,bqkr->bqhr', p, ckv_g)
        o = jnp.einsum('bqhr,hrd->bqhd', o_lat, w_uv)
        return o.reshape(B, Q_CHUNK, N_DSA_HEADS * HEAD_DIM)

    outs = lax.map(chunk, jnp.arange(T // Q_CHUNK))
    return outs.transpose(1, 0, 2, 3).reshape(B, T, N_DSA_HEADS * HEAD_DIM) @ w_out


def swiglu(x, w1, w3, w2):
    return (jax.nn.silu(x @ w1) * (x @ w3)) @ w2


def moe_swiglu(x, router, w1, w3, w2):
    B, T, D = x.shape
    N = B * T
    xf = x.reshape(N, D)
    logits = (xf @ router).astype(F32)
    top_logits, top_e = lax.top_k(logits, MOE_TOPK)
    gates = jax.nn.softmax(top_logits, axis=-1)
    n_assign = N * MOE_TOPK
    flat_e = top_e.reshape(-1).astype(jnp.int32)
    order = jnp.argsort(flat_e)
    sorted_e = flat_e[order]
    sorted_tok = (order // MOE_TOPK).astype(jnp.int32)
    counts = jnp.bincount(flat_e, length=N_EXPERTS)
    padded = (counts + EXPERT_ROWS - 1) // EXPERT_ROWS * EXPERT_ROWS
    pend = jnp.cumsum(padded)
    pstart = pend - padded
    ustart = jnp.cumsum(counts) - counts
    dest = (pstart[sorted_e] + jnp.arange(n_assign) - ustart[sorted_e]).astype(jnp.int32)
    n_rows = -(-(n_assign + N_EXPERTS * (EXPERT_ROWS - 1)) // EXPERT_ROWS) * EXPERT_ROWS
    n_blk = n_rows // EXPERT_ROWS
    row_tok = jnp.zeros((n_rows,), jnp.int32).at[dest].set(sorted_tok)
    blk_e = jnp.minimum(jnp.searchsorted(pend, jnp.arange(n_blk) * EXPERT_ROWS, side='right'), N_EXPERTS - 1)

    def expert_block(args):
        tok, e = args
        xb = xf[tok]
        h = jax.nn.silu(xb @ w1[e]) * (xb @ w3[e])
        return h @ w2[e]

    rows = lax.map(expert_block, (row_tok.reshape(n_blk, EXPERT_ROWS), blk_e)).reshape(n_rows, D)
    assign_row = jnp.zeros((n_assign,), jnp.int32).at[order].set(dest)
    y = rows[assign_row].reshape(N, MOE_TOPK, D)
    out = jnp.einsum('nkd,nk->nd', y, gates.astype(y.dtype))
    return out.reshape(B, T, D)


def setup_inputs(seed: int = 0) -> dict:
    key = jax.random.key(seed)
    keys = iter(jax.random.split(key, 48))

    def nrm(shape, scale):
        return jax.random.normal(next(keys), shape, F32) * scale

    D = D_MODEL
    beta = DEEPNORM_BETA
    wa = N_MOBA_HEADS * HEAD_DIM
    wb = N_FOX_HEADS * HEAD_DIM
    wc = N_DSA_HEADS * HEAD_DIM
    sd = D ** -0.5
    inp = {}
    inp['x'] = nrm((BATCH, SEQ, D), 1.0)
    inp['rel_table'] = nrm((REL_BUCKETS, N_BIAS_HEADS), 0.3)
    inp['ev_w_in'] = jnp.concatenate([
        nrm((N_EVEN, D, wa), sd), nrm((N_EVEN, D, wa), sd), nrm((N_EVEN, D, wa), beta * sd),
        nrm((N_EVEN, D, wb), sd), nrm((N_EVEN, D, wb), sd), nrm((N_EVEN, D, wb), beta * sd),
        nrm((N_EVEN, D, N_FOX_HEADS), sd)], axis=-1)
    inp['ev_b_forget'] = FORGET_BIAS_CENTER + nrm((N_EVEN, N_FOX_HEADS), 0.1)
    inp['ev_w_out'] = nrm((N_EVEN, wa + wb, D), beta * (wa + wb) ** -0.5)
    inp['ev_ln1_g'] = 1.0 + nrm((N_EVEN, D), 0.01)
    inp['ev_ln1_b'] = nrm((N_EVEN, D), 0.01)
    inp['ev_ffn_w1'] = nrm((N_EVEN, D, D_FF_DENSE), beta * sd)
    inp['ev_ffn_w3'] = nrm((N_EVEN, D, D_FF_DENSE), beta * sd)
    inp['ev_ffn_w2'] = nrm((N_EVEN, D_FF_DENSE, D), beta * D_FF_DENSE ** -0.5)
    inp['ev_ln2_g'] = 1.0 + nrm((N_EVEN, D), 0.01)
    inp['ev_ln2_b'] = nrm((N_EVEN, D), 0.01)
    inp['od_w_in'] = nrm((N_ODD, D, DSA_Q_RANK + DSA_KV_RANK + IDX_DIM + IDX_HEADS), sd)
    inp['od_q_norm_g'] = 1.0 + nrm((N_ODD, DSA_Q_RANK), 0.01)
    inp['od_kv_norm_g'] = 1.0 + nrm((N_ODD, DSA_KV_RANK), 0.01)
    inp['od_w_uq'] = nrm((N_ODD, DSA_Q_RANK, wc), DSA_Q_RANK ** -0.5)
    inp['od_w_qidx'] = nrm((N_ODD, DSA_Q_RANK, IDX_HEADS * IDX_DIM), DSA_Q_RANK ** -0.5)
    inp['od_w_uk'] = nrm((N_ODD, N_DSA_HEADS, DSA_KV_RANK, HEAD_DIM), DSA_KV_RANK ** -0.5)
    inp['od_w_uv'] = nrm((N_ODD, N_DSA_HEADS, DSA_KV_RANK, HEAD_DIM), beta * DSA_KV_RANK ** -0.5)
    inp['od_w_out'] = nrm((N_ODD, wc, D), beta * wc ** -0.5)
    inp['od_ln1_g'] = 1.0 + nrm((N_ODD, D), 0.01)
    inp['od_ln1_b'] = nrm((N_ODD, D), 0.01)
    inp['od_router'] = nrm((N_ODD, D, N_EXPERTS), sd)
    inp['od_exp_w1'] = nrm((N_ODD, N_EXPERTS, D, D_FF_EXPERT), beta * sd)
    inp['od_exp_w3'] = nrm((N_ODD, N_EXPERTS, D, D_FF_EXPERT), beta * sd)
    inp['od_exp_w2'] = nrm((N_ODD, N_EXPERTS, D_FF_EXPERT, D), beta * D_FF_EXPERT ** -0.5)
    inp['od_ln2_g'] = 1.0 + nrm((N_ODD, D), 0.01)
    inp['od_ln2_b'] = nrm((N_ODD, D), 0.01)
    return inp


def reference(x, rel_table, ev_w_in, ev_b_forget, ev_w_out, ev_ln1_g, ev_ln1_b, ev_ffn_w1, ev_ffn_w3, ev_ffn_w2, ev_ln2_g, ev_ln2_b, od_w_in, od_q_norm_g, od_kv_norm_g, od_w_uq, od_w_qidx, od_w_uk, od_w_uv, od_w_out, od_ln1_g, od_ln1_b, od_router, od_exp_w1, od_exp_w3, od_exp_w2, od_ln2_g, od_ln2_b):
    h = x
    for layer in range(DEPTH):
        i = layer // 2
        if layer % 2 == 0:
            mix = moba_fox_mixer(h, ev_w_in[i], ev_b_forget[i], ev_w_out[i], rel_table)
            h = layer_norm(DEEPNORM_ALPHA * h + mix, ev_ln1_g[i], ev_ln1_b[i])
            ff = swiglu(h, ev_ffn_w1[i], ev_ffn_w3[i], ev_ffn_w2[i])
            h = layer_norm(DEEPNORM_ALPHA * h + ff, ev_ln2_g[i], ev_ln2_b[i])
        else:
            mix = dsa_mixer(h, od_w_in[i], od_q_norm_g[i], od_kv_norm_g[i], od_w_uq[i], od_w_qidx[i], od_w_uk[i], od_w_uv[i], od_w_out[i], rel_table)
            h = layer_norm(DEEPNORM_ALPHA * h + mix, od_ln1_g[i], od_ln1_b[i])
            ff = moe_swiglu(h, od_router[i], od_exp_w1[i], od_exp_w3[i], od_exp_w2[i])
            h = layer_norm(DEEPNORM_ALPHA * h + ff, od_ln2_g[i], od_ln2_b[i])
    return h
```

```python
import contextlib, math
import numpy as np
import concourse.bass as bass
import concourse.mybir as mybir
from concourse.bass_utils import run_bass_kernel_spmd

F32 = mybir.dt.float32
BF16 = mybir.dt.bfloat16
AF = mybir.ActivationFunctionType
ALU = mybir.AluOpType

D = 2048
T = 8192
NLOC = 2048
TS = [[j, 7 - j, 8 + j, 15 - j] for j in range(4)]
RANK_OF = {}
LIDX_OF = {}
for _j in range(4):
    for _l, _t in enumerate(TS[_j]):
        RANK_OF[_t] = _j
        LIDX_OF[_t] = _l
ALPHA = 4 ** 0.25
SCALE = 128 ** -0.5
NEG = -30000.0
GROUP4 = [[0, 1, 2, 3], [4, 5, 6, 7]]
GROUP8 = [[0, 1, 2, 3, 4, 5, 6, 7]]
PAIRS = [[0, 4], [1, 5], [2, 6], [3, 7]]
SL0 = 4224
SL1 = 4224


def gpos128(kt):
    i512, m = divmod(kt, 4)
    return RANK_OF[i512] * 16 + LIDX_OF[i512] * 4 + m


class Buf:
    __slots__ = ("w", "pw", "r")

    def __init__(self):
        self.w = None
        self.pw = []
        self.r = []


class Op:
    __slots__ = ("eng", "fn", "deps", "sig", "ev", "dma", "cc")


class Rec:
    EPOCH = 30000

    def __init__(self, nc, n_dma_sems=32):
        self.nc = nc
        self.ops = []
        self.engs = {"pe": nc.tensor, "dve": nc.vector, "act": nc.scalar, "pool": nc.gpsimd, "sp": nc.sync}
        self.n_dma_sems = n_dma_sems
        self.last = {}
        self.open_dma = []
        self.bar_deps = []
        self.bar_pending = set()

    def op(self, eng, fn, reads=(), writes=(), pwrites=(), dma=False, cc=False):
        ops = self.ops
        deps = set()
        for b in reads:
            if b.w is not None:
                deps.add(b.w)
            deps.update(b.pw)
        for b in writes:
            if b.w is not None:
                deps.add(b.w)
            deps.update(b.pw)
            deps.update(b.r)
        for b in pwrites:
            if b.w is not None:
                deps.add(b.w)
            deps.update(b.r)
        if eng in self.bar_pending:
            deps.update(self.bar_deps)
            self.bar_pending.discard(eng)
        i = len(ops)
        o = Op()
        o.eng = eng
        o.fn = fn
        o.dma = dma or cc
        o.cc = cc
        o.sig = False
        o.ev = None
        if eng == "pe" and not o.dma:
            o.deps = [d for d in deps if not (ops[d].eng == "pe" and not ops[d].dma)]
        else:
            o.deps = list(deps)
        ops.append(o)
        for b in writes:
            b.w = i
            b.pw = []
            b.r = []
        for b in pwrites:
            b.pw.append(i)
        for b in reads:
            if b.w == i:
                continue
            if not o.dma:
                b.r = [x for x in b.r if ops[x].dma or ops[x].eng != eng]
            b.r.append(i)
        if o.dma:
            self.open_dma.append(i)
        else:
            self.last[eng] = i
        return i

    def dma(self, eng, out, in_, reads=(), writes=(), pwrites=()):
        return self.op(eng, lambda e: e.dma_start(out=out, in_=in_), reads, writes, pwrites, dma=True)

    def barrier(self):
        self.bar_deps = list(self.last.values()) + list(self.open_dma)
        self.open_dma = []
        self.bar_pending = set(self.engs)

    def emit(self, st, final_wait_ops=()):
        nc = self.nc
        ops = self.ops
        for o in ops:
            for d in o.deps:
                ops[d].sig = True
        for d in final_wait_ops:
            ops[d].sig = True
        dma_sems = [st.enter_context(nc.semaphore(f"dq{i}")) for i in range(self.n_dma_sems)]
        dma_cnt = [0] * self.n_dma_sems
        rr = 0
        esem = {}
        ecnt = {}
        nep = {}
        for e in self.engs:
            esem[e] = st.enter_context(nc.semaphore(f"e_{e}_0"))
            ecnt[e] = 0
            nep[e] = 0
        waited = {e: {} for e in self.engs}
        for o in ops:
            E = self.engs[o.eng]
            need = {}
            for d in o.deps:
                s, v = ops[d].ev
                k = id(s)
                if k not in need or need[k][1] < v:
                    need[k] = (s, v)
            wd = waited[o.eng]
            for k, (s, v) in need.items():
                if wd.get(k, 0) >= v:
                    continue
                E.wait_ge(s, v)
                wd[k] = v
            ins = o.fn(E)
            if o.cc:
                if not hasattr(self, "_ccs"):
                    self._ccs = [st.enter_context(nc.semaphore(f"cc{i}")) for i in range(12)]
                    self._ccn = [0] * 12
                    self._ccr = 0
                q = self._ccr
                self._ccr = (q + 1) % 12
                self._ccn[q] += 1
                ins.then_inc(self._ccs[q])
                o.ev = (self._ccs[q], self._ccn[q])
            elif o.sig or o.dma:
                if o.dma:
                    q = rr
                    rr = (rr + 1) % self.n_dma_sems
                    dma_cnt[q] += 16
                    ins.then_inc(dma_sems[q], 16)
                    o.ev = (dma_sems[q], dma_cnt[q])
                else:
                    if ecnt[o.eng] >= self.EPOCH:
                        nep[o.eng] += 1
                        esem[o.eng] = st.enter_context(nc.semaphore(f"e_{o.eng}_{nep[o.eng]}"))
                        ecnt[o.eng] = 0
                    ecnt[o.eng] += 1
                    ins.then_inc(esem[o.eng], 1)
                    o.ev = (esem[o.eng], ecnt[o.eng])
        for d in final_wait_ops:
            s, v = ops[d].ev
            nc.sync.wait_ge(s, v)


class Arena:
    def __init__(self, t, cap16):
        self.t = t
        self.cap = cap16
        self.off = 0

    def reset(self):
        self.off = 0

    def alloc(self, shape, dt, parts=128):
        n = 1
        for s in shape:
            n *= s
        n16 = n * (2 if dt == F32 else 1)
        n16 = (n16 + 1) // 2 * 2
        o = self.off
        self.off += n16
        assert self.off <= self.cap, (self.off, self.cap)
        ap = self.t[0:parts, o:o + n16]
        if dt == F32:
            ap = ap.bitcast(F32)
        if dt != F32 and n16 != n:
            ap = ap[:, 0:n]
        if len(shape) == 2:
            ap = ap.rearrange("p (a b) -> p a b", b=shape[1])
        elif len(shape) == 3:
            ap = ap.rearrange("p (a b c) -> p a b c", b=shape[1], c=shape[2])
        return ap


def pack_layout(shapes):
    off = 0
    lay = {}
    for name, K, N in shapes:
        lay[name] = (off, K, N)
        off += K * N
        off = (off + 1023) // 1024 * 1024
    rows = off // 1024
    rows = (rows + 8 * 1024 - 1) // (8 * 1024) * (8 * 1024)
    return lay, rows


class Builder:
    def __init__(self, F0, FE, stop_after=None):
        self.F0 = F0
        self.FE = FE
        self.stop_after = stop_after
        self.shapes = [
            ("w_in0", D, 6152), ("w_out0", D, D), ("w1", D, F0), ("w3", D, F0), ("w2", F0, D),
            ("w_in1", D, 1104), ("w_uq", 512, D), ("w_qidx", 512, 1024), ("w_uk", 512, D), ("w_uv", 512, D),
            ("w_out1", D, D), ("router", D, 8),
        ]
        self.lay, self.NR = pack_layout(self.shapes)
        self.NRS = self.NR // 8

    def wview(self, name):
        off, K, N = self.lay[name]
        flat = self.WG.rearrange("r c -> (r c)")
        return flat[off:off + K * N].rearrange("(k n) -> k n", n=N)

    def mm(self, out, lhsT, rhs, start, stop, reads, writes, pw=()):
        self.R.op("pe", lambda e: e.matmul(out, lhsT=lhsT, rhs=rhs, start=start, stop=stop), reads, writes, pw)

    def tr(self, out, in_, ident, reads, writes, pw=()):
        self.R.op("pe", lambda e: e.transpose(out=out, in_=in_, identity=ident), reads, writes, pw)

    def act(self, out, in_, func, reads, writes, bias=None, scale=None, eng="act", pw=()):
        kw = {}
        if bias is not None:
            kw["bias"] = bias
        if scale is not None:
            kw["scale"] = scale
        self.R.op(eng, lambda e: e.activation(out=out, in_=in_, func=func, **kw), reads, writes, pw)

    def copy(self, eng, out, in_, reads, writes, pw=()):
        if eng == "act":
            self.R.op("act", lambda e: e.copy(out=out, in_=in_), reads, writes, pw)
        else:
            self.R.op(eng, lambda e: e.tensor_copy(out=out, in_=in_), reads, writes, pw)

    def tt(self, eng, out, in0, in1, op, reads, writes, pw=()):
        self.R.op(eng, lambda e: e.tensor_tensor(out=out, in0=in0, in1=in1, op=op), reads, writes, pw)

    def ts(self, eng, out, in0, s1, s2, op0, op1, reads, writes, accum_out=None, pw=()):
        if op1 is None:
            self.R.op(eng, lambda e: e.tensor_scalar(out=out, in0=in0, scalar1=s1, scalar2=None, op0=op0), reads, writes, pw)
        elif accum_out is None:
            self.R.op(eng, lambda e: e.tensor_scalar(out=out, in0=in0, scalar1=s1, scalar2=s2, op0=op0, op1=op1), reads, writes, pw)
        else:
            self.R.op(eng, lambda e: e.tensor_scalar(out=out, in0=in0, scalar1=s1, scalar2=s2, op0=op0, op1=op1, accum_out=accum_out), reads, writes, pw)

    def stt(self, eng, out, in0, scalar, in1, op0, op1, reads, writes, pw=()):
        self.R.op(eng, lambda e: e.scalar_tensor_tensor(out=out, in0=in0, scalar=scalar, in1=in1, op0=op0, op1=op1), reads, writes, pw)

    def dyn(self, eng, fn, reads, writes, pw=()):
        self.R.op(eng, fn, reads, writes, pw, dma=True)

    def memset(self, eng, ap, val, writes):
        self.R.op(eng, lambda e: e.memset(ap, val), (), writes)

    def dyn_put(self, eng, dram, rows_per_slot, row0, nrows, cols, src, group, reads, pwrites):
        c0, c1 = cols

        def fn(e):
            r = e.partition_id() % group
            return e.dma_start(out=dram[bass.ds(r * rows_per_slot + row0, nrows), c0:c1], in_=src)
        self.R.op(eng, fn, reads, (), pwrites, dma=True)

    def allreduce(self, groups, src, dst, reads, outbuf, esize=2):
        rows, cols = src.shape[0], src.shape[1]
        per = max(1, (4 * 1024 * 1024 // esize) // cols)
        for r0 in range(0, rows, per):
            n = min(per, rows - r0)
            self.R.op("pool", (lambda e, r0=r0, n=n: e.collective_compute(
                "AllReduce", ALU.add, replica_groups=groups, ins=[src[r0:r0 + n, :].opt()], outs=[dst[r0:r0 + n, :].opt()])),
                reads, (), (), cc=True)
            outbuf.pw.append(len(self.R.ops) - 1)

    def allreduce8(self, src, tmp, dst, reads, outbuf, esize=2):
        mid = Buf()
        self.allreduce(GROUP4, src, tmp, reads, mid, esize)
        self.allreduce(PAIRS, tmp, dst, [mid], outbuf, esize)

    def zero_fill(self, dram, rows, cols, bufs):
        if cols > 8192:
            for r in range(0, rows, 128):
                for c in range(0, cols, 8192):
                    w = min(8192, cols - c)
                    b = Buf()
                    self.R.dma("sp", dram[r:r + 128, c:c + w], self.zt[:, 0:w], reads=[self.b_zt], writes=[b])
                    bufs.append(b)
            return
        per = 8192 // cols * 128
        r = 0
        while r < rows:
            n = min(per, rows - r)
            b = Buf()
            a = n // 128
            self.R.dma("sp", dram[r:r + n, :].rearrange("(p a) c -> p (a c)", p=128), self.zt[:, 0:a * cols],
                       reads=[self.b_zt], writes=[b])
            bufs.append(b)
            r += n

    def layer_norm(self, z, gt, bt, out, bz, bout, tmpst, reads_extra=()):
        st6, mv, rs = tmpst
        bs = Buf()
        for c in range(4):
            self.R.op("dve", (lambda e, c=c: e.bn_stats(out=st6[:, c, :], in_=z[:, c * 512:(c + 1) * 512])), [bz], [bs])
        self.R.op("dve", lambda e: e.bn_aggr(out=mv, in_=st6), [bs], [bs])
        self.ts("dve", rs[:, 0:1], mv[:, 1:2], 1e-5, None, ALU.add, None, [bs], [bs])
        self.act(rs[:, 1:2], rs[:, 0:1], AF.Sqrt, [bs], [bs])
        self.R.op("dve", lambda e: e.reciprocal(out=rs[:, 2:3], in_=rs[:, 1:2]), [bs], [bs])
        self.ts("dve", out, z, mv[:, 0:1], rs[:, 2:3], ALU.subtract, ALU.mult, [bz, bs], [bout])
        self.tt("pool", out, out, gt, ALU.mult, [bout] + list(reads_extra), [bout])
        self.tt("dve", out, out, bt, ALU.add, [bout] + list(reads_extra), [bout])


PADT = 1536
NT = 20
PT3 = 3
OFF_K = 0
OFF_V = 2048 * 512
OFF_X = 2 * 2048 * 512
SLOT = 2146304
PUTB = [(3, 0), (7, 1), (11, 0), (15, 1)]
TSC = [(0, 1), (7, -1), (8, 1), (15, -1)]
CSTW = 5120


class Prog(Builder):
    def build(self):
        nc = bass.Bass("TRN2", target_bir_lowering=False)
        self.nc = nc
        F0, FE = self.F0, self.FE
        dt = nc.dram_tensor
        self.x_loc = dt("x_loc", [NLOC, D], F32, kind="ExternalInput").ap()
        self.wsh = dt("wsh", [self.NRS, 1024], F32, kind="ExternalInput").ap()
        self.vecs = dt("vecs", [16, D], F32, kind="ExternalInput").ap()
        self.bfor = dt("bfor", [1, 8], F32, kind="ExternalInput").ap()
        self.tab31 = dt("tab31", [1, 16], F32, kind="ExternalInput").ap()
        self.tbraw = dt("tbraw", [128, 16 * 384], F32, kind="ExternalInput").ap()
        self.cst = dt("cst", [128, CSTW], F32, kind="ExternalInput").ap()
        self.ew1 = dt("ew1", [D, FE], F32, kind="ExternalInput").ap()
        self.ew3 = dt("ew3", [D, FE], F32, kind="ExternalInput").ap()
        self.ew2 = dt("ew2", [FE, D], F32, kind="ExternalInput").ap()
        self.y = dt("y", [NLOC, D], F32, kind="ExternalOutput").ap()
        self.WZ = dt("WZ", [self.NR, 1024], BF16).ap()
        self.WG = dt("WG", [self.NR, 1024], BF16).ap()
        self.WT = dt("WT", [self.NR, 1024], BF16).ap()
        self.EXL = dt("EXL", [4, SLOT], BF16).ap()
        self.EXZ = dt("EXZ", [NT, SLOT], BF16).ap()
        self.EXG = dt("EXG", [NT, SLOT], BF16).ap()
        self.WIN = [dt(f"WIN{l}", [4 * (l + 1), SLOT], BF16).ap() for l in range(4)]
        self.CTS = dt("CTS", [NT, SLOT], BF16).ap()
        self.WZ3 = self.WZ.rearrange("(s r) c -> s r c", s=8)
        self.QI = dt("QI", [1024, NLOC], BF16).ap()
        self.WS = dt("WS", [NLOC, 32], F32).ap()
        self.MK = [dt(f"MK{l}", [16 * (l + 1), 128, 512], BF16).ap() for l in range(4)]
        self.GT = dt("GT", [NLOC, 8], F32).ap()
        self.HZ = dt("HZ", [8 * D, NLOC], BF16).ap()
        self.HT8 = dt("HT8", [8 * D, NLOC], BF16).ap()
        self.HG = dt("HG", [8 * D, NLOC], BF16).ap()
        self.GZ = dt("GZ", [8 * NLOC, 8], F32).ap()
        self.GT8 = dt("GT8", [8 * NLOC, 8], F32).ap()
        self.GG = dt("GG", [8 * NLOC, 8], F32).ap()
        self.EW1 = dt("EW1", [D, FE], BF16).ap()
        self.EW3 = dt("EW3", [D, FE], BF16).ap()
        self.EW2 = dt("EW2", [FE, D], BF16).ap()
        self.MO = dt("MO", [8 * NLOC, D], F32).ap()
        self.MT8 = dt("MT8", [8 * NLOC, D], F32).ap()
        self.MS = dt("MS", [8 * NLOC, D], F32).ap()
        self.FF = dt("FF", [NLOC, D], F32).ap()
        self.QT0 = dt("QT0", [2048, NLOC], BF16).ap()
        self.ATT = dt("ATT", [NLOC, 2048], BF16).ap()
        self.HM = dt("HM", [NLOC, D], F32).ap()
        self.HMT = dt("HMT", [D, NLOC], BF16).ap()
        self.H1 = dt("H1", [NLOC, D], F32).ap()
        self.H1T = dt("H1T", [D, NLOC], BF16).ap()
        with contextlib.ExitStack() as st:
            self.st = st
            ar_t = st.enter_context(nc.sbuf_tensor("arena", [128, 84 * 1024], BF16))
            cn_t = st.enter_context(nc.sbuf_tensor("consts", [128, 10 * 1024], BF16))
            self.A = Arena(ar_t, 84 * 1024)
            self.C = Arena(cn_t, 10 * 1024)
            self.ps = [st.enter_context(nc.psum_tensor(f"ps{i}", [128, 512], F32)) for i in range(8)]
            self.pb = [Buf() for _ in range(8)]
            self.R = Rec(nc)
            self.consts()
            self.phase_w()
            sa = self.stop_after
            if sa == "w":
                last = self.copy_out(self.WG[0:NLOC, :].bitcast(F32).rearrange("r (a c) -> (r a) c", c=2048) if False else self.x_loc)
            else:
                self.phase_a()
                if sa == "a":
                    last = self.copy_out(self.x_loc)
                else:
                    self.phase_b()
                    if sa == "b":
                        last = self.copy_out(self.x_loc)
                    else:
                        self.phase_c1("w_out0", self.x_loc, 0, self.HM, self.HMT)
                        self.phase_c2()
                        if sa == 0:
                            last = self.copy_out(self.H1)
                        else:
                            last = self.layer1()
            self.R.emit(st, final_wait_ops=last)
        return nc

    def copy_out(self, src):
        R, A = self.R, self.A
        R.barrier()
        A.reset()
        t = [A.alloc([D], F32) for _ in range(2)]
        b = [Buf(), Buf()]
        last = []
        for tt in range(16):
            k = tt % 2
            R.dma("sp", t[k], src[tt * 128:(tt + 1) * 128, :], (), [b[k]])
            last.append(R.dma("sp", self.y[tt * 128:(tt + 1) * 128, :], t[k], [b[k]], ()))
        return last

    def consts(self):
        C, R = self.C, self.R
        self.zt = C.alloc([8192], BF16)
        self.b_zt = Buf()
        self.memset("pool", self.zt, 0.0, [self.b_zt])
        self.b_c = Buf()
        self.identb = C.alloc([128], BF16)
        self.identf = C.alloc([128], F32)
        self.memset("pool", self.identb, 0.0, [self.b_c])
        R.op("pool", lambda e: e.affine_select(out=self.identb, in_=self.identb, pattern=[[-1, 128]],
                                               compare_op=ALU.not_equal, fill=1.0, base=0, channel_multiplier=1),
             [self.b_c], [self.b_c])
        self.copy("dve", self.identf, self.identb, [self.b_c], [self.b_c])
        self.ones1 = C.alloc([1], F32)
        self.memset("dve", self.ones1, 1.0, [self.b_c])
        self.t31 = C.alloc([16], F32)
        R.dma("sp", self.t31, self.tab31.partition_broadcast(128), (), [self.b_c])
        self.bf_t = C.alloc([8], F32)
        R.dma("sp", self.bf_t, self.bfor.partition_broadcast(128), (), [self.b_c])
        self.CA = C.alloc([512], BF16)
        self.memset("pool", self.CA, 0.0, [self.b_c])
        R.op("pool", lambda e: e.affine_select(out=self.CA, in_=self.CA, pattern=[[1, 512]],
                                               compare_op=ALU.is_ge, fill=NEG, base=0, channel_multiplier=-1),
             [self.b_c], [self.b_c])

    def dynreg(self, e, kind):
        if not hasattr(self, "_dr"):
            self._dr = {}
        key = (id(e), kind)
        if key not in self._dr:
            if kind == "R8":
                v = e.partition_id()
            elif kind == "R":
                v = e.partition_id() % 4
            else:
                v = self.dynreg(e, "R") * (-1) + 3
            self._dr[key] = e.snap(v)
        return self._dr[key]

    def put_slots(self, reads):
        self.b_exz = Buf()
        for l in range(4):
            base, sel = PUTB[l]

            def fn(e, l=l, base=base, sel=sel):
                reg = self.dynreg(e, "R" if sel == 0 else "RP")
                return e.dma_start(out=self.EXZ[base:NT, :][bass.ds(reg, 1), :].rearrange("a (p c) -> p (a c)", p=128),
                                   in_=self.EXL[l:l + 1, :].rearrange("a (p c) -> p (a c)", p=128))
            self.dyn("pool", fn, reads + self.ex_zb, (), [self.b_exz])

    def exchange(self):
        R = self.R
        self.b_exg = Buf()
        self.allreduce(GROUP4, self.EXZ.rearrange("a (p c) -> (a p) c", p=128), self.EXG.rearrange("a (p c) -> (a p) c", p=128),
                       [self.b_exz] + self.ex_zb, self.b_exg)
        self.b_win = [Buf() for _ in range(4)]
        for l in range(4):
            n = 4 * (l + 1)

            def fn(e, l=l, n=n):
                reg = self.dynreg(e, "R" if l % 2 == 0 else "RP")
                return e.dma_start(out=self.WIN[l].rearrange("a (p c) -> p a c", p=128),
                                   in_=self.EXG[bass.ds(reg, n), :].rearrange("a (p c) -> p a c", p=128))
            self.dyn("sp", fn, [self.b_exg], [self.b_win[l]])

    def phase_w(self):
        R = self.R
        zb = []
        self.zero_fill(self.WZ, self.NR, 1024, zb)
        self.ex_zb = []
        exz2 = self.EXZ.rearrange("a (p c) -> (a p) c", p=128)
        self.zero_fill(exz2, NT * 128, SLOT // 128, self.ex_zb)
        bput = Buf()
        NRS = self.NRS
        def fn(e):
            r = self.dynreg(e, "R8")
            return e.dma_start(out=self.WZ3[bass.ds(r, 1), :, :].rearrange("s r c -> (s r c)").rearrange("(p x) -> p x", p=128),
                               in_=self.wsh.rearrange("r c -> (r c)").rearrange("(p x) -> p x", p=128))
        self.dyn("pool", fn, zb, (), [bput])
        self.b_wg = Buf()
        self.allreduce8(self.WZ, self.WT, self.WG, [bput] + zb, self.b_wg)

    def transpose16(self, src_bf, b_src, dstT, col0, b_dst):
        for half in range(2):
            pi = 6 + half
            pv = self.ps[pi][:, :].bitcast(BF16)
            for q in range(8):
                dc = half * 8 + q
                self.tr(pv[:, q * 128:(q + 1) * 128], src_bf[:, dc * 128:(dc + 1) * 128], self.identb,
                        [b_src, self.b_c], [self.pb[pi]])
            dst = dstT[:, half * 8:(half + 1) * 8, col0:col0 + 128]
            src = pv.rearrange("p (a b) -> p a b", b=128)
            if half == 0:
                self.copy("dve", dst, src, [self.pb[pi]], [b_dst])
            else:
                self.copy("act", dst, src, [self.pb[pi]], (), [b_dst])

    def phase_a(self):
        R, A = self.R, self.A
        A.reset()
        xT = A.alloc([16, NLOC], BF16)
        b_xT = [Buf() for _ in range(16)]
        xs = [A.alloc([D], F32) for _ in range(2)]
        b_xs = [Buf(), Buf()]
        xb = [A.alloc([D], BF16) for _ in range(2)]
        b_xb = [Buf(), Buf()]
        for tt in range(16):
            k = tt % 2
            R.dma("sp", xs[k], self.x_loc[tt * 128:(tt + 1) * 128, :], (), [b_xs[k]])
            self.copy("act", xb[k], xs[k], [b_xs[k]], [b_xb[k]])
            self.transpose16(xb[k], b_xb[k], xT, tt * 128, b_xT[tt])
        W = self.wview("w_in0")
        slab = [A.alloc([16, 512], BF16) for _ in range(2)]
        b_slab = [Buf(), Buf()]
        stg = [A.alloc([512], BF16) for _ in range(4)]
        b_stg = [Buf() for _ in range(4)]
        kms = A.alloc([8, 8], F32)
        kmb = A.alloc([8, 8], BF16)
        b_km = Buf()
        lfs = A.alloc([16, 8], F32)
        lft = A.alloc([2, 8], F32)
        b_lf = Buf()
        b_lft = Buf()
        self.b_exl = Buf()
        self.b_qt0 = Buf()
        EXLk = self.EXL[:, OFF_K:OFF_K + 2048 * 512].rearrange("l (f c) -> l f c", c=512)
        EXLv = self.EXL[:, OFF_V:OFF_V + 2048 * 512].rearrange("l (h p c) -> l h p c", h=16, p=128)
        lf3 = A.alloc([3, 16, 8], BF16)
        lfr = A.alloc([16, 8], F32)
        fm = [(0, "q", 0), (512, "q", 4), (1024, "k", 0), (1536, "k", 4),
              (3072, "q", 8), (3584, "q", 12), (4096, "k", 8), (4608, "k", 12)]
        tm = [(2048, 0), (2560, 4), (5120, 8), (5632, 12)]
        slabs = [(c, "fm", kind, hb) for c, kind, hb in fm] + [(c, "tm", None, hb) for c, hb in tm] + [(6144, "f", None, 0)]

        def load_slab(i):
            col0, typ = slabs[i][0], slabs[i][1]
            k = i % 2
            if typ == "f":
                R.dma("sp", slab[k][:, :, 0:8], W[:, 6144:6152].rearrange("(a p) n -> p a n", p=128), [self.b_wg], [b_slab[k]])
            else:
                R.dma("sp", slab[k], W[:, col0:col0 + 512].rearrange("(a p) n -> p a n", p=128), [self.b_wg], [b_slab[k]])
        load_slab(0)
        pcnt = 0
        scnt = 0
        for i, (col0, typ, kind, hb) in enumerate(slabs):
            k = i % 2
            if i + 1 < len(slabs):
                load_slab(i + 1)
            if typ == "fm":
                for l in range(4):
                    for hc in range(4):
                        pi = pcnt % 4
                        pcnt += 1
                        for dc in range(16):
                            self.mm(self.ps[pi][:, :], slab[k][:, dc, hc * 128:(hc + 1) * 128], xT[:, dc, l * 512:(l + 1) * 512],
                                    dc == 0, dc == 15, [b_slab[k]] + b_xT[l * 4:l * 4 + 4], [self.pb[pi]])
                        si = scnt % 4
                        scnt += 1
                        h = hb + hc
                        feat0 = h * 128
                        if kind == "q":
                            self.act(stg[si], self.ps[pi][:, :], AF.Copy, [self.pb[pi]], [b_stg[si]], scale=SCALE)
                            R.dma("pool", self.QT0[feat0:feat0 + 128, l * 512:(l + 1) * 512], stg[si], [b_stg[si]], (), [self.b_qt0])
                        else:
                            self.copy("dve", stg[si], self.ps[pi][:, :], [self.pb[pi]], [b_stg[si]])
                            if h < 8:
                                R.op("dve", (lambda e, h=h, l=l, pi=pi: e.reduce_sum(
                                    out=kms[:, h, 2 * l:2 * l + 2],
                                    in_=self.ps[pi][:, :].rearrange("p (a b) -> p a b", b=256),
                                    axis=mybir.AxisListType.X)), [self.pb[pi]], (), [b_km])

                            R.dma("pool", EXLk[l, feat0:feat0 + 128, :], stg[si], [b_stg[si]], (), [self.b_exl])
            elif typ == "tm":
                for tt in range(16):
                    pi = pcnt % 4
                    pcnt += 1
                    for dc in range(16):
                        self.mm(self.ps[pi][:, :], xT[:, dc, tt * 128:(tt + 1) * 128], slab[k][:, dc, :],
                                dc == 0, dc == 15, [b_slab[k], b_xT[tt]], [self.pb[pi]])
                    si = scnt % 4
                    scnt += 1
                    self.copy("act" if tt % 2 else "dve", stg[si], self.ps[pi][:, :], [self.pb[pi]], [b_stg[si]])

                    sub = tt % 4
                    R.dma("pool", EXLv[tt // 4, hb:hb + 4, :, sub * 128:(sub + 1) * 128].rearrange("h p d -> p h d"),
                          stg[si].rearrange("p (h d) -> p h d", d=128), [b_stg[si]], (), [self.b_exl])
            else:
                for tt in range(16):
                    pi = pcnt % 4
                    pcnt += 1
                    for dc in range(16):
                        self.mm(self.ps[pi][:, 0:8], xT[:, dc, tt * 128:(tt + 1) * 128], slab[k][:, dc, 0:8],
                                dc == 0, dc == 15, [b_slab[k], b_xT[tt]], [self.pb[pi]])
                    self.tt("dve", lft[:, 0, :], self.ps[pi][:, 0:8], self.bf_t, ALU.add, [self.pb[pi], self.b_c], [b_lft])
                    self.act(lft[:, 1, :], lft[:, 0, :], AF.Exp, [b_lft], [b_lft], scale=-1.0)
                    self.act(lft[:, 0, :], lft[:, 1, :], AF.Ln, [b_lft, self.b_c], [b_lft], bias=self.ones1)
                    self.ts("dve", lfs[:, tt, :], lft[:, 0, :], -1.0, None, ALU.mult, None, [b_lft], (), pw=[b_lf])
        self.ts("dve", kmb, kms, 1.0 / 256.0, None, ALU.mult, None, [b_km], [b_km])
        self.copy("dve", lf3[:, 0, :, :], lfs, [b_lf], [b_lf])
        self.tt("dve", lfr, lfs, lf3[:, 0, :, :], ALU.subtract, [b_lf], [b_lf])
        self.copy("dve", lf3[:, 1, :, :], lfr, [b_lf], [b_lf])
        self.tt("dve", lfr, lfr, lf3[:, 1, :, :], ALU.subtract, [b_lf], [b_lf])
        self.copy("dve", lf3[:, 2, :, :], lfr, [b_lf], [b_lf])
        for l in range(4):
            R.dma("pool", self.EXL[l, OFF_X:OFF_X + 2048].rearrange("(p h u) -> p h u", p=128, u=2),
                  kmb[:, :, 2 * l:2 * l + 2], [b_km], (), [self.b_exl])
            R.dma("pool", self.EXL[l, OFF_X + 2048:OFF_X + 2048 + 12288].rearrange("(p k a h) -> p k a h", p=128, k=3, h=8),
                  lf3[:, :, 4 * l:4 * l + 4, :], [b_lf], (), [self.b_exl])
        self.put_slots([self.b_exl])
        self.exchange()
        R.barrier()

    def phase_b(self):
        R, A = self.R, self.A
        A.reset()
        cst = A.alloc([CSTW], F32)
        b_cst = Buf()
        R.dma("sp", cst, self.cst, (), [b_cst])
        U = cst[:, 0:128]
        LT = cst[0:64, 128:192]
        SEL127 = cst[:, 192:320]
        ones64 = cst[0:64, 320:448]
        ESELf = cst[0:32, 832:832 + 4096]
        lf = A.alloc([64, 8], F32)
        ccb = A.alloc([NT, 64], BF16)
        cc = ccb.rearrange("p a c -> p (a c)").bitcast(F32).rearrange("p (g h) -> p g h", h=8)
        tot = A.alloc([8], F32)
        rhsM = A.alloc([64, 8], F32)
        b_lf = Buf()
        b_cc = Buf()
        lf3g = A.alloc([16, 3, 32], BF16)
        R.dma("sp", lf3g, self.EXG[PT3:PT3 + 16, OFF_X + 2048:OFF_X + 2048 + 12288].rearrange("a (p k c) -> p a k c", p=128, k=3),
              [self.b_exg], [b_lf])
        lfv = lf.rearrange("p (a s) h -> p a (s h)", s=4)
        self.tt("dve", lfv, lf3g[:, :, 0, :], lf3g[:, :, 1, :], ALU.add, [b_lf], [b_lf])
        self.tt("dve", lfv, lfv, lf3g[:, :, 2, :], ALU.add, [b_lf], [b_lf])
        lf2 = lf.rearrange("p g h -> p (g h)")
        p0 = self.ps[6]
        for h in range(8):
            self.mm(p0[0:64, h:h + 1], lf[:, :, h], self.ones1, True, True, [b_lf, self.b_c], [self.pb[6]])
        self.copy("dve", tot[0:64, :], p0[0:64, 0:8], [self.pb[6]], [b_cc])
        for h in range(8):
            self.ts("dve", rhsM[0:64, :, h], LT, tot[0:64, h:h + 1], None, ALU.mult, None, [b_cst, b_cc], [b_cc])
        p1 = self.ps[7]
        self.mm(p1[:, :], U, lf2, True, False, [b_cst, b_lf], [self.pb[7]])
        self.mm(p1[:, :], ones64, rhsM[0:64, :, :].rearrange("p g h -> p (g h)"), False, True, [b_cst, b_cc], [self.pb[7]])
        self.memset("dve", cc, 30000.0, [b_cc])
        self.copy("dve", cc[:, PT3 * 4:PT3 * 4 + 64, :].rearrange("p g h -> p (g h)"), p1[:, :], [self.pb[7]], [b_cc])
        b_ct = Buf()
        R.dma("pool", self.CTS[:, 0:8192].rearrange("a (p c) -> p a c", p=128), ccb, [b_cc], [b_ct])
        tbr = A.alloc([8, 384], F32)
        TB = A.alloc([8, 384], BF16)
        OWN = A.alloc([8, 256], BF16)
        eselb = A.alloc([32 * 128], BF16, parts=32)
        b_tb = Buf()
        R.dma("sp", tbr, self.tbraw[:, 0:8 * 384].rearrange("p (h w) -> p h w", w=384), (), [b_tb])
        for h in range(8):
            self.ts("dve", TB[:, h, :], tbr[:, h, :], self.t31[:, h:h + 1], None, ALU.subtract, None, [b_tb, self.b_c], [b_tb])
        for h in range(8):
            self.tt("dve", OWN[:, h, :], TB[:, h, 128:384], self.CA[:, 0:256], ALU.add, [b_tb, self.b_c], [b_tb])
        self.copy("dve", eselb, ESELf, [b_cst], [b_tb])
        KT = [A.alloc([T], BF16) for _ in range(2)]
        VA = [A.alloc([64, 130], BF16) for _ in range(2)]
        QT = [A.alloc([512], BF16) for _ in range(2)]
        KM = [A.alloc([32], BF16) for _ in range(2)]
        b_kv = [Buf(), Buf()]
        for k in range(2):
            self.memset("pool", VA[k][:, :, 128:130], 1.0, [b_kv[k]])
            self.memset("pool", KM[k], 0.0, [b_kv[k]])
        CWb = A.alloc([16, 64], BF16)
        CW = CWb.rearrange("p a c -> p (a c)").bitcast(F32).rearrange("p (g h) -> p g h", h=8)
        CLB = A.alloc([64, 8], F32)
        b_cw = Buf()
        cb = [A.alloc([64], F32) for _ in range(2)]
        b_cb = [Buf(), Buf()]
        PT = [A.alloc([256], BF16) for _ in range(2)]
        b_pt = [Buf(), Buf()]
        gm = A.alloc([2, 32], F32)
        g2 = A.alloc([2, 32], F32)
        m8 = A.alloc([2, 8], F32)
        maskT = A.alloc([256], BF16, parts=32)
        b_gm = Buf()
        b_mask = Buf()
        ost = [A.alloc([128], BF16) for _ in range(4)]
        b_ost = [Buf() for _ in range(4)]
        rec = A.alloc([4], F32)
        b_rec = [Buf() for _ in range(4)]
        self.b_att = Buf()
        iters = [(l, h) for l in range(4) for h in range(16)]

        def loads(it):
            l, h = iters[it]
            k = it % 2
            n = 4 * (l + 1)
            Wk = self.WIN[l][:, OFF_K:OFF_K + 2048 * 512].rearrange("a (f c) -> a f c", c=512)
            Wv = self.WIN[l][:, OFF_V:OFF_V + 2048 * 512].rearrange("a (h p c) -> a h p c", h=16, p=128)
            Wm = self.WIN[l][:, OFF_X:OFF_X + 2048].rearrange("a (p c) -> a p c", p=128)
            R.dma("sp", KT[k][:, 0:n * 512].rearrange("p (a c) -> p a c", c=512),
                  Wk[:, h * 128:(h + 1) * 128, :].rearrange("a p c -> p a c"), [self.b_win[l]], [b_kv[k]])
            for a in range(n):
                R.dma("sp", VA[k][:, 4 * a:4 * a + 4, 0:128], Wv[a, h, :, :].rearrange("p (s d) -> p s d", d=128),
                      [self.b_win[l]], (), [b_kv[k]])
            R.dma("sp", QT[k], self.QT0[h * 128:(h + 1) * 128, l * 512:(l + 1) * 512], [self.b_qt0], (), [b_kv[k]])
            if h < 8:
                R.dma("sp", KM[k][:, 0:2 * n].rearrange("p (a u) -> p a u", u=2),
                      Wm[:, :, 2 * h:2 * h + 2].rearrange("a p u -> p a u"), [self.b_win[l]], (), [b_kv[k]])
        scnt = 0
        ocnt = 0
        cbn = 0
        loads(0)
        for it, (l, h) in enumerate(iters):
            k = it % 2
            n = 4 * (l + 1)
            moba = h < 8
            if h == 0:
                self.dyn("sp", (lambda e, n=n, l=l: e.dma_start(
                    out=CWb[:, 0:n, :],
                    in_=self.CTS[bass.ds(self.dynreg(e, "R" if l % 2 == 0 else "RP"), n), 0:8192].rearrange("a (p c) -> p a c", p=128))),
                    [b_ct], [b_cw])
                for c in range(0, 32 * n, 512):
                    self.mm(self.ps[6][:, :], SEL127, CW.rearrange("p g h -> p (g h)")[:, c:c + 512], True, True,
                            [b_cst, b_cw], [self.pb[6]])
                    self.copy("dve", CLB.rearrange("p g h -> p (g h)")[:, c:c + 512], self.ps[6][:, :], [self.pb[6]], (), [b_cw])
            if it + 1 < len(iters):
                loads(it + 1)
            for u in range(2):
                kd0 = 4 * (n - 1) + 2 * u
                kd1 = kd0 + 1
                kp = kd0 - 1
                q0 = u * 256
                if moba:
                    bv = cst[:, 448 + (2 * l + u) * 32:448 + (2 * l + u + 1) * 32]
                    for v in range(2):
                        self.mm(self.ps[4][:, v * 32:v * 32 + 32], QT[k][:, q0 + v * 128:q0 + (v + 1) * 128], KM[k][:, 0:32],
                                True, True, [b_kv[k]], [self.pb[4]])
                    for v in range(2):
                        self.tt("dve", gm[:, v, :], self.ps[4][:, v * 32:v * 32 + 32], bv, ALU.add, [self.pb[4], b_cst],
                                [b_gm] if v == 0 else (), () if v == 0 else [b_gm])
                    for v in range(2):
                        R.op("dve", (lambda e, v=v: e.max(out=m8[:, v, :], in_=gm[:, v, :])), [b_gm], [b_gm])
                    for v in range(2):
                        self.ts("dve", g2[:, v, :], gm[:, v, :], m8[:, v, 2:3], None, ALU.is_ge, None, [b_gm], [b_gm])
                    self.ts("dve", gm, gm, -1e29, None, ALU.is_gt, None, [b_gm], [b_gm])
                    self.tt("dve", g2, g2, gm, ALU.mult, [b_gm], [b_gm])
                    self.ts("dve", g2, g2, -NEG, NEG, ALU.mult, ALU.add, [b_gm], [b_gm])
                    for v in range(2):
                        self.tr(self.ps[5][0:32, v * 128:(v + 1) * 128], g2[:, v, :], self.identf, [b_gm, self.b_c], [self.pb[5]])
                    self.copy("dve", maskT, self.ps[5][0:32, 0:256], [self.pb[5]], [b_mask])
                    bias_ap = self.t31[:, h:h + 1]
                else:
                    hh = h - 8
                    cbk = cbn % 2
                    cbn += 1
                    self.ts("dve", cb[cbk][:, 0:kd1 + 1], CW[:, 0:kd1 + 1, hh], -1.0, CLB[:, kd1, hh:hh + 1], ALU.mult, ALU.add,
                            [b_cw], [b_cb[cbk]])
                for kk in range(0, kd1 + 1):
                    si = scnt % 2
                    scnt += 1
                    S = self.ps[si]
                    qa, qb = (128, 256) if kk == kd1 else (0, 256)
                    extra = []
                    if moba:
                        if kk < kd0:
                            wb = kk // 2
                            extra.append((eselb[0:32, wb * 128:(wb + 1) * 128], maskT[0:32, qa:qb], [b_tb, b_mask]))
                        if kk == kp:
                            extra.append((self.identb, TB[:, h, 0:256], [self.b_c, b_tb]))
                        if kk == kd0:
                            extra.append((self.identb, OWN[:, h, 0:256], [self.b_c, b_tb]))
                        if kk == kd1:
                            extra.append((self.identb, OWN[:, h, 0:128], [self.b_c, b_tb]))
                    else:
                        if kk == kd0:
                            extra.append((self.identb, self.CA[:, 0:256], [self.b_c]))
                        if kk == kd1:
                            extra.append((self.identb, self.CA[:, 0:128], [self.b_c]))
                    self.mm(S[:, qa:qb], KT[k][:, kk * 128:(kk + 1) * 128], QT[k][:, q0 + qa:q0 + qb],
                            True, len(extra) == 0, [b_kv[k]], [self.pb[si]])
                    for ei, (lt_, rh_, rd_) in enumerate(extra):
                        self.mm(S[:, qa:qb], lt_, rh_, False, ei == len(extra) - 1, rd_, [self.pb[si]])
                    if moba:
                        self.act(PT[si][:, qa:qb], S[:, qa:qb], AF.Exp, [self.pb[si], self.b_c], [b_pt[si]], bias=bias_ap)
                    else:
                        self.act(PT[si][:, qa:qb], S[:, qa:qb], AF.Exp, [self.pb[si], b_cb[cbk]], [b_pt[si]],
                                 bias=cb[cbk][:, kk:kk + 1])
                    for v in range(2):
                        if kk == kd1 and v == 0:
                            continue
                        lastk = kd0 if v == 0 else kd1
                        self.mm(self.ps[2 + v][:, 0:130], PT[si][:, v * 128:(v + 1) * 128], VA[k][:, kk, :],
                                kk == 0, kk == lastk, [b_pt[si], b_kv[k]], [self.pb[2 + v]])
                for v in range(2):
                    oi = ocnt % 4
                    ocnt += 1
                    O = self.ps[2 + v]
                    R.op("dve", (lambda e, O=O, oi=oi: e.reciprocal(out=rec[:, oi:oi + 1], in_=O[:, 128:129])),
                         [self.pb[2 + v]], [b_rec[oi]])
                    self.act(ost[oi], O[:, 0:128], AF.Copy, [self.pb[2 + v], b_rec[oi]], [b_ost[oi]], scale=rec[:, oi:oi + 1])
                    row0 = l * 512 + q0 + v * 128
                    R.dma("pool", self.ATT[row0:row0 + 128, h * 128:(h + 1) * 128], ost[oi], [b_ost[oi]], (), [self.b_att])
        R.barrier()

    def phase_c1(self, wname, res_src, vrow, HMdst, HMTdst, router=False):
        R, A = self.R, self.A
        A.reset()
        Wv = self.wview(wname)
        Wo = A.alloc([16, D], BF16)
        b_wo = Buf()
        for c in range(4):
            R.dma("sp", Wo[:, :, c * 512:(c + 1) * 512], Wv[:, c * 512:(c + 1) * 512].rearrange("(a p) n -> p a n", p=128),
                  [self.b_wg], (), [b_wo])
        gt = A.alloc([D], F32)
        bt = A.alloc([D], F32)
        b_gb = Buf()
        R.dma("sp", gt, self.vecs[vrow:vrow + 1, :].partition_broadcast(128), (), (), [b_gb])
        R.dma("sp", bt, self.vecs[vrow + 1:vrow + 2, :].partition_broadcast(128), (), (), [b_gb])
        at = [A.alloc([D], BF16) for _ in range(2)]
        b_at = [Buf(), Buf()]
        aT = [A.alloc([16, 128], BF16) for _ in range(2)]
        b_aT = [Buf(), Buf()]
        xr = [A.alloc([D], F32) for _ in range(2)]
        b_xr = [Buf(), Buf()]
        z = [A.alloc([D], F32) for _ in range(2)]
        b_z = [Buf(), Buf()]
        hb = [A.alloc([D], BF16) for _ in range(2)]
        b_hb = [Buf(), Buf()]
        hT = [A.alloc([16, 128], BF16) for _ in range(2)]
        b_hT = [Buf(), Buf()]
        st6 = A.alloc([4, 6], F32)
        mv = A.alloc([2], F32)
        rs = A.alloc([4], F32)
        self.b_hm = Buf()
        self.b_hmt = Buf()
        if router:
            wr = A.alloc([16, 8], BF16)
            R.dma("sp", wr, self.wview("router").rearrange("(a p) n -> p a n", p=128), [self.b_wg], (), [b_wo])
            rt = A.alloc([64], F32)
            b_rt = Buf()
            self.b_gt = Buf()
        for tt in range(16):
            k = tt % 2
            R.dma("sp", at[k], self.ATT[tt * 128:(tt + 1) * 128, :], [self.b_att], [b_at[k]])
            R.dma("sp", xr[k], res_src[tt * 128:(tt + 1) * 128, :], [self.b_hm] if res_src is not self.x_loc else [], [b_xr[k]])
            self.transpose16(at[k], b_at[k], aT[k], 0, b_aT[k])
            for c in range(4):
                for f in range(16):
                    self.mm(self.ps[c][:, :], aT[k][:, f, :], Wo[:, f, c * 512:(c + 1) * 512], f == 0, f == 15,
                            [b_aT[k], b_wo], [self.pb[c]])
                self.stt("dve", z[k][:, c * 512:(c + 1) * 512], xr[k][:, c * 512:(c + 1) * 512], ALPHA, self.ps[c][:, :],
                         ALU.mult, ALU.add, [b_xr[k], self.pb[c]], [b_z[k]] if c == 0 else (), () if c == 0 else [b_z[k]])
            self.ln_inplace(z[k], gt, bt, b_z[k], (st6, mv, rs), b_gb)
            R.dma("pool", HMdst[tt * 128:(tt + 1) * 128, :], z[k], [b_z[k]], (), [self.b_hm])
            self.copy("act", hb[k], z[k], [b_z[k]], [b_hb[k]])
            self.transpose16(hb[k], b_hb[k], hT[k], 0, b_hT[k])
            R.dma("pool", HMTdst[:, tt * 128:(tt + 1) * 128].rearrange("(a p) t -> p a t", p=128), hT[k], [b_hT[k]], (), [self.b_hmt])
            if router:
                for dc in range(16):
                    self.mm(self.ps[5][:, 0:8], hT[k][:, dc, :], wr[:, dc, :], dc == 0, dc == 15, [b_hT[k], b_wo], [self.pb[5]])
                lg16, m8, ex, g1, g2, ga, gb = rt[:, 40:56], rt[:, 8:16], rt[:, 16:17], rt[:, 17:18], rt[:, 18:19], rt[:, 24:32], rt[:, 32:40]
                lg = lg16[:, 0:8]
                nl1 = rt[:, 19:20]
                self.memset("dve", lg16, -1e30, [b_rt])
                self.copy("dve", lg, self.ps[5][:, 0:8], [self.pb[5]], [b_rt])
                R.op("dve", lambda e: e.max(out=m8, in_=lg16), [b_rt], [b_rt])
                self.ts("dve", nl1, m8[:, 0:1], -1.0, None, ALU.mult, None, [b_rt], [b_rt])
                self.act(ex, m8[:, 1:2], AF.Exp, [b_rt], [b_rt], bias=nl1)
                self.ts("dve", g1, ex, 1.0, None, ALU.add, None, [b_rt], [b_rt])
                R.op("dve", lambda e: e.reciprocal(out=g2, in_=g1), [b_rt], [b_rt])
                self.copy("dve", g1, g2, [b_rt], [b_rt])
                self.tt("dve", g2, ex, g1, ALU.mult, [b_rt], [b_rt])
                self.ts("dve", ga, lg, m8[:, 0:1], None, ALU.is_equal, None, [b_rt], [b_rt])
                self.ts("dve", ga, ga, g1, None, ALU.mult, None, [b_rt], [b_rt])
                self.ts("dve", gb, lg, m8[:, 1:2], None, ALU.is_equal, None, [b_rt], [b_rt])
                self.ts("dve", gb, gb, g2, None, ALU.mult, None, [b_rt], [b_rt])
                self.tt("dve", ga, ga, gb, ALU.add, [b_rt], [b_rt])
                R.dma("pool", self.GT[tt * 128:(tt + 1) * 128, :], ga, [b_rt], (), [self.b_gt])
        R.barrier()

    def ln_inplace(self, z, gt, bt, bz, tmpst, b_gb):
        st6, mv, rs = tmpst
        bs = Buf()
        R = self.R
        for c in range(4):
            R.op("dve", (lambda e, c=c: e.bn_stats(out=st6[:, c, :], in_=z[:, c * 512:(c + 1) * 512])), [bz],
                 [bs] if c == 0 else (), () if c == 0 else [bs])
        R.op("dve", lambda e: e.bn_aggr(out=mv, in_=st6), [bs], [bs])
        self.ts("dve", rs[:, 0:1], mv[:, 1:2], 1e-5, None, ALU.add, None, [bs], [bs])
        self.act(rs[:, 1:2], rs[:, 0:1], AF.Sqrt, [bs], [bs])
        R.op("dve", lambda e: e.reciprocal(out=rs[:, 2:3], in_=rs[:, 1:2]), [bs], [bs])
        self.ts("dve", z, z, mv[:, 0:1], rs[:, 2:3], ALU.subtract, ALU.mult, [bs], [bz])
        self.tt("pool", z, z, gt, ALU.mult, [b_gb], [bz])
        self.tt("dve", z, z, bt, ALU.add, [b_gb], [bz])

    def phase_c2(self):
        R, A = self.R, self.A
        A.reset()
        F0 = self.F0
        NF = F0 // 128
        W1 = self.wview("w1")
        W3 = self.wview("w3")
        W2 = self.wview("w2")
        gt = A.alloc([D], F32)
        bt = A.alloc([D], F32)
        b_gb = Buf()
        R.dma("sp", gt, self.vecs[2:3, :].partition_broadcast(128), (), (), [b_gb])
        R.dma("sp", bt, self.vecs[3:4, :].partition_broadcast(128), (), (), [b_gb])
        hT = A.alloc([16, 512], BF16)
        b_hT = Buf()
        hm4 = A.alloc([4, D], F32)
        b_hm4 = [Buf() for _ in range(4)]
        gT = A.alloc([NF, 512], BF16)
        b_gT = Buf()
        w13 = [A.alloc([2, 16, 128], BF16) for _ in range(2)]
        b_w13 = [Buf(), Buf()]
        w2s = A.alloc([NF, 256], BF16)
        b_w2 = Buf()
        sA = [A.alloc([512], BF16) for _ in range(2)]
        b_sA = [Buf(), Buf()]
        hb = A.alloc([D], BF16)
        b_hb = Buf()
        oT = A.alloc([16, 128], BF16)
        b_oT = Buf()
        st6 = A.alloc([4, 6], F32)
        mv = A.alloc([2], F32)
        rs = A.alloc([4], F32)
        self.b_h1 = Buf()
        self.b_h1t = Buf()
        wc = 0
        pc = 0
        for l in range(4):
            R.dma("sp", hT, self.HMT[:, l * 512:(l + 1) * 512].rearrange("(a p) t -> p a t", p=128), [self.b_hmt], [b_hT])
            for ts_ in range(4):
                R.dma("sp", hm4[:, ts_, :], self.HM[l * 512 + ts_ * 128: l * 512 + (ts_ + 1) * 128, :], [self.b_hm], [b_hm4[ts_]])
            for f0 in range(0, F0, 128):
                k = wc % 2
                wc += 1
                R.dma("sp", w13[k][:, 0, :, :], W1[:, f0:f0 + 128].rearrange("(a p) n -> p a n", p=128), [self.b_wg], [b_w13[k]])
                R.dma("sp", w13[k][:, 1, :, :], W3[:, f0:f0 + 128].rearrange("(a p) n -> p a n", p=128), [self.b_wg], (), [b_w13[k]])
                for fc in range(1):
                    fidx = f0 // 128 + fc
                    pa = (pc % 2) * 2
                    sk = pc % 2
                    pc += 1
                    for dc in range(16):
                        self.mm(self.ps[pa][:, :], w13[k][:, 0, dc, fc * 128:(fc + 1) * 128], hT[:, dc, :], dc == 0, dc == 15,
                                [b_w13[k], b_hT], [self.pb[pa]])
                    for dc in range(16):
                        self.mm(self.ps[pa + 1][:, :], w13[k][:, 1, dc, fc * 128:(fc + 1) * 128], hT[:, dc, :], dc == 0, dc == 15,
                                [b_w13[k], b_hT], [self.pb[pa + 1]])
                    self.act(sA[sk], self.ps[pa][:, :], AF.Silu, [self.pb[pa]], [b_sA[sk]])
                    self.tt("dve", gT[:, fidx, :], sA[sk], self.ps[pa + 1][:, :], ALU.mult, [b_sA[sk], self.pb[pa + 1]],
                            [b_gT] if fidx == 0 else (), () if fidx == 0 else [b_gT])
            for c0 in range(0, D, 256):
                R.dma("sp", w2s, W2[:, c0:c0 + 256].rearrange("(a p) n -> p a n", p=128), [self.b_wg], [b_w2])
                for ts_ in range(4):
                    pi = 4 + (pc % 2)
                    pc += 1
                    for f in range(NF):
                        self.mm(self.ps[pi][:, 0:256], gT[:, f, ts_ * 128:(ts_ + 1) * 128], w2s[:, f, :], f == 0, f == NF - 1,
                                [b_gT, b_w2], [self.pb[pi]])
                    self.stt("dve", hm4[:, ts_, c0:c0 + 256], hm4[:, ts_, c0:c0 + 256], ALPHA, self.ps[pi][:, 0:256],
                             ALU.mult, ALU.add, [self.pb[pi]], (), [b_hm4[ts_]])
            for ts_ in range(4):
                self.ln_inplace(hm4[:, ts_, :], gt, bt, b_hm4[ts_], (st6, mv, rs), b_gb)
                row0 = l * 512 + ts_ * 128
                R.dma("pool", self.H1[row0:row0 + 128, :], hm4[:, ts_, :], [b_hm4[ts_]], (), [self.b_h1])
                self.copy("act", hb, hm4[:, ts_, :], [b_hm4[ts_]], [b_hb])
                self.transpose16(hb, b_hb, oT, 0, b_oT)
                R.dma("pool", self.H1T[:, row0:row0 + 128].rearrange("(a p) t -> p a t", p=128), oT, [b_oT], (), [self.b_h1t])
        R.barrier()

    def layer1(self):
        self.l1_proj()
        if self.stop_after == "p":
            return self.copy_out(self.H1)
        self.l1_index()
        if self.stop_after == "i":
            return self.copy_out(self.H1)
        self.l1_attn()
        if self.stop_after == "t":
            return self.copy_out(self.H1)
        self.phase_c1("w_out1", self.H1, 4, self.HM, self.HMT, router=True)
        if self.stop_after == "d":
            return self.copy_out(self.HM)
        self.l1_moe()
        return self.l1_final()

    def l1_proj(self):
        R, A = self.R, self.A
        A.reset()
        nc = self.nc
        Win = self.wview("w_in1")
        Wqi = self.wview("w_qidx")
        Wuq = self.wview("w_uq")
        Wuk = self.wview("w_uk")
        Wuv = self.wview("w_uv")
        win = A.alloc([16, 1104], BF16)
        wqi = A.alloc([4, 1024], BF16)
        wuq = A.alloc([4, D], BF16)
        wuk = A.alloc([4, D], BF16)
        wuv = A.alloc([4, D], BF16)
        b_w = Buf()
        R.dma("sp", win, Win.rearrange("(a p) n -> p a n", p=128), [self.b_wg], [b_w])
        R.dma("sp", wqi, Wqi.rearrange("(a p) n -> p a n", p=128), [self.b_wg], (), [b_w])
        R.dma("sp", wuq, Wuq.rearrange("(a p) n -> p a n", p=128), [self.b_wg], (), [b_w])
        R.dma("sp", wuk, Wuk.rearrange("(a p) n -> p a n", p=128), [self.b_wg], (), [b_w])
        R.dma("sp", wuv, Wuv.rearrange("(a p) n -> p a n", p=128), [self.b_wg], (), [b_w])
        gq = A.alloc([512], F32)
        gkv = A.alloc([512], F32)
        R.dma("sp", gq, self.vecs[8:9, 0:512].partition_broadcast(128), (), (), [b_w])
        R.dma("sp", gkv, self.vecs[9:10, 0:512].partition_broadcast(128), (), (), [b_w])
        hT = A.alloc([16, 512], BF16)
        b_hT = Buf()
        cqT = A.alloc([4, 512], BF16)
        ckT = A.alloc([4, 512], BF16)
        kiT = A.alloc([512], BF16, parts=64)
        b_cT = Buf()
        junk = A.alloc([512], BF16)
        ssq = A.alloc([4], F32)
        b_ss = Buf()
        nrm = [A.alloc([512], BF16) for _ in range(2)]
        b_nrm = [Buf(), Buf()]
        kib = A.alloc([64], BF16)
        wsb = A.alloc([32], F32)
        b_ws = Buf()
        stg = [A.alloc([512], BF16) for _ in range(4)]
        b_stg = [Buf() for _ in range(4)]
        self.b_qi = Buf()
        self.b_ws = Buf()
        self.b_exl = Buf()
        self.b_qt0 = Buf()
        EXLk = self.EXL[:, OFF_K:OFF_K + 2048 * 512].rearrange("l (f c) -> l f c", c=512)
        EXLv = self.EXL[:, OFF_V:OFF_V + 2048 * 512].rearrange("l (h p c) -> l h p c", h=16, p=128)
        scnt = 0
        pcnt = 0
        for l in range(4):
            R.dma("sp", hT, self.H1T[:, l * 512:(l + 1) * 512].rearrange("(a p) t -> p a t", p=128), [self.b_h1t], [b_hT])
            for ts_ in range(4):
                tok = slice(ts_ * 128, (ts_ + 1) * 128)
                for gi, (c0, w) in enumerate([(0, 512), (512, 512), (1024, 80)]):
                    for dc in range(16):
                        self.mm(self.ps[gi][:, 0:w], hT[:, dc, tok], win[:, dc, c0:c0 + w], dc == 0, dc == 15, [b_hT, b_w], [self.pb[gi]])
                for gi, (gvec, dstT) in enumerate([(gq, cqT), (gkv, ckT)]):
                    self.memset("dve", ssq[:, 0:1], 0.0, [b_ss])
                    R.op("act", (lambda e, gi=gi: e.activation(out=junk, in_=self.ps[gi][:, :], func=AF.Square, accum_out=ssq[:, 0:1])),
                         [self.pb[gi], b_ss], [b_ss])
                    self.ts("dve", ssq[:, 1:2], ssq[:, 0:1], 1.0 / 512.0, 1e-6, ALU.mult, ALU.add, [b_ss], [b_ss])
                    self.act(ssq[:, 2:3], ssq[:, 1:2], AF.Sqrt, [b_ss], [b_ss])
                    R.op("dve", lambda e: e.reciprocal(out=ssq[:, 3:4], in_=ssq[:, 2:3]), [b_ss], [b_ss])
                    k = gi
                    self.stt("dve", nrm[k], self.ps[gi][:, :], ssq[:, 3:4], gvec, ALU.mult, ALU.mult, [self.pb[gi], b_ss, b_w], [b_nrm[k]])
                    pv = self.ps[6 + gi][:, :].bitcast(BF16)
                    for rc in range(4):
                        self.tr(pv[:, rc * 128:(rc + 1) * 128], nrm[k][:, rc * 128:(rc + 1) * 128], self.identb, [b_nrm[k], self.b_c], [self.pb[6 + gi]])
                    self.copy("act", dstT[:, :, tok], pv[:, 0:512].rearrange("p (a b) -> p a b", b=128), [self.pb[6 + gi]], (), [b_cT])
                self.copy("dve", kib, self.ps[2][:, 0:64], [self.pb[2]], [b_ws])
                self.ts("dve", wsb[:, 0:16], self.ps[2][:, 64:80], 1.0 / 32.0, None, ALU.mult, None, [self.pb[2]], [b_ws])
                R.op("act", lambda e: e.sign(out=wsb[:, 16:32], in_=wsb[:, 0:16]), [b_ws], [b_ws])
                self.tt("dve", wsb[:, 0:16], wsb[:, 0:16], wsb[:, 16:32], ALU.mult, [b_ws], [b_ws])
                row0 = l * 512 + ts_ * 128
                R.dma("pool", self.WS[row0:row0 + 128, :], wsb, [b_ws], (), [self.b_ws])
                pv = self.ps[5][:, :].bitcast(BF16)
                self.tr(pv[0:64, 0:128], kib, self.identb, [b_ws, self.b_c], [self.pb[5]])
                self.copy("dve", kiT[0:64, tok], pv[0:64, 0:128], [self.pb[5]], (), [b_cT])
            R.dma("pool", self.EXL[l, OFF_X:OFF_X + 32768].rearrange("(p c) -> p c", p=64), kiT, [b_cT], (), [self.b_exl])
            jobs = [("qi", mc) for mc in range(8)] + [("q", h) for h in range(16)] + [("k", h) for h in range(16)]
            for kind, idx in jobs:
                pi = pcnt % 4
                pcnt += 1
                if kind == "qi":
                    W_, src, cols = wqi, cqT, slice(idx * 128, (idx + 1) * 128)
                elif kind == "q":
                    W_, src, cols = wuq, cqT, slice(idx * 128, (idx + 1) * 128)
                else:
                    W_, src, cols = wuk, ckT, slice(idx * 128, (idx + 1) * 128)
                for rc in range(4):
                    self.mm(self.ps[pi][:, :], W_[:, rc, cols], src[:, rc, :], rc == 0, rc == 3, [b_w, b_cT], [self.pb[pi]])
                si = scnt % 4
                scnt += 1
                if kind == "q":
                    self.act(stg[si], self.ps[pi][:, :], AF.Copy, [self.pb[pi]], [b_stg[si]], scale=SCALE)
                    R.dma("pool", self.QT0[idx * 128:(idx + 1) * 128, l * 512:(l + 1) * 512], stg[si], [b_stg[si]], (), [self.b_qt0])
                elif kind == "qi":
                    self.copy("dve", stg[si], self.ps[pi][:, :], [self.pb[pi]], [b_stg[si]])
                    R.dma("pool", self.QI[idx * 128:(idx + 1) * 128, l * 512:(l + 1) * 512], stg[si], [b_stg[si]], (), [self.b_qi])
                else:
                    self.copy("dve", stg[si], self.ps[pi][:, :], [self.pb[pi]], [b_stg[si]])
                    R.dma("pool", EXLk[l, idx * 128:(idx + 1) * 128, :], stg[si], [b_stg[si]], (), [self.b_exl])
            for ts_ in range(4):
                for hb in range(0, 16, 4):
                    pi = pcnt % 4
                    pcnt += 1
                    for rc in range(4):
                        self.mm(self.ps[pi][:, :], ckT[:, rc, ts_ * 128:(ts_ + 1) * 128], wuv[:, rc, hb * 128:(hb + 4) * 128],
                                rc == 0, rc == 3, [b_w, b_cT], [self.pb[pi]])
                    si = scnt % 4
                    scnt += 1
                    self.copy("act", stg[si], self.ps[pi][:, :], [self.pb[pi]], [b_stg[si]])
                    R.dma("pool", EXLv[l, hb:hb + 4, :, ts_ * 128:(ts_ + 1) * 128].rearrange("h p d -> p h d"),
                          stg[si].rearrange("p (h d) -> p h d", d=128), [b_stg[si]], (), [self.b_exl])
        self.put_slots([self.b_exl])
        self.exchange()
        R.barrier()

    def l1_index(self):
        R, A = self.R, self.A
        A.reset()
        cst = A.alloc([CSTW], F32)
        b_cst = Buf()
        R.dma("sp", cst, self.cst, (), [b_cst])
        CM = A.alloc([4, 512], F32)
        for m in range(4):
            self.memset("pool", CM[:, m, :], 0.0, [b_cst])
            R.op("pool", (lambda e, m=m: e.affine_select(out=CM[:, m, :], in_=CM[:, m, :], pattern=[[-1, 512]],
                                                         compare_op=ALU.is_ge, fill=-1e30, base=m * 128, channel_multiplier=1)),
                 [b_cst], [b_cst])
        KI = A.alloc([T], BF16)
        b_ki = Buf()
        QIt = [A.alloc([8, 128], BF16) for _ in range(2)]
        wst = [A.alloc([32], F32) for _ in range(2)]
        b_q = [Buf(), Buf()]
        sc = A.alloc([T], F32)
        sc2 = A.alloc([512], F32)
        b_sc = Buf()
        b_sc2 = Buf()
        rl = [A.alloc([512], F32) for _ in range(4)]
        b_rl = [Buf() for _ in range(4)]
        junk = A.alloc([T], BF16)
        b_junk = Buf()
        bis = A.alloc([8], F32)
        b_bis = Buf()
        mk = A.alloc([T], F32)
        b_mk = Buf()
        mstg = [A.alloc([8, 128], BF16) for _ in range(2)]
        b_mstg = [Buf(), Buf()]
        self.b_mk = Buf()
        rcnt = 0
        tcnt = 0
        qn = 0
        for l in range(4):
            n = 4 * (l + 1)
            L = n * 512
            src = self.WIN[l][:, OFF_X:OFF_X + 32768].rearrange("a (p c) -> p a c", p=64)
            R.dma("sp", KI[0:64, 0:L].rearrange("p (a c) -> p a c", c=512), src, [self.b_win[l]], [b_ki])
            R.dma("sp", KI[64:128, 0:L].rearrange("p (a c) -> p a c", c=512), src, [self.b_win[l]], (), [b_ki])
            for ts_ in range(4):
                k = qn % 2
                qn += 1
                row0 = l * 512 + ts_ * 128
                R.dma("sp", QIt[k], self.QI[:, row0:row0 + 128].rearrange("(a p) t -> p a t", p=128), [self.b_qi], [b_q[k]])
                R.dma("sp", wst[k], self.WS[row0:row0 + 128, :], [self.b_ws], (), [b_q[k]])
                for a in range(n):
                    ksl = slice(a * 512, (a + 1) * 512)
                    virt = cst[:, 4928 + l * 16 + a:4928 + l * 16 + a + 1]
                    for hh in range(16):
                        mc, half = divmod(hh, 2)
                        pi = hh % 2
                        prt = slice(half * 64, (half + 1) * 64)
                        self.mm(self.ps[pi][:, :], QIt[k][prt, mc, :], KI[prt, ksl], True, True, [b_q[k], b_ki], [self.pb[pi]])
                        ri = rcnt % 4
                        rcnt += 1
                        self.act(rl[ri], self.ps[pi][:, :], AF.Relu, [self.pb[pi], b_q[k]], [b_rl[ri]], scale=wst[k][:, hh:hh + 1])
                        sg = wst[k][:, 16 + hh:17 + hh]
                        if hh == 0:
                            self.ts("dve", sc[:, ksl], rl[ri], sg, virt, ALU.mult, ALU.add, [b_rl[ri], b_q[k], b_cst], [b_sc])
                        else:
                            self.stt("dve", sc[:, ksl], rl[ri], sg, sc[:, ksl], ALU.mult, ALU.add, [b_rl[ri], b_q[k]], [b_sc])
                    if a == n - 1:
                        self.tt("dve", sc[:, ksl], sc[:, ksl], CM[:, ts_, :], ALU.add, [b_cst], [b_sc])
                self.memset("dve", bis[:, 0:1], -64.0, [b_bis])
                for it in range(28):
                    hstep = 64.0 / (2 ** it)
                    self.ts("dve", bis[:, 1:2], bis[:, 0:1], hstep, None, ALU.add, None, [b_bis], [b_bis])
                    self.memset("dve", bis[:, 2:3], 0.0, [b_bis])
                    self.ts("dve", junk[:, 0:L], sc[:, 0:L], bis[:, 1:2], 0.0, ALU.is_ge, ALU.add, [b_sc, b_bis], [b_junk],
                            accum_out=bis[:, 2:3])
                    self.ts("dve", bis[:, 3:4], bis[:, 2:3], 256.0, hstep, ALU.is_ge, ALU.mult, [b_junk, b_bis], [b_bis])
                    self.tt("dve", bis[:, 0:1], bis[:, 0:1], bis[:, 3:4], ALU.add, [b_bis], [b_bis])
                self.ts("dve", mk[:, 0:L], sc[:, 0:L], bis[:, 0:1], None, ALU.is_ge, None, [b_sc, b_bis], [b_mk])
                self.ts("pool", mk[:, 0:L], mk[:, 0:L], -NEG, NEG, ALU.mult, ALU.add, [b_mk], [b_mk])
                for g0 in range(0, 4 * n, 4):
                    pi = 4 + (tcnt % 2)
                    mi = tcnt % 2
                    tcnt += 1
                    for q in range(4):
                        kk = g0 + q
                        self.tr(self.ps[pi][:, q * 128:(q + 1) * 128], mk[:, kk * 128:(kk + 1) * 128], self.identf,
                                [b_mk, self.b_c], [self.pb[pi]])
                    self.copy("act", mstg[mi][:, 0:4, :], self.ps[pi][:, :].rearrange("p (a b) -> p a b", b=128), [self.pb[pi]], [b_mstg[mi]])
                    R.dma("pool", self.MK[l][g0:g0 + 4, :, ts_ * 128:(ts_ + 1) * 128].rearrange("k s q -> s k q"), mstg[mi][:, 0:4, :],
                          [b_mstg[mi]], (), [self.b_mk])
        R.barrier()

    def l1_attn(self):
        R, A = self.R, self.A
        A.reset()
        tbr = A.alloc([384], F32)
        TB = A.alloc([16, 384], BF16)
        b_tb = Buf()
        for h in range(16):
            R.dma("sp", tbr, self.tbraw[:, h * 384:(h + 1) * 384], (), [b_tb])
            self.ts("dve", TB[:, h, :], tbr, self.t31[:, h:h + 1], None, ALU.subtract, None, [b_tb, self.b_c], [b_tb])
        KT = [A.alloc([T], BF16) for _ in range(2)]
        VA = [A.alloc([64, 130], BF16) for _ in range(2)]
        QT = [A.alloc([512], BF16) for _ in range(2)]
        b_kv = [Buf(), Buf()]
        for k in range(2):
            self.memset("pool", VA[k][:, :, 128:130], 1.0, [b_kv[k]])
        MKs = A.alloc([64, 512], BF16)
        b_mks = Buf()
        PT = [A.alloc([512], BF16) for _ in range(2)]
        b_pt = [Buf(), Buf()]
        ost = [A.alloc([128], BF16) for _ in range(4)]
        b_ost = [Buf() for _ in range(4)]
        rec = A.alloc([4], F32)
        b_rec = [Buf() for _ in range(4)]
        self.b_att = Buf()
        iters = [(l, h) for l in range(4) for h in range(16)]

        def loads(it):
            l, h = iters[it]
            k = it % 2
            n = 4 * (l + 1)
            Wk = self.WIN[l][:, OFF_K:OFF_K + 2048 * 512].rearrange("a (f c) -> a f c", c=512)
            Wv = self.WIN[l][:, OFF_V:OFF_V + 2048 * 512].rearrange("a (h p c) -> a h p c", h=16, p=128)
            R.dma("sp", KT[k][:, 0:n * 512].rearrange("p (a c) -> p a c", c=512),
                  Wk[:, h * 128:(h + 1) * 128, :].rearrange("a p c -> p a c"), [self.b_win[l]], [b_kv[k]])
            for a in range(n):
                R.dma("sp", VA[k][:, 4 * a:4 * a + 4, 0:128], Wv[a, h, :, :].rearrange("p (s d) -> p s d", d=128),
                      [self.b_win[l]], (), [b_kv[k]])
            R.dma("sp", QT[k], self.QT0[h * 128:(h + 1) * 128, l * 512:(l + 1) * 512], [self.b_qt0], (), [b_kv[k]])
        scnt = 0
        ocnt = 0
        loads(0)
        for it, (l, h) in enumerate(iters):
            k = it % 2
            n = 4 * (l + 1)
            if h == 0:
                for g0 in range(0, 4 * n, 16):
                    R.dma("sp", MKs[:, g0:g0 + 16, :], self.MK[l][g0:g0 + 16, :, :].rearrange("k s q -> s k q"), [self.b_mk],
                          [b_mks] if g0 == 0 else (), () if g0 == 0 else [b_mks])
            if it + 1 < len(iters):
                loads(it + 1)
            kown = 4 * (n - 1)
            for kk in range(0, 4 * n):
                si = scnt % 2
                scnt += 1
                S = self.ps[si]
                m = kk - kown
                qa = 128 * m if m > 0 else 0
                extra = [(self.identb, MKs[:, kk, qa:512], qa, 512, [self.b_c, b_mks])]
                if kk == kown - 1:
                    extra.append((self.identb, TB[:, h, 0:128], 0, 128, [self.b_c, b_tb]))
                if m >= 0:
                    w = min(256, 512 - 128 * m)
                    extra.append((self.identb, TB[:, h, 128:128 + w], 128 * m, 128 * m + w, [self.b_c, b_tb]))
                self.mm(S[:, qa:512], KT[k][:, kk * 128:(kk + 1) * 128], QT[k][:, qa:512], True, False, [b_kv[k]], [self.pb[si]])
                for ei, (lt_, rh_, c0, c1, rd_) in enumerate(extra):
                    self.mm(S[:, c0:c1], lt_, rh_, False, ei == len(extra) - 1, rd_, [self.pb[si]])
                self.act(PT[si][:, qa:512], S[:, qa:512], AF.Exp, [self.pb[si], self.b_c], [b_pt[si]], bias=self.t31[:, h:h + 1])
                for v in range(4):
                    if m > v:
                        continue
                    self.mm(self.ps[2 + v][:, 0:130], PT[si][:, v * 128:(v + 1) * 128], VA[k][:, kk, :],
                            kk == 0, kk == kown + v, [b_pt[si], b_kv[k]], [self.pb[2 + v]])
            for v in range(4):
                oi = ocnt % 4
                ocnt += 1
                O = self.ps[2 + v]
                R.op("dve", (lambda e, O=O, oi=oi: e.reciprocal(out=rec[:, oi:oi + 1], in_=O[:, 128:129])),
                     [self.pb[2 + v]], [b_rec[oi]])
                self.act(ost[oi], O[:, 0:128], AF.Copy, [self.pb[2 + v], b_rec[oi]], [b_ost[oi]], scale=rec[:, oi:oi + 1])
                row0 = l * 512 + v * 128
                R.dma("pool", self.ATT[row0:row0 + 128, h * 128:(h + 1) * 128], ost[oi], [b_ost[oi]], (), [self.b_att])
        R.barrier()


    def l1_moe(self):
        R, A = self.R, self.A
        A.reset()
        FE = self.FE
        NF = FE // 128
        hz_zb = []
        self.zero_fill(self.HZ, 8 * D, NLOC, hz_zb)
        gz_zb = Buf()
        R.dma("sp", self.GZ.rearrange("(p a) c -> p (a c)", p=128), self.zt[:, 0:2048].bitcast(F32), [self.b_zt], [gz_zb])
        b_hz = Buf()
        b_gz = Buf()

        def put_h(e):
            r = self.dynreg(e, "R8")
            return e.dma_start(out=self.HZ.rearrange("(s r) c -> s r c", s=8)[bass.ds(r, 1), :, :].rearrange("s (a p) c -> p (s a) c", p=128),
                               in_=self.HMT.rearrange("(a p) c -> p a c", p=128))
        self.dyn("pool", put_h, [self.b_hmt] + hz_zb, (), [b_hz])

        def put_g(e):
            r = self.dynreg(e, "R8")
            return e.dma_start(out=self.GZ.rearrange("(s r) c -> s r c", s=8)[bass.ds(r, 1), :, :].rearrange("s (p a) c -> p (s a c)", p=128),
                               in_=self.GT.rearrange("(p a) c -> p (a c)", p=128))
        self.dyn("sp", put_g, [self.b_gt, gz_zb], (), [b_gz])
        b_hg = Buf()
        self.allreduce8(self.HZ, self.HT8, self.HG, [b_hz] + hz_zb, b_hg)
        b_gg = Buf()
        self.allreduce8(self.GZ, self.GT8, self.GG, [b_gz, gz_zb], b_gg, esize=4)
        b_ew = Buf()
        for src, dst, rows, cols in ((self.ew1, self.EW1, D, FE), (self.ew3, self.EW3, D, FE), (self.ew2, self.EW2, FE, D)):
            step = max(128, (1 << 20) // cols // 128 * 128)
            for r0 in range(0, rows, step):
                nrow = min(step, rows - r0)
                R.dma("pool", dst[r0:r0 + nrow, :], src[r0:r0 + nrow, :], (), (), [b_ew])
        cst = A.alloc([16], F32)
        b_cst = Buf()
        R.dma("sp", cst, self.cst[:, 4992:5008], (), [b_cst])
        xT = A.alloc([16, 512], BF16)
        b_xT = Buf()
        gts = A.alloc([4, 8], F32)
        gcol = A.alloc([4], F32)
        b_g = Buf()
        gT = A.alloc([NF, 512], BF16)
        b_gT = Buf()
        w13 = [A.alloc([2, 16, 128], BF16) for _ in range(2)]
        b_w13 = [Buf(), Buf()]
        w2s = [A.alloc([NF, 256], BF16) for _ in range(2)]
        b_w2 = [Buf(), Buf()]
        sA = [A.alloc([512], BF16) for _ in range(2)]
        b_sA = [Buf(), Buf()]
        ot = [A.alloc([256], F32) for _ in range(4)]
        b_ot = [Buf() for _ in range(4)]
        self.b_mo = Buf()
        wc = 0
        pc = 0
        w2c = 0
        oc = 0
        for s_ in range(8):
            for tq in range(4):
                r0 = s_ * NLOC + tq * 512
                R.dma("sp", xT, self.HG[s_ * D:(s_ + 1) * D, tq * 512:(tq + 1) * 512].rearrange("(a p) t -> p a t", p=128), [b_hg], [b_xT])
                R.dma("sp", gts, self.GG[r0:r0 + 512, :].rearrange("(a p) e -> p a e", p=128), [b_gg], [b_g])
                for a in range(4):
                    self.tt("dve", gts[:, a, :], gts[:, a, :], cst[:, 0:8], ALU.mult, [b_cst], [b_g])
                R.op("dve", lambda e: e.reduce_sum(out=gcol, in_=gts, axis=mybir.AxisListType.X), [b_g], [b_g])
                for f0 in range(0, FE, 128):
                    k = wc % 2
                    wc += 1
                    R.dma("sp", w13[k][:, 0, :, :], self.EW1[:, f0:f0 + 128].rearrange("(a p) n -> p a n", p=128), [b_ew], [b_w13[k]])
                    R.dma("sp", w13[k][:, 1, :, :], self.EW3[:, f0:f0 + 128].rearrange("(a p) n -> p a n", p=128), [b_ew], (), [b_w13[k]])
                    fidx = f0 // 128
                    pa = (pc % 2) * 2
                    sk = pc % 2
                    pc += 1
                    for dc in range(16):
                        self.mm(self.ps[pa][:, :], w13[k][:, 0, dc, :], xT[:, dc, :], dc == 0, dc == 15, [b_w13[k], b_xT], [self.pb[pa]])
                    for dc in range(16):
                        self.mm(self.ps[pa + 1][:, :], w13[k][:, 1, dc, :], xT[:, dc, :], dc == 0, dc == 15, [b_w13[k], b_xT], [self.pb[pa + 1]])
                    self.act(sA[sk], self.ps[pa][:, :], AF.Silu, [self.pb[pa]], [b_sA[sk]])
                    self.tt("dve", gT[:, fidx, :], sA[sk], self.ps[pa + 1][:, :], ALU.mult, [b_sA[sk], self.pb[pa + 1]],
                            [b_gT] if fidx == 0 else (), () if fidx == 0 else [b_gT])
                for c0 in range(0, D, 256):
                    k2 = w2c % 2
                    w2c += 1
                    R.dma("sp", w2s[k2], self.EW2[:, c0:c0 + 256].rearrange("(a p) n -> p a n", p=128), [b_ew], [b_w2[k2]])
                    for ts_ in range(4):
                        pi = 4 + (pc % 2)
                        pc += 1
                        for f in range(NF):
                            self.mm(self.ps[pi][:, 0:256], gT[:, f, ts_ * 128:(ts_ + 1) * 128], w2s[k2][:, f, :], f == 0, f == NF - 1,
                                    [b_gT, b_w2[k2]], [self.pb[pi]])
                        oi = oc % 4
                        oc += 1
                        self.ts("dve" if oi % 2 == 0 else "pool" if False else "dve", ot[oi], self.ps[pi][:, 0:256], gcol[:, ts_:ts_ + 1], None,
                                ALU.mult, None, [self.pb[pi], b_g], [b_ot[oi]])
                        R.dma("pool", self.MO[r0 + ts_ * 128:r0 + (ts_ + 1) * 128, c0:c0 + 256], ot[oi], [b_ot[oi]], (), [self.b_mo])
        R.barrier()
        self.b_ms = Buf()
        self.allreduce8(self.MO, self.MT8, self.MS, [self.b_mo], self.b_ms, esize=4)
        self.b_ff = Buf()

        def get_ff(e):
            r = self.dynreg(e, "R8")
            return e.dma_start(out=self.FF.rearrange("(p a) c -> p (a c)", p=128),
                               in_=self.MS.rearrange("(s r) c -> s r c", s=8)[bass.ds(r, 1), :, :].rearrange("s (p a) c -> p (s a c)", p=128))
        self.dyn("sp", get_ff, [self.b_ms], [self.b_ff])
        R.barrier()

    def l1_final(self):
        R, A = self.R, self.A
        A.reset()
        gt = A.alloc([D], F32)
        bt = A.alloc([D], F32)
        b_gb = Buf()
        R.dma("sp", gt, self.vecs[6:7, :].partition_broadcast(128), (), (), [b_gb])
        R.dma("sp", bt, self.vecs[7:8, :].partition_broadcast(128), (), (), [b_gb])
        hm = [A.alloc([D], F32) for _ in range(2)]
        ff = [A.alloc([D], F32) for _ in range(2)]
        b_h = [Buf(), Buf()]
        b_f = [Buf(), Buf()]
        st6 = A.alloc([4, 6], F32)
        mv = A.alloc([2], F32)
        rs = A.alloc([4], F32)
        last = []
        for tt in range(16):
            k = tt % 2
            R.dma("sp", hm[k], self.HM[tt * 128:(tt + 1) * 128, :], [self.b_hm], [b_h[k]])
            R.dma("sp", ff[k], self.FF[tt * 128:(tt + 1) * 128, :], [self.b_ff], [b_f[k]])
            self.stt("dve", ff[k], hm[k], ALPHA, ff[k], ALU.mult, ALU.add, [b_h[k]], [b_f[k]])
            self.ln_inplace(ff[k], gt, bt, b_f[k], (st6, mv, rs), b_gb)
            last.append(R.dma("pool", self.y[tt * 128:(tt + 1) * 128, :], ff[k], [b_f[k]], ()))
        return last


def _rel_bucket_np(d):
    d = np.maximum(d, 0)
    exact = 16
    nf = np.maximum(d, 1).astype(np.float32)
    large = exact + (np.log(nf / exact) / math.log(128 / exact) * (32 - exact)).astype(np.int32)
    large = np.minimum(large, 31)
    return np.where(d < exact, d, large)


def _consts_for(j, core):
    c = np.zeros((128, CSTW), np.float32)
    p = np.arange(128)
    c[:, 0:128] = (p[:, None] <= p[None, :])
    g = np.arange(64)
    c[0:64, 128:192] = (g[:, None] < g[None, :])
    c[127, 192:320] = 1.0
    c[0:64, 320:448] = 1.0
    for l in range(4):
        n = 4 * (l + 1)
        i = TS[j][l]
        ws = i + 1 - n
        nvirt = 2 * max(0, -ws)
        for u in range(2):
            wb_own = 2 * (n - 1) + u
            row = np.zeros(32, np.float32)
            row[wb_own:] = -1e30
            row[:nvirt] = -1e30
            c[:, 448 + (2 * l + u) * 32:448 + (2 * l + u + 1) * 32] = row[None, :]
    for wb in range(32):
        c[wb, 832 + wb * 128:832 + (wb + 1) * 128] = 1.0
    for l in range(4):
        n = 4 * (l + 1)
        ws = TS[j][l] + 1 - n
        for a in range(max(0, -ws)):
            c[:, 4928 + l * 16 + a] = -1e30
    c[:, 4992 + core] = 1.0
    return c


_CACHE = {}


def kernel(**inp):
    stop_after = inp.pop("_stop_after", None)
    x = np.asarray(inp["x"], np.float32)
    F0 = inp["ev_ffn_w1"].shape[-1]
    FE = inp["od_exp_w1"].shape[-1]
    key = (F0, FE, stop_after)
    if key not in _CACHE:
        pr = Prog(F0, FE, stop_after)
        pr.build()
        _CACHE[key] = pr
    pr = _CACHE[key]
    mats = {
        "w_in0": inp["ev_w_in"][0], "w_out0": inp["ev_w_out"][0], "w1": inp["ev_ffn_w1"][0], "w3": inp["ev_ffn_w3"][0],
        "w2": inp["ev_ffn_w2"][0], "w_in1": inp["od_w_in"][0], "w_uq": inp["od_w_uq"][0], "w_qidx": inp["od_w_qidx"][0],
        "w_uk": np.transpose(inp["od_w_uk"][0], (1, 0, 2)).reshape(512, D),
        "w_uv": np.transpose(inp["od_w_uv"][0], (1, 0, 2)).reshape(512, D),
        "w_out1": inp["od_w_out"][0], "router": inp["od_router"][0],
    }
    flat = np.zeros(pr.NR * 1024, np.float32)
    for name, K, N in pr.shapes:
        off = pr.lay[name][0]
        flat[off:off + K * N] = np.asarray(mats[name], np.float32).reshape(-1)
    flat = flat.reshape(pr.NR, 1024)
    vecs = np.zeros((16, D), np.float32)
    for i, nm in enumerate(["ev_ln1_g", "ev_ln1_b", "ev_ln2_g", "ev_ln2_b", "od_ln1_g", "od_ln1_b", "od_ln2_g", "od_ln2_b"]):
        vecs[i] = inp[nm][0]
    vecs[8, :512] = inp["od_q_norm_g"][0]
    vecs[9, :512] = inp["od_kv_norm_g"][0]
    table = np.asarray(inp["rel_table"], np.float32)
    tab31 = table[31:32, :].copy()
    s_l = np.arange(128)[:, None]
    yp = np.arange(384)[None, :]
    dd = yp - 128 - s_l
    bk = np.where(dd >= 0, _rel_bucket_np(dd), 31)
    tbraw = np.ascontiguousarray(np.transpose(table[bk, :], (0, 2, 1))).reshape(128, 16 * 384)
    in_maps = []
    for c in range(8):
        b, j = divmod(c, 4)
        xl = np.concatenate([x[b, t * 512:(t + 1) * 512] for t in TS[j]], 0)
        in_maps.append({
            "x_loc": np.ascontiguousarray(xl),
            "wsh": np.ascontiguousarray(flat[c * pr.NRS:(c + 1) * pr.NRS]),
            "vecs": vecs, "bfor": np.asarray(inp["ev_b_forget"], np.float32).reshape(1, 8),
            "tab31": tab31, "tbraw": tbraw, "cst": _consts_for(j, c),
            "ew1": np.ascontiguousarray(inp["od_exp_w1"][0, c]), "ew3": np.ascontiguousarray(inp["od_exp_w3"][0, c]),
            "ew2": np.ascontiguousarray(inp["od_exp_w2"][0, c]),
        })
    res = run_bass_kernel_spmd(pr.nc, in_maps, core_ids=list(range(8)))
    out = np.zeros((2, T, D), np.float32)
    for c in range(8):
        b, j = divmod(c, 4)
        yl = np.asarray(res.results[c]["y"], np.float32)
        for l, t in enumerate(TS[j]):
            out[b, t * 512:(t + 1) * 512] = yl[l * 512:(l + 1) * 512]
    return out
```

```python
import contextlib, math
import numpy as np
import concourse.bass as bass
import concourse.mybir as mybir
from concourse.bass_utils import run_bass_kernel_spmd

F32 = mybir.dt.float32
BF16 = mybir.dt.bfloat16
AF = mybir.ActivationFunctionType
ALU = mybir.AluOpType

D = 2048
T = 8192
NLOC = 2048
TS = [[j, 7 - j, 8 + j, 15 - j] for j in range(4)]
RANK_OF = {}
LIDX_OF = {}
for _j in range(4):
    for _l, _t in enumerate(TS[_j]):
        RANK_OF[_t] = _j
        LIDX_OF[_t] = _l
ALPHA = 4 ** 0.25
SCALE = 128 ** -0.5
NEG = -30000.0
GROUP4 = [[0, 1, 2, 3], [4, 5, 6, 7]]
GROUP8 = [[0, 1, 2, 3, 4, 5, 6, 7]]
PAIRS = [[0, 4], [1, 5], [2, 6], [3, 7]]
SL0 = 4224
SL1 = 4224


def gpos128(kt):
    i512, m = divmod(kt, 4)
    return RANK_OF[i512] * 16 + LIDX_OF[i512] * 4 + m


class Buf:
    __slots__ = ("w", "pw", "r")

    def __init__(self):
        self.w = None
        self.pw = []
        self.r = []


class Op:
    __slots__ = ("eng", "fn", "deps", "sig", "ev", "dma", "cc")


class Rec:
    EPOCH = 30000

    def __init__(self, nc, n_dma_sems=32):
        self.nc = nc
        self.ops = []
        self.engs = {"pe": nc.tensor, "dve": nc.vector, "act": nc.scalar, "pool": nc.gpsimd, "sp": nc.sync}
        self.n_dma_sems = n_dma_sems
        self.last = {}
        self.open_dma = []
        self.bar_deps = []
        self.bar_pending = set()

    def op(self, eng, fn, reads=(), writes=(), pwrites=(), dma=False, cc=False):
        ops = self.ops
        deps = set()
        for b in reads:
            if b.w is not None:
                deps.add(b.w)
            deps.update(b.pw)
        for b in writes:
            if b.w is not None:
                deps.add(b.w)
            deps.update(b.pw)
            deps.update(b.r)
        for b in pwrites:
            if b.w is not None:
                deps.add(b.w)
            deps.update(b.r)
        if eng in self.bar_pending:
            deps.update(self.bar_deps)
            self.bar_pending.discard(eng)
        i = len(ops)
        o = Op()
        o.eng = eng
        o.fn = fn
        o.dma = dma or cc
        o.cc = cc
        o.sig = False
        o.ev = None
        if eng == "pe" and not o.dma:
            o.deps = [d for d in deps if not (ops[d].eng == "pe" and not ops[d].dma)]
        else:
            o.deps = list(deps)
        ops.append(o)
        for b in writes:
            b.w = i
            b.pw = []
            b.r = []
        for b in pwrites:
            b.pw.append(i)
        for b in reads:
            if b.w == i:
                continue
            if not o.dma:
                b.r = [x for x in b.r if ops[x].dma or ops[x].eng != eng]
            b.r.append(i)
        if o.dma:
            self.open_dma.append(i)
        else:
            self.last[eng] = i
        return i

    def dma(self, eng, out, in_, reads=(), writes=(), pwrites=()):
        return self.op(eng, lambda e: e.dma_start(out=out, in_=in_), reads, writes, pwrites, dma=True)

    def barrier(self):
        self.bar_deps = list(self.last.values()) + list(self.open_dma)
        self.open_dma = []
        self.bar_pending = set(self.engs)

    def emit(self, st, final_wait_ops=()):
        nc = self.nc
        ops = self.ops
        for o in ops:
            for d in o.deps:
                ops[d].sig = True
        for d in final_wait_ops:
            ops[d].sig = True
        dma_sems = [st.enter_context(nc.semaphore(f"dq{i}")) for i in range(self.n_dma_sems)]
        dma_cnt = [0] * self.n_dma_sems
        rr = 0
        esem = {}
        ecnt = {}
        nep = {}
        for e in self.engs:
            esem[e] = st.enter_context(nc.semaphore(f"e_{e}_0"))
            ecnt[e] = 0
            nep[e] = 0
        waited = {e: {} for e in self.engs}
        for o in ops:
            E = self.engs[o.eng]
            need = {}
            for d in o.deps:
                s, v = ops[d].ev
                k = id(s)
                if k not in need or need[k][1] < v:
                    need[k] = (s, v)
            wd = waited[o.eng]
            for k, (s, v) in need.items():
                if wd.get(k, 0) >= v:
                    continue
                E.wait_ge(s, v)
                wd[k] = v
            ins = o.fn(E)
            if o.cc:
                if not hasattr(self, "_ccs"):
                    self._ccs = [st.enter_context(nc.semaphore(f"cc{i}")) for i in range(12)]
                    self._ccn = [0] * 12
                    self._ccr = 0
                q = self._ccr
                self._ccr = (q + 1) % 12
                self._ccn[q] += 1
                ins.then_inc(self._ccs[q])
                o.ev = (self._ccs[q], self._ccn[q])
            elif o.sig or o.dma:
                if o.dma:
                    q = rr
                    rr = (rr + 1) % self.n_dma_sems
                    dma_cnt[q] += 16
                    ins.then_inc(dma_sems[q], 16)
                    o.ev = (dma_sems[q], dma_cnt[q])
                else:
                    if ecnt[o.eng] >= self.EPOCH:
                        nep[o.eng] += 1
                        esem[o.eng] = st.enter_context(nc.semaphore(f"e_{o.eng}_{nep[o.eng]}"))
                        ecnt[o.eng] = 0
                    ecnt[o.eng] += 1
                    ins.then_inc(esem[o.eng], 1)
                    o.ev = (esem[o.eng], ecnt[o.eng])
        for d in final_wait_ops:
            s, v = ops[d].ev
            nc.sync.wait_ge(s, v)


class Arena:
    def __init__(self, t, cap16):
        self.t = t
        self.cap = cap16
        self.off = 0

    def reset(self):
        self.off = 0

    def alloc(self, shape, dt, parts=128):
        n = 1
        for s in shape:
            n *= s
        n16 = n * (2 if dt == F32 else 1)
        n16 = (n16 + 1) // 2 * 2
        o = self.off
        self.off += n16
        assert self.off <= self.cap, (self.off, self.cap)
        ap = self.t[0:parts, o:o + n16]
        if dt == F32:
            ap = ap.bitcast(F32)
        if dt != F32 and n16 != n:
            ap = ap[:, 0:n]
        if len(shape) == 2:
            ap = ap.rearrange("p (a b) -> p a b", b=shape[1])
        elif len(shape) == 3:
            ap = ap.rearrange("p (a b c) -> p a b c", b=shape[1], c=shape[2])
        return ap


def pack_layout(shapes):
    off = 0
    lay = {}
    for name, K, N in shapes:
        lay[name] = (off, K, N)
        off += K * N
        off = (off + 1023) // 1024 * 1024
    rows = off // 1024
    rows = (rows + 8 * 1024 - 1) // (8 * 1024) * (8 * 1024)
    return lay, rows


class Builder:
    def __init__(self, F0, FE, stop_after=None):
        self.F0 = F0
        self.FE = FE
        self.stop_after = stop_after
        self.shapes = [
            ("w_in0", D, 6152), ("w_out0", D, D), ("w1", D, F0), ("w3", D, F0), ("w2", F0, D),
            ("w_in1", D, 1104), ("w_uq", 512, D), ("w_qidx", 512, 1024), ("w_uk", 512, D), ("w_uv", 512, D),
            ("w_out1", D, D), ("router", D, 8),
        ]
        self.lay, self.NR = pack_layout(self.shapes)
        self.NRS = self.NR // 8

    def wview(self, name):
        off, K, N = self.lay[name]
        flat = self.WG.rearrange("r c -> (r c)")
        return flat[off:off + K * N].rearrange("(k n) -> k n", n=N)

    def mm(self, out, lhsT, rhs, start, stop, reads, writes, pw=()):
        self.R.op("pe", lambda e: e.matmul(out, lhsT=lhsT, rhs=rhs, start=start, stop=stop), reads, writes, pw)

    def tr(self, out, in_, ident, reads, writes, pw=()):
        self.R.op("pe", lambda e: e.transpose(out=out, in_=in_, identity=ident), reads, writes, pw)

    def act(self, out, in_, func, reads, writes, bias=None, scale=None, eng="act", pw=()):
        kw = {}
        if bias is not None:
            kw["bias"] = bias
        if scale is not None:
            kw["scale"] = scale
        self.R.op(eng, lambda e: e.activation(out=out, in_=in_, func=func, **kw), reads, writes, pw)

    def copy(self, eng, out, in_, reads, writes, pw=()):
        if eng == "act":
            self.R.op("act", lambda e: e.copy(out=out, in_=in_), reads, writes, pw)
        else:
            self.R.op(eng, lambda e: e.tensor_copy(out=out, in_=in_), reads, writes, pw)

    def tt(self, eng, out, in0, in1, op, reads, writes, pw=()):
        self.R.op(eng, lambda e: e.tensor_tensor(out=out, in0=in0, in1=in1, op=op), reads, writes, pw)

    def ts(self, eng, out, in0, s1, s2, op0, op1, reads, writes, accum_out=None, pw=()):
        if op1 is None:
            self.R.op(eng, lambda e: e.tensor_scalar(out=out, in0=in0, scalar1=s1, scalar2=None, op0=op0), reads, writes, pw)
        elif accum_out is None:
            self.R.op(eng, lambda e: e.tensor_scalar(out=out, in0=in0, scalar1=s1, scalar2=s2, op0=op0, op1=op1), reads, writes, pw)
        else:
            self.R.op(eng, lambda e: e.tensor_scalar(out=out, in0=in0, scalar1=s1, scalar2=s2, op0=op0, op1=op1, accum_out=accum_out), reads, writes, pw)

    def stt(self, eng, out, in0, scalar, in1, op0, op1, reads, writes, pw=()):
        self.R.op(eng, lambda e: e.scalar_tensor_tensor(out=out, in0=in0, scalar=scalar, in1=in1, op0=op0, op1=op1), reads, writes, pw)

    def dyn(self, eng, fn, reads, writes, pw=()):
        self.R.op(eng, fn, reads, writes, pw, dma=True)

    def memset(self, eng, ap, val, writes):
        self.R.op(eng, lambda e: e.memset(ap, val), (), writes)

    def dyn_put(self, eng, dram, rows_per_slot, row0, nrows, cols, src, group, reads, pwrites):
        c0, c1 = cols

        def fn(e):
            r = e.partition_id() % group
            return e.dma_start(out=dram[bass.ds(r * rows_per_slot + row0, nrows), c0:c1], in_=src)
        self.R.op(eng, fn, reads, (), pwrites, dma=True)

    def allreduce(self, groups, src, dst, reads, outbuf, esize=2):
        rows, cols = src.shape[0], src.shape[1]
        per = max(1, (4 * 1024 * 1024 // esize) // cols)
        for r0 in range(0, rows, per):
            n = min(per, rows - r0)
            self.R.op("pool", (lambda e, r0=r0, n=n: e.collective_compute(
                "AllReduce", ALU.add, replica_groups=groups, ins=[src[r0:r0 + n, :].opt()], outs=[dst[r0:r0 + n, :].opt()])),
                reads, (), (), cc=True)
            outbuf.pw.append(len(self.R.ops) - 1)

    def allreduce8(self, src, tmp, dst, reads, outbuf, esize=2, row_lo=0, row_hi=None):
        rows, cols = src.shape[0], src.shape[1]
        if row_hi is None:
            row_hi = rows
        per = max(1, (4 * 1024 * 1024 // esize) // cols)
        for r0 in range(row_lo, row_hi, per):
            n = min(per, row_hi - r0)
            self.R.op("pool", (lambda e, r0=r0, n=n: e.collective_compute(
                "AllReduce", ALU.add, replica_groups=GROUP4, ins=[src[r0:r0 + n, :].opt()], outs=[tmp[r0:r0 + n, :].opt()])),
                reads, (), (), cc=True)
            mid = Buf()
            mid.pw.append(len(self.R.ops) - 1)
            self.R.op("pool", (lambda e, r0=r0, n=n: e.collective_compute(
                "AllReduce", ALU.add, replica_groups=PAIRS, ins=[tmp[r0:r0 + n, :].opt()], outs=[dst[r0:r0 + n, :].opt()])),
                [mid], (), (), cc=True)
            outbuf.pw.append(len(self.R.ops) - 1)

    def zero_fill(self, dram, rows, cols, bufs):
        if cols > 8192:
            for r in range(0, rows, 128):
                for c in range(0, cols, 8192):
                    w = min(8192, cols - c)
                    b = Buf()
                    self.R.dma("sp", dram[r:r + 128, c:c + w], self.zt[:, 0:w], reads=[self.b_zt], writes=[b])
                    bufs.append(b)
            return
        per = 8192 // cols * 128
        r = 0
        while r < rows:
            n = min(per, rows - r)
            b = Buf()
            a = n // 128
            self.R.dma("sp", dram[r:r + n, :].rearrange("(p a) c -> p (a c)", p=128), self.zt[:, 0:a * cols],
                       reads=[self.b_zt], writes=[b])
            bufs.append(b)
            r += n

    def layer_norm(self, z, gt, bt, out, bz, bout, tmpst, reads_extra=()):
        st6, mv, rs = tmpst
        bs = Buf()
        for c in range(4):
            self.R.op("dve", (lambda e, c=c: e.bn_stats(out=st6[:, c, :], in_=z[:, c * 512:(c + 1) * 512])), [bz], [bs])
        self.R.op("dve", lambda e: e.bn_aggr(out=mv, in_=st6), [bs], [bs])
        self.ts("dve", rs[:, 0:1], mv[:, 1:2], 1e-5, None, ALU.add, None, [bs], [bs])
        self.act(rs[:, 1:2], rs[:, 0:1], AF.Sqrt, [bs], [bs])
        self.R.op("dve", lambda e: e.reciprocal(out=rs[:, 2:3], in_=rs[:, 1:2]), [bs], [bs])
        self.ts("dve", out, z, mv[:, 0:1], rs[:, 2:3], ALU.subtract, ALU.mult, [bz, bs], [bout])
        self.tt("pool", out, out, gt, ALU.mult, [bout] + list(reads_extra), [bout])
        self.tt("dve", out, out, bt, ALU.add, [bout] + list(reads_extra), [bout])


PADT = 1536
NT = 20
PT3 = 3
OFF_K = 0
OFF_V = 2048 * 512
OFF_X = 2 * 2048 * 512
SLOT = 2146304
PUTB = [(3, 0), (7, 1), (11, 0), (15, 1)]
TSC = [(0, 1), (7, -1), (8, 1), (15, -1)]
CSTW = 5120


class Prog(Builder):
    def build(self):
        nc = bass.Bass("TRN2", target_bir_lowering=False)
        self.nc = nc
        F0, FE = self.F0, self.FE
        dt = nc.dram_tensor
        self.x_loc = dt("x_loc", [NLOC, D], F32, kind="ExternalInput").ap()
        self.wsh = dt("wsh", [self.NRS, 1024], F32, kind="ExternalInput").ap()
        self.vecs = dt("vecs", [16, D], F32, kind="ExternalInput").ap()
        self.bfor = dt("bfor", [1, 8], F32, kind="ExternalInput").ap()
        self.tab31 = dt("tab31", [1, 16], F32, kind="ExternalInput").ap()
        self.tbraw = dt("tbraw", [128, 16 * 384], F32, kind="ExternalInput").ap()
        self.cst = dt("cst", [128, CSTW], F32, kind="ExternalInput").ap()
        self.ew1 = dt("ew1", [D, FE], F32, kind="ExternalInput").ap()
        self.ew3 = dt("ew3", [D, FE], F32, kind="ExternalInput").ap()
        self.ew2 = dt("ew2", [FE, D], F32, kind="ExternalInput").ap()
        self.y = dt("y", [NLOC, D], F32, kind="ExternalOutput").ap()
        self.WZ = dt("WZ", [self.NR, 1024], BF16).ap()
        self.WG = dt("WG", [self.NR, 1024], BF16).ap()
        self.WT = dt("WT", [self.NR, 1024], BF16).ap()
        self.EXL = dt("EXL", [4, SLOT], BF16).ap()
        self.EXZ = dt("EXZ", [NT, SLOT], BF16).ap()
        self.EXG = dt("EXG", [NT, SLOT], BF16).ap()
        self.WIN = [dt(f"WIN{l}", [4 * (l + 1), SLOT], BF16).ap() for l in range(4)]
        self.CTS = dt("CTS", [NT, SLOT], BF16).ap()
        self.WZ3 = self.WZ.rearrange("(s r) c -> s r c", s=8)
        self.QI = dt("QI", [1024, NLOC], BF16).ap()
        self.WS = dt("WS", [NLOC, 32], F32).ap()
        self.MK = [dt(f"MK{l}", [16 * (l + 1), 128, 512], BF16).ap() for l in range(4)]
        self.GT = dt("GT", [NLOC, 8], F32).ap()
        self.HZ = dt("HZ", [8 * D, NLOC], BF16).ap()
        self.HT8 = dt("HT8", [8 * D, NLOC], BF16).ap()
        self.HG = dt("HG", [8 * D, NLOC], BF16).ap()
        self.GZ = dt("GZ", [8 * NLOC, 8], F32).ap()
        self.GT8 = dt("GT8", [8 * NLOC, 8], F32).ap()
        self.GG = dt("GG", [8 * NLOC, 8], F32).ap()
        self.EW1 = dt("EW1", [D, FE], BF16).ap()
        self.EW3 = dt("EW3", [D, FE], BF16).ap()
        self.EW2 = dt("EW2", [FE, D], BF16).ap()
        self.MO = dt("MO", [8 * NLOC, D], F32).ap()
        self.MT8 = dt("MT8", [8 * NLOC, D], F32).ap()
        self.MS = dt("MS", [8 * NLOC, D], F32).ap()
        self.FF = dt("FF", [NLOC, D], F32).ap()
        self.QT0 = dt("QT0", [2048, NLOC], BF16).ap()
        self.ATT = dt("ATT", [NLOC, 2048], BF16).ap()
        self.HM = dt("HM", [NLOC, D], F32).ap()
        self.HMT = dt("HMT", [D, NLOC], BF16).ap()
        self.H1 = dt("H1", [NLOC, D], F32).ap()
        self.H1T = dt("H1T", [D, NLOC], BF16).ap()
        with contextlib.ExitStack() as st:
            self.st = st
            ar_t = st.enter_context(nc.sbuf_tensor("arena", [128, 84 * 1024], BF16))
            cn_t = st.enter_context(nc.sbuf_tensor("consts", [128, 10 * 1024], BF16))
            self.A = Arena(ar_t, 84 * 1024)
            self.C = Arena(cn_t, 10 * 1024)
            self.ps = [st.enter_context(nc.psum_tensor(f"ps{i}", [128, 512], F32)) for i in range(8)]
            self.pb = [Buf() for _ in range(8)]
            self.R = Rec(nc)
            self.consts()
            self.phase_w()
            sa = self.stop_after
            if sa == "w":
                last = self.copy_out(self.WG[0:NLOC, :].bitcast(F32).rearrange("r (a c) -> (r a) c", c=2048) if False else self.x_loc)
            else:
                self.phase_a()
                if sa == "a":
                    last = self.copy_out(self.x_loc)
                else:
                    self.phase_b()
                    if sa == "b":
                        last = self.copy_out(self.x_loc)
                    else:
                        self.phase_c1("w_out0", self.x_loc, 0, self.HM, self.HMT)
                        self.phase_c2()
                        if sa == 0:
                            last = self.copy_out(self.H1)
                        else:
                            last = self.layer1()
            self.R.emit(st, final_wait_ops=last)
        return nc

    def copy_out(self, src):
        R, A = self.R, self.A
        R.barrier()
        A.reset()
        t = [A.alloc([D], F32) for _ in range(2)]
        b = [Buf(), Buf()]
        last = []
        for tt in range(16):
            k = tt % 2
            R.dma("sp", t[k], src[tt * 128:(tt + 1) * 128, :], (), [b[k]])
            last.append(R.dma("sp", self.y[tt * 128:(tt + 1) * 128, :], t[k], [b[k]], ()))
        return last

    def consts(self):
        C, R = self.C, self.R
        self.zt = C.alloc([8192], BF16)
        self.b_zt = Buf()
        self.memset("pool", self.zt, 0.0, [self.b_zt])
        self.b_c = Buf()
        self.identb = C.alloc([128], BF16)
        self.identf = C.alloc([128], F32)
        self.memset("pool", self.identb, 0.0, [self.b_c])
        R.op("pool", lambda e: e.affine_select(out=self.identb, in_=self.identb, pattern=[[-1, 128]],
                                               compare_op=ALU.not_equal, fill=1.0, base=0, channel_multiplier=1),
             [self.b_c], [self.b_c])
        self.copy("dve", self.identf, self.identb, [self.b_c], [self.b_c])
        self.ones1 = C.alloc([1], F32)
        self.memset("dve", self.ones1, 1.0, [self.b_c])
        self.t31 = C.alloc([16], F32)
        R.dma("sp", self.t31, self.tab31.partition_broadcast(128), (), [self.b_c])
        self.bf_t = C.alloc([8], F32)
        R.dma("sp", self.bf_t, self.bfor.partition_broadcast(128), (), [self.b_c])
        self.CA = C.alloc([512], BF16)
        self.memset("pool", self.CA, 0.0, [self.b_c])
        R.op("pool", lambda e: e.affine_select(out=self.CA, in_=self.CA, pattern=[[1, 512]],
                                               compare_op=ALU.is_ge, fill=NEG, base=0, channel_multiplier=-1),
             [self.b_c], [self.b_c])

    def dynreg(self, e, kind):
        if not hasattr(self, "_dr"):
            self._dr = {}
        key = (id(e), kind)
        if key not in self._dr:
            if kind == "R8":
                v = e.partition_id()
            elif kind == "R":
                v = e.partition_id() % 4
            else:
                v = self.dynreg(e, "R") * (-1) + 3
            self._dr[key] = e.snap(v)
        return self._dr[key]

    def put_slots(self, reads):
        self.b_exz = Buf()
        for l in range(4):
            base, sel = PUTB[l]

            def fn(e, l=l, base=base, sel=sel):
                reg = self.dynreg(e, "R" if sel == 0 else "RP")
                return e.dma_start(out=self.EXZ[base:NT, :][bass.ds(reg, 1), :].rearrange("a (p c) -> p (a c)", p=128),
                                   in_=self.EXL[l:l + 1, :].rearrange("a (p c) -> p (a c)", p=128))
            self.dyn("pool", fn, reads + self.ex_zb, (), [self.b_exz])

    def exchange(self):
        R = self.R
        self.b_exg = Buf()
        self.allreduce(GROUP4, self.EXZ[PT3:PT3 + 16, :].rearrange("a (p c) -> (a p) c", p=128),
                       self.EXG[PT3:PT3 + 16, :].rearrange("a (p c) -> (a p) c", p=128), [self.b_exz] + self.ex_zb, self.b_exg)
        self.b_win = [Buf() for _ in range(4)]
        for l in range(4):
            n = 4 * (l + 1)

            def fn(e, l=l, n=n):
                reg = self.dynreg(e, "R" if l % 2 == 0 else "RP")
                return e.dma_start(out=self.WIN[l].rearrange("a (p c) -> p a c", p=128),
                                   in_=self.EXG[bass.ds(reg, n), :].rearrange("a (p c) -> p a c", p=128))
            self.dyn("sp", fn, [self.b_exg] + self.exg_zb, [self.b_win[l]])

    def phase_w(self):
        R = self.R
        zb = []
        self.zero_fill(self.WZ, self.NR, 1024, zb)
        self.ex_zb = []
        exz2 = self.EXZ.rearrange("a (p c) -> (a p) c", p=128)
        self.zero_fill(exz2, NT * 128, SLOT // 128, self.ex_zb)
        bput = Buf()
        NRS = self.NRS
        def fn(e):
            r = self.dynreg(e, "R8")
            return e.dma_start(out=self.WZ3[bass.ds(r, 1), :, :].rearrange("s r c -> (s r c)").rearrange("(p x) -> p x", p=128),
                               in_=self.wsh.rearrange("r c -> (r c)").rearrange("(p x) -> p x", p=128))
        self.dyn("pool", fn, zb, (), [bput])
        self.b_wg = Buf()
        self.b_wg0 = Buf()
        off, K, N = self.lay["w_in0"]
        self.w_split = ((off + K * N + 1023) // 1024 + 2047) // 2048 * 2048
        self.w_reads = [bput] + zb
        self.allreduce8(self.WZ, self.WT, self.WG, self.w_reads, self.b_wg0, row_hi=self.w_split)
        self.b_wg.pw.extend(self.b_wg0.pw)
        self.exg_zb = []
        exg2 = self.EXG.rearrange("a (p c) -> (a p) c", p=128)
        self.zero_fill(exg2[0:PT3 * 128, :], PT3 * 128, SLOT // 128, self.exg_zb)
        self.zero_fill(exg2[(PT3 + 16) * 128:NT * 128, :], (NT - PT3 - 16) * 128, SLOT // 128, self.exg_zb)

    def rest_weights(self):
        self.allreduce8(self.WZ, self.WT, self.WG, self.w_reads, self.b_wg, row_lo=self.w_split)

    def transpose16(self, src_bf, b_src, dstT, col0, b_dst):
        for half in range(2):
            pi = 6 + half
            pv = self.ps[pi][:, :].bitcast(BF16)
            for q in range(8):
                dc = half * 8 + q
                self.tr(pv[:, q * 128:(q + 1) * 128], src_bf[:, dc * 128:(dc + 1) * 128], self.identb,
                        [b_src, self.b_c], [self.pb[pi]])
            dst = dstT[:, half * 8:(half + 1) * 8, col0:col0 + 128]
            src = pv.rearrange("p (a b) -> p a b", b=128)
            if half == 0:
                self.copy("dve", dst, src, [self.pb[pi]], [b_dst])
            else:
                self.copy("act", dst, src, [self.pb[pi]], (), [b_dst])

    def phase_a(self):
        R, A = self.R, self.A
        A.reset()
        xT = A.alloc([16, NLOC], BF16)
        b_xT = [Buf() for _ in range(16)]
        xs = [A.alloc([D], F32) for _ in range(2)]
        b_xs = [Buf(), Buf()]
        xb = [A.alloc([D], BF16) for _ in range(2)]
        b_xb = [Buf(), Buf()]
        for tt in range(16):
            k = tt % 2
            R.dma("sp", xs[k], self.x_loc[tt * 128:(tt + 1) * 128, :], (), [b_xs[k]])
            self.copy("act", xb[k], xs[k], [b_xs[k]], [b_xb[k]])
            self.transpose16(xb[k], b_xb[k], xT, tt * 128, b_xT[tt])
        W = self.wview("w_in0")
        slab = [A.alloc([16, 512], BF16) for _ in range(2)]
        b_slab = [Buf(), Buf()]
        stg = [A.alloc([512], BF16) for _ in range(4)]
        b_stg = [Buf() for _ in range(4)]
        kms = A.alloc([8, 8], F32)
        kmb = A.alloc([8, 8], BF16)
        b_km = Buf()
        lfs = A.alloc([16, 8], F32)
        lft = A.alloc([2, 8], F32)
        b_lf = Buf()
        b_lft = Buf()
        self.b_exl = Buf()
        self.b_qt0 = Buf()
        EXLk = self.EXL[:, OFF_K:OFF_K + 2048 * 512].rearrange("l (f c) -> l f c", c=512)
        EXLv = self.EXL[:, OFF_V:OFF_V + 2048 * 512].rearrange("l (h p c) -> l h p c", h=16, p=128)
        lf3 = A.alloc([3, 16, 8], BF16)
        lfr = A.alloc([16, 8], F32)
        fm = [(0, "q", 0), (512, "q", 4), (1024, "k", 0), (1536, "k", 4),
              (3072, "q", 8), (3584, "q", 12), (4096, "k", 8), (4608, "k", 12)]
        tm = [(2048, 0), (2560, 4), (5120, 8), (5632, 12)]
        slabs = [(c, "fm", kind, hb) for c, kind, hb in fm] + [(c, "tm", None, hb) for c, hb in tm] + [(6144, "f", None, 0)]

        def load_slab(i):
            col0, typ = slabs[i][0], slabs[i][1]
            k = i % 2
            if typ == "f":
                R.dma("sp", slab[k][:, :, 0:8], W[:, 6144:6152].rearrange("(a p) n -> p a n", p=128), [self.b_wg0], [b_slab[k]])
            else:
                R.dma("sp", slab[k], W[:, col0:col0 + 512].rearrange("(a p) n -> p a n", p=128), [self.b_wg0], [b_slab[k]])
        load_slab(0)
        pcnt = 0
        scnt = 0
        for i, (col0, typ, kind, hb) in enumerate(slabs):
            k = i % 2
            if i + 1 < len(slabs):
                load_slab(i + 1)
            if typ == "fm":
                for l in range(4):
                    for hc in range(4):
                        pi = pcnt % 4
                        pcnt += 1
                        for dc in range(16):
                            self.mm(self.ps[pi][:, :], slab[k][:, dc, hc * 128:(hc + 1) * 128], xT[:, dc, l * 512:(l + 1) * 512],
                                    dc == 0, dc == 15, [b_slab[k]] + b_xT[l * 4:l * 4 + 4], [self.pb[pi]])
                        si = scnt % 4
                        scnt += 1
                        h = hb + hc
                        feat0 = h * 128
                        if kind == "q":
                            self.act(stg[si], self.ps[pi][:, :], AF.Copy, [self.pb[pi]], [b_stg[si]], scale=SCALE)
                            R.dma("pool", self.QT0[feat0:feat0 + 128, l * 512:(l + 1) * 512], stg[si], [b_stg[si]], (), [self.b_qt0])
                        else:
                            self.copy("dve", stg[si], self.ps[pi][:, :], [self.pb[pi]], [b_stg[si]])
                            if h < 8:
                                R.op("dve", (lambda e, h=h, l=l, pi=pi: e.reduce_sum(
                                    out=kms[:, h, 2 * l:2 * l + 2],
                                    in_=self.ps[pi][:, :].rearrange("p (a b) -> p a b", b=256),
                                    axis=mybir.AxisListType.X)), [self.pb[pi]], (), [b_km])

                            R.dma("pool", EXLk[l, feat0:feat0 + 128, :], stg[si], [b_stg[si]], (), [self.b_exl])
            elif typ == "tm":
                for tt in range(16):
                    pi = pcnt % 4
                    pcnt += 1
                    for dc in range(16):
                        self.mm(self.ps[pi][:, :], xT[:, dc, tt * 128:(tt + 1) * 128], slab[k][:, dc, :],
                                dc == 0, dc == 15, [b_slab[k], b_xT[tt]], [self.pb[pi]])
                    si = scnt % 4
                    scnt += 1
                    self.copy("act" if tt % 2 else "dve", stg[si], self.ps[pi][:, :], [self.pb[pi]], [b_stg[si]])

                    sub = tt % 4
                    R.dma("pool", EXLv[tt // 4, hb:hb + 4, :, sub * 128:(sub + 1) * 128].rearrange("h p d -> p h d"),
                          stg[si].rearrange("p (h d) -> p h d", d=128), [b_stg[si]], (), [self.b_exl])
            else:
                for tt in range(16):
                    pi = pcnt % 4
                    pcnt += 1
                    for dc in range(16):
                        self.mm(self.ps[pi][:, 0:8], xT[:, dc, tt * 128:(tt + 1) * 128], slab[k][:, dc, 0:8],
                                dc == 0, dc == 15, [b_slab[k], b_xT[tt]], [self.pb[pi]])
                    self.tt("dve", lft[:, 0, :], self.ps[pi][:, 0:8], self.bf_t, ALU.add, [self.pb[pi], self.b_c], [b_lft])
                    self.act(lft[:, 1, :], lft[:, 0, :], AF.Exp, [b_lft], [b_lft], scale=-1.0)
                    self.act(lft[:, 0, :], lft[:, 1, :], AF.Ln, [b_lft, self.b_c], [b_lft], bias=self.ones1)
                    self.ts("dve", lfs[:, tt, :], lft[:, 0, :], -1.0, None, ALU.mult, None, [b_lft], (), pw=[b_lf])
        self.ts("dve", kmb, kms, 1.0 / 256.0, None, ALU.mult, None, [b_km], [b_km])
        self.copy("dve", lf3[:, 0, :, :], lfs, [b_lf], [b_lf])
        self.tt("dve", lfr, lfs, lf3[:, 0, :, :], ALU.subtract, [b_lf], [b_lf])
        self.copy("dve", lf3[:, 1, :, :], lfr, [b_lf], [b_lf])
        self.tt("dve", lfr, lfr, lf3[:, 1, :, :], ALU.subtract, [b_lf], [b_lf])
        self.copy("dve", lf3[:, 2, :, :], lfr, [b_lf], [b_lf])
        for l in range(4):
            R.dma("pool", self.EXL[l, OFF_X:OFF_X + 2048].rearrange("(p h u) -> p h u", p=128, u=2),
                  kmb[:, :, 2 * l:2 * l + 2], [b_km], (), [self.b_exl])
            R.dma("pool", self.EXL[l, OFF_X + 2048:OFF_X + 2048 + 12288].rearrange("(p k a h) -> p k a h", p=128, k=3, h=8),
                  lf3[:, :, 4 * l:4 * l + 4, :], [b_lf], (), [self.b_exl])
        self.put_slots([self.b_exl])
        self.exchange()
        R.barrier()

    def phase_b(self):
        R, A = self.R, self.A
        A.reset()
        self.rest_weights()
        cst = A.alloc([CSTW], F32)
        b_cst = Buf()
        R.dma("sp", cst, self.cst, (), [b_cst])
        U = cst[:, 0:128]
        LT = cst[0:64, 128:192]
        SEL127 = cst[:, 192:320]
        ones64 = cst[0:64, 320:448]
        ESELf = cst[0:32, 832:832 + 4096]
        lf = A.alloc([64, 8], F32)
        ccb = A.alloc([NT, 64], BF16)
        cc = ccb.rearrange("p a c -> p (a c)").bitcast(F32).rearrange("p (g h) -> p g h", h=8)
        tot = A.alloc([8], F32)
        rhsM = A.alloc([64, 8], F32)
        b_lf = Buf()
        b_cc = Buf()
        lf3g = A.alloc([16, 3, 32], BF16)
        R.dma("sp", lf3g, self.EXG[PT3:PT3 + 16, OFF_X + 2048:OFF_X + 2048 + 12288].rearrange("a (p k c) -> p a k c", p=128, k=3),
              [self.b_exg], [b_lf])
        lfv = lf.rearrange("p (a s) h -> p a (s h)", s=4)
        self.tt("dve", lfv, lf3g[:, :, 0, :], lf3g[:, :, 1, :], ALU.add, [b_lf], [b_lf])
        self.tt("dve", lfv, lfv, lf3g[:, :, 2, :], ALU.add, [b_lf], [b_lf])
        lf2 = lf.rearrange("p g h -> p (g h)")
        p0 = self.ps[6]
        for h in range(8):
            self.mm(p0[0:64, h:h + 1], lf[:, :, h], self.ones1, True, True, [b_lf, self.b_c], [self.pb[6]])
        self.copy("dve", tot[0:64, :], p0[0:64, 0:8], [self.pb[6]], [b_cc])
        for h in range(8):
            self.ts("dve", rhsM[0:64, :, h], LT, tot[0:64, h:h + 1], None, ALU.mult, None, [b_cst, b_cc], [b_cc])
        p1 = self.ps[7]
        self.mm(p1[:, :], U, lf2, True, False, [b_cst, b_lf], [self.pb[7]])
        self.mm(p1[:, :], ones64, rhsM[0:64, :, :].rearrange("p g h -> p (g h)"), False, True, [b_cst, b_cc], [self.pb[7]])
        self.memset("dve", cc, 30000.0, [b_cc])
        self.copy("dve", cc[:, PT3 * 4:PT3 * 4 + 64, :].rearrange("p g h -> p (g h)"), p1[:, :], [self.pb[7]], [b_cc])
        b_ct = Buf()
        R.dma("act", self.CTS[:, 0:8192].rearrange("a (p c) -> p a c", p=128), ccb, [b_cc], [b_ct])
        tbr = A.alloc([8, 384], F32)
        TB = A.alloc([8, 384], BF16)
        OWN = A.alloc([8, 256], BF16)
        eselb = A.alloc([32 * 128], BF16, parts=32)
        b_tb = Buf()
        R.dma("sp", tbr, self.tbraw[:, 0:8 * 384].rearrange("p (h w) -> p h w", w=384), (), [b_tb])
        for h in range(8):
            self.ts("dve", TB[:, h, :], tbr[:, h, :], self.t31[:, h:h + 1], None, ALU.subtract, None, [b_tb, self.b_c], [b_tb])
        for h in range(8):
            self.tt("dve", OWN[:, h, :], TB[:, h, 128:384], self.CA[:, 0:256], ALU.add, [b_tb, self.b_c], [b_tb])
        self.copy("dve", eselb, ESELf, [b_cst], [b_tb])
        KT = [A.alloc([T], BF16) for _ in range(2)]
        VA = [A.alloc([64, 130], BF16) for _ in range(2)]
        QT = [A.alloc([512], BF16) for _ in range(2)]
        KM = [A.alloc([32], BF16) for _ in range(2)]
        b_kv = [Buf(), Buf()]
        for k in range(2):
            self.memset("dve", VA[k][:, :, 128:130], 1.0, [b_kv[k]])
            self.memset("dve", KM[k], 0.0, [b_kv[k]])
        CWb = A.alloc([16, 64], BF16)
        CW = CWb.rearrange("p a c -> p (a c)").bitcast(F32).rearrange("p (g h) -> p g h", h=8)
        CLB = A.alloc([64, 8], F32)
        b_cw = Buf()
        cb = [A.alloc([64], F32) for _ in range(2)]
        b_cb = [Buf(), Buf()]
        PT = [A.alloc([256], BF16) for _ in range(2)]
        b_pt = [Buf(), Buf()]
        gm = A.alloc([2, 32], F32)
        g2 = A.alloc([2, 32], F32)
        m8 = A.alloc([2, 8], F32)
        maskT = A.alloc([256], BF16, parts=32)
        b_gm = Buf()
        b_mask = Buf()
        ost = [A.alloc([128], BF16) for _ in range(4)]
        b_ost = [Buf() for _ in range(4)]
        rec = A.alloc([4], F32)
        b_rec = [Buf() for _ in range(4)]
        self.b_att = Buf()
        iters = [(l, h) for l in range(4) for h in range(16)]

        def loads(it):
            l, h = iters[it]
            k = it % 2
            n = 4 * (l + 1)
            Wk = self.WIN[l][:, OFF_K:OFF_K + 2048 * 512].rearrange("a (f c) -> a f c", c=512)
            Wv = self.WIN[l][:, OFF_V:OFF_V + 2048 * 512].rearrange("a (h p c) -> a h p c", h=16, p=128)
            Wm = self.WIN[l][:, OFF_X:OFF_X + 2048].rearrange("a (p c) -> a p c", p=128)
            R.dma("sp", KT[k][:, 0:n * 512].rearrange("p (a c) -> p a c", c=512),
                  Wk[:, h * 128:(h + 1) * 128, :].rearrange("a p c -> p a c"), [self.b_win[l]], [b_kv[k]])
            for a in range(n):
                R.dma("sp", VA[k][:, 4 * a:4 * a + 4, 0:128], Wv[a, h, :, :].rearrange("p (s d) -> p s d", d=128),
                      [self.b_win[l]], (), [b_kv[k]])
            R.dma("sp", QT[k], self.QT0[h * 128:(h + 1) * 128, l * 512:(l + 1) * 512], [self.b_qt0], (), [b_kv[k]])
            if h < 8:
                R.dma("sp", KM[k][:, 0:2 * n].rearrange("p (a u) -> p a u", u=2),
                      Wm[:, :, 2 * h:2 * h + 2].rearrange("a p u -> p a u"), [self.b_win[l]], (), [b_kv[k]])
        scnt = 0
        ocnt = 0
        cbn = 0
        loads(0)
        for it, (l, h) in enumerate(iters):
            k = it % 2
            n = 4 * (l + 1)
            moba = h < 8
            if h == 0:
                self.dyn("sp", (lambda e, n=n, l=l: e.dma_start(
                    out=CWb[:, 0:n, :],
                    in_=self.CTS[bass.ds(self.dynreg(e, "R" if l % 2 == 0 else "RP"), n), 0:8192].rearrange("a (p c) -> p a c", p=128))),
                    [b_ct], [b_cw])
                for c in range(0, 32 * n, 512):
                    self.mm(self.ps[6][:, :], SEL127, CW.rearrange("p g h -> p (g h)")[:, c:c + 512], True, True,
                            [b_cst, b_cw], [self.pb[6]])
                    self.copy("dve", CLB.rearrange("p g h -> p (g h)")[:, c:c + 512], self.ps[6][:, :], [self.pb[6]], (), [b_cw])
            if it + 1 < len(iters):
                loads(it + 1)
            for u in range(2):
                kd0 = 4 * (n - 1) + 2 * u
                kd1 = kd0 + 1
                kp = kd0 - 1
                q0 = u * 256
                if moba:
                    bv = cst[:, 448 + (2 * l + u) * 32:448 + (2 * l + u + 1) * 32]
                    for v in range(2):
                        self.mm(self.ps[4][:, v * 32:v * 32 + 32], QT[k][:, q0 + v * 128:q0 + (v + 1) * 128], KM[k][:, 0:32],
                                True, True, [b_kv[k]], [self.pb[4]])
                    for v in range(2):
                        self.tt("dve", gm[:, v, :], self.ps[4][:, v * 32:v * 32 + 32], bv, ALU.add, [self.pb[4], b_cst],
                                [b_gm] if v == 0 else (), () if v == 0 else [b_gm])
                    for v in range(2):
                        R.op("dve", (lambda e, v=v: e.max(out=m8[:, v, :], in_=gm[:, v, :])), [b_gm], [b_gm])
                    for v in range(2):
                        self.ts("dve", g2[:, v, :], gm[:, v, :], m8[:, v, 2:3], None, ALU.is_ge, None, [b_gm], [b_gm])
                    self.ts("dve", gm, gm, -1e29, None, ALU.is_gt, None, [b_gm], [b_gm])
                    self.tt("dve", g2, g2, gm, ALU.mult, [b_gm], [b_gm])
                    self.ts("dve", g2, g2, -NEG, NEG, ALU.mult, ALU.add, [b_gm], [b_gm])
                    for v in range(2):
                        self.tr(self.ps[5][0:32, v * 128:(v + 1) * 128], g2[:, v, :], self.identf, [b_gm, self.b_c], [self.pb[5]])
                    self.copy("dve", maskT, self.ps[5][0:32, 0:256], [self.pb[5]], [b_mask])
                    bias_ap = self.t31[:, h:h + 1]
                else:
                    hh = h - 8
                    cbk = cbn % 2
                    cbn += 1
                    self.ts("dve", cb[cbk][:, 0:kd1 + 1], CW[:, 0:kd1 + 1, hh], -1.0, CLB[:, kd1, hh:hh + 1], ALU.mult, ALU.add,
                            [b_cw], [b_cb[cbk]])
                for kk in range(0, kd1 + 1):
                    si = scnt % 2
                    scnt += 1
                    S = self.ps[si]
                    qa, qb = (128, 256) if kk == kd1 else (0, 256)
                    extra = []
                    if moba:
                        if kk < kd0:
                            wb = kk // 2
                            extra.append((eselb[0:32, wb * 128:(wb + 1) * 128], maskT[0:32, qa:qb], [b_tb, b_mask]))
                        if kk == kp:
                            extra.append((self.identb, TB[:, h, 0:256], [self.b_c, b_tb]))
                        if kk == kd0:
                            extra.append((self.identb, OWN[:, h, 0:256], [self.b_c, b_tb]))
                        if kk == kd1:
                            extra.append((self.identb, OWN[:, h, 0:128], [self.b_c, b_tb]))
                    else:
                        if kk == kd0:
                            extra.append((self.identb, self.CA[:, 0:256], [self.b_c]))
                        if kk == kd1:
                            extra.append((self.identb, self.CA[:, 0:128], [self.b_c]))
                    self.mm(S[:, qa:qb], KT[k][:, kk * 128:(kk + 1) * 128], QT[k][:, q0 + qa:q0 + qb],
                            True, len(extra) == 0, [b_kv[k]], [self.pb[si]])
                    for ei, (lt_, rh_, rd_) in enumerate(extra):
                        self.mm(S[:, qa:qb], lt_, rh_, False, ei == len(extra) - 1, rd_, [self.pb[si]])
                    if moba:
                        self.act(PT[si][:, qa:qb], S[:, qa:qb], AF.Exp, [self.pb[si], self.b_c], [b_pt[si]], bias=bias_ap)
                    else:
                        self.act(PT[si][:, qa:qb], S[:, qa:qb], AF.Exp, [self.pb[si], b_cb[cbk]], [b_pt[si]],
                                 bias=cb[cbk][:, kk:kk + 1])
                    for v in range(2):
                        if kk == kd1 and v == 0:
                            continue
                        lastk = kd0 if v == 0 else kd1
                        self.mm(self.ps[2 + v][:, 0:130], PT[si][:, v * 128:(v + 1) * 128], VA[k][:, kk, :],
                                kk == 0, kk == lastk, [b_pt[si], b_kv[k]], [self.pb[2 + v]])
                for v in range(2):
                    oi = ocnt % 4
                    ocnt += 1
                    O = self.ps[2 + v]
                    R.op("dve", (lambda e, O=O, oi=oi: e.reciprocal(out=rec[:, oi:oi + 1], in_=O[:, 128:129])),
                         [self.pb[2 + v]], [b_rec[oi]])
                    self.act(ost[oi], O[:, 0:128], AF.Copy, [self.pb[2 + v], b_rec[oi]], [b_ost[oi]], scale=rec[:, oi:oi + 1])
                    row0 = l * 512 + q0 + v * 128
                    R.dma("act", self.ATT[row0:row0 + 128, h * 128:(h + 1) * 128], ost[oi], [b_ost[oi]], (), [self.b_att])
        R.barrier()

    def phase_c1(self, wname, res_src, vrow, HMdst, HMTdst, router=False):
        R, A = self.R, self.A
        A.reset()
        Wv = self.wview(wname)
        Wo = A.alloc([16, D], BF16)
        b_wo = Buf()
        for c in range(4):
            R.dma("sp", Wo[:, :, c * 512:(c + 1) * 512], Wv[:, c * 512:(c + 1) * 512].rearrange("(a p) n -> p a n", p=128),
                  [self.b_wg], (), [b_wo])
        gt = A.alloc([D], F32)
        bt = A.alloc([D], F32)
        b_gb = Buf()
        R.dma("sp", gt, self.vecs[vrow:vrow + 1, :].partition_broadcast(128), (), (), [b_gb])
        R.dma("sp", bt, self.vecs[vrow + 1:vrow + 2, :].partition_broadcast(128), (), (), [b_gb])
        at = [A.alloc([D], BF16) for _ in range(2)]
        b_at = [Buf(), Buf()]
        aT = [A.alloc([16, 128], BF16) for _ in range(2)]
        b_aT = [Buf(), Buf()]
        xr = [A.alloc([D], F32) for _ in range(2)]
        b_xr = [Buf(), Buf()]
        z = [A.alloc([D], F32) for _ in range(2)]
        b_z = [Buf(), Buf()]
        hb = [A.alloc([D], BF16) for _ in range(2)]
        b_hb = [Buf(), Buf()]
        hT = [A.alloc([16, 128], BF16) for _ in range(2)]
        b_hT = [Buf(), Buf()]
        st6 = A.alloc([4, 6], F32)
        mv = A.alloc([2], F32)
        rs = A.alloc([4], F32)
        self.b_hm = Buf()
        self.b_hmt = Buf()
        if router:
            wr = A.alloc([16, 8], BF16)
            R.dma("sp", wr, self.wview("router").rearrange("(a p) n -> p a n", p=128), [self.b_wg], (), [b_wo])
            rt = A.alloc([64], F32)
            b_rt = Buf()
            self.b_gt = Buf()
        for tt in range(16):
            k = tt % 2
            R.dma("sp", at[k], self.ATT[tt * 128:(tt + 1) * 128, :], [self.b_att], [b_at[k]])
            R.dma("sp", xr[k], res_src[tt * 128:(tt + 1) * 128, :], [self.b_hm] if res_src is not self.x_loc else [], [b_xr[k]])
            self.transpose16(at[k], b_at[k], aT[k], 0, b_aT[k])
            for c in range(4):
                for f in range(16):
                    self.mm(self.ps[c][:, :], aT[k][:, f, :], Wo[:, f, c * 512:(c + 1) * 512], f == 0, f == 15,
                            [b_aT[k], b_wo], [self.pb[c]])
                self.stt("dve", z[k][:, c * 512:(c + 1) * 512], xr[k][:, c * 512:(c + 1) * 512], ALPHA, self.ps[c][:, :],
                         ALU.mult, ALU.add, [b_xr[k], self.pb[c]], [b_z[k]] if c == 0 else (), () if c == 0 else [b_z[k]])
            self.ln_inplace(z[k], gt, bt, b_z[k], (st6, mv, rs), b_gb)
            R.dma("pool", HMdst[tt * 128:(tt + 1) * 128, :], z[k], [b_z[k]], (), [self.b_hm])
            self.copy("act", hb[k], z[k], [b_z[k]], [b_hb[k]])
            self.transpose16(hb[k], b_hb[k], hT[k], 0, b_hT[k])
            R.dma("pool", HMTdst[:, tt * 128:(tt + 1) * 128].rearrange("(a p) t -> p a t", p=128), hT[k], [b_hT[k]], (), [self.b_hmt])
            if router:
                for dc in range(16):
                    self.mm(self.ps[5][:, 0:8], hT[k][:, dc, :], wr[:, dc, :], dc == 0, dc == 15, [b_hT[k], b_wo], [self.pb[5]])
                lg16, m8, ex, g1, g2, ga, gb = rt[:, 40:56], rt[:, 8:16], rt[:, 16:17], rt[:, 17:18], rt[:, 18:19], rt[:, 24:32], rt[:, 32:40]
                lg = lg16[:, 0:8]
                nl1 = rt[:, 19:20]
                self.memset("dve", lg16, -1e30, [b_rt])
                self.copy("dve", lg, self.ps[5][:, 0:8], [self.pb[5]], [b_rt])
                R.op("dve", lambda e: e.max(out=m8, in_=lg16), [b_rt], [b_rt])
                self.ts("dve", nl1, m8[:, 0:1], -1.0, None, ALU.mult, None, [b_rt], [b_rt])
                self.act(ex, m8[:, 1:2], AF.Exp, [b_rt], [b_rt], bias=nl1)
                self.ts("dve", g1, ex, 1.0, None, ALU.add, None, [b_rt], [b_rt])
                R.op("dve", lambda e: e.reciprocal(out=g2, in_=g1), [b_rt], [b_rt])
                self.copy("dve", g1, g2, [b_rt], [b_rt])
                self.tt("dve", g2, ex, g1, ALU.mult, [b_rt], [b_rt])
                self.ts("dve", ga, lg, m8[:, 0:1], None, ALU.is_equal, None, [b_rt], [b_rt])
                self.ts("dve", ga, ga, g1, None, ALU.mult, None, [b_rt], [b_rt])
                self.ts("dve", gb, lg, m8[:, 1:2], None, ALU.is_equal, None, [b_rt], [b_rt])
                self.ts("dve", gb, gb, g2, None, ALU.mult, None, [b_rt], [b_rt])
                self.tt("dve", ga, ga, gb, ALU.add, [b_rt], [b_rt])
                R.dma("pool", self.GT[tt * 128:(tt + 1) * 128, :], ga, [b_rt], (), [self.b_gt])
        R.barrier()

    def ln_inplace(self, z, gt, bt, bz, tmpst, b_gb):
        st6, mv, rs = tmpst
        bs = Buf()
        R = self.R
        for c in range(4):
            R.op("dve", (lambda e, c=c: e.bn_stats(out=st6[:, c, :], in_=z[:, c * 512:(c + 1) * 512])), [bz],
                 [bs] if c == 0 else (), () if c == 0 else [bs])
        R.op("dve", lambda e: e.bn_aggr(out=mv, in_=st6), [bs], [bs])
        self.ts("dve", rs[:, 0:1], mv[:, 1:2], 1e-5, None, ALU.add, None, [bs], [bs])
        self.act(rs[:, 1:2], rs[:, 0:1], AF.Sqrt, [bs], [bs])
        R.op("dve", lambda e: e.reciprocal(out=rs[:, 2:3], in_=rs[:, 1:2]), [bs], [bs])
        self.ts("dve", z, z, mv[:, 0:1], rs[:, 2:3], ALU.subtract, ALU.mult, [bs], [bz])
        self.tt("pool", z, z, gt, ALU.mult, [b_gb], [bz])
        self.tt("dve", z, z, bt, ALU.add, [b_gb], [bz])

    def phase_c2(self):
        R, A = self.R, self.A
        A.reset()
        F0 = self.F0
        NF = F0 // 128
        W1 = self.wview("w1")
        W3 = self.wview("w3")
        W2 = self.wview("w2")
        gt = A.alloc([D], F32)
        bt = A.alloc([D], F32)
        b_gb = Buf()
        R.dma("sp", gt, self.vecs[2:3, :].partition_broadcast(128), (), (), [b_gb])
        R.dma("sp", bt, self.vecs[3:4, :].partition_broadcast(128), (), (), [b_gb])
        hT = A.alloc([16, 512], BF16)
        b_hT = Buf()
        hm4 = A.alloc([4, D], F32)
        b_hm4 = [Buf() for _ in range(4)]
        gT = A.alloc([NF, 512], BF16)
        b_gT = Buf()
        w13 = [A.alloc([2, 16, 128], BF16) for _ in range(2)]
        b_w13 = [Buf(), Buf()]
        w2s = A.alloc([NF, 256], BF16)
        b_w2 = Buf()
        sA = [A.alloc([512], BF16) for _ in range(2)]
        b_sA = [Buf(), Buf()]
        hb = A.alloc([D], BF16)
        b_hb = Buf()
        oT = A.alloc([16, 128], BF16)
        b_oT = Buf()
        st6 = A.alloc([4, 6], F32)
        mv = A.alloc([2], F32)
        rs = A.alloc([4], F32)
        self.b_h1 = Buf()
        self.b_h1t = Buf()
        wc = 0
        pc = 0
        for l in range(4):
            R.dma("sp", hT, self.HMT[:, l * 512:(l + 1) * 512].rearrange("(a p) t -> p a t", p=128), [self.b_hmt], [b_hT])
            for ts_ in range(4):
                R.dma("sp", hm4[:, ts_, :], self.HM[l * 512 + ts_ * 128: l * 512 + (ts_ + 1) * 128, :], [self.b_hm], [b_hm4[ts_]])
            for f0 in range(0, F0, 128):
                k = wc % 2
                wc += 1
                R.dma("sp", w13[k][:, 0, :, :], W1[:, f0:f0 + 128].rearrange("(a p) n -> p a n", p=128), [self.b_wg], [b_w13[k]])
                R.dma("sp", w13[k][:, 1, :, :], W3[:, f0:f0 + 128].rearrange("(a p) n -> p a n", p=128), [self.b_wg], (), [b_w13[k]])
                for fc in range(1):
                    fidx = f0 // 128 + fc
                    pa = (pc % 2) * 2
                    sk = pc % 2
                    pc += 1
                    for dc in range(16):
                        self.mm(self.ps[pa][:, :], w13[k][:, 0, dc, fc * 128:(fc + 1) * 128], hT[:, dc, :], dc == 0, dc == 15,
                                [b_w13[k], b_hT], [self.pb[pa]])
                    for dc in range(16):
                        self.mm(self.ps[pa + 1][:, :], w13[k][:, 1, dc, fc * 128:(fc + 1) * 128], hT[:, dc, :], dc == 0, dc == 15,
                                [b_w13[k], b_hT], [self.pb[pa + 1]])
                    self.act(sA[sk], self.ps[pa][:, :], AF.Silu, [self.pb[pa]], [b_sA[sk]])
                    self.tt("dve", gT[:, fidx, :], sA[sk], self.ps[pa + 1][:, :], ALU.mult, [b_sA[sk], self.pb[pa + 1]],
                            [b_gT] if fidx == 0 else (), () if fidx == 0 else [b_gT])
            for c0 in range(0, D, 256):
                R.dma("sp", w2s, W2[:, c0:c0 + 256].rearrange("(a p) n -> p a n", p=128), [self.b_wg], [b_w2])
                for ts_ in range(4):
                    pi = 4 + (pc % 2)
                    pc += 1
                    for f in range(NF):
                        self.mm(self.ps[pi][:, 0:256], gT[:, f, ts_ * 128:(ts_ + 1) * 128], w2s[:, f, :], f == 0, f == NF - 1,
                                [b_gT, b_w2], [self.pb[pi]])
                    self.stt("dve", hm4[:, ts_, c0:c0 + 256], hm4[:, ts_, c0:c0 + 256], ALPHA, self.ps[pi][:, 0:256],
                             ALU.mult, ALU.add, [self.pb[pi]], (), [b_hm4[ts_]])
            for ts_ in range(4):
                self.ln_inplace(hm4[:, ts_, :], gt, bt, b_hm4[ts_], (st6, mv, rs), b_gb)
                row0 = l * 512 + ts_ * 128
                R.dma("pool", self.H1[row0:row0 + 128, :], hm4[:, ts_, :], [b_hm4[ts_]], (), [self.b_h1])
                self.copy("act", hb, hm4[:, ts_, :], [b_hm4[ts_]], [b_hb])
                self.transpose16(hb, b_hb, oT, 0, b_oT)
                R.dma("pool", self.H1T[:, row0:row0 + 128].rearrange("(a p) t -> p a t", p=128), oT, [b_oT], (), [self.b_h1t])
        R.barrier()

    def layer1(self):
        self.l1_proj()
        if self.stop_after == "p":
            return self.copy_out(self.H1)
        self.l1_index()
        if self.stop_after == "i":
            return self.copy_out(self.H1)
        self.l1_attn()
        if self.stop_after == "t":
            return self.copy_out(self.H1)
        self.phase_c1("w_out1", self.H1, 4, self.HM, self.HMT, router=True)
        if self.stop_after == "d":
            return self.copy_out(self.HM)
        self.l1_moe()
        return self.l1_final()

    def l1_proj(self):
        R, A = self.R, self.A
        A.reset()
        nc = self.nc
        Win = self.wview("w_in1")
        Wqi = self.wview("w_qidx")
        Wuq = self.wview("w_uq")
        Wuk = self.wview("w_uk")
        Wuv = self.wview("w_uv")
        win = A.alloc([16, 1104], BF16)
        wqi = A.alloc([4, 1024], BF16)
        wuq = A.alloc([4, D], BF16)
        wuk = A.alloc([4, D], BF16)
        wuv = A.alloc([4, D], BF16)
        b_w = Buf()
        R.dma("sp", win, Win.rearrange("(a p) n -> p a n", p=128), [self.b_wg], [b_w])
        R.dma("sp", wqi, Wqi.rearrange("(a p) n -> p a n", p=128), [self.b_wg], (), [b_w])
        R.dma("sp", wuq, Wuq.rearrange("(a p) n -> p a n", p=128), [self.b_wg], (), [b_w])
        R.dma("sp", wuk, Wuk.rearrange("(a p) n -> p a n", p=128), [self.b_wg], (), [b_w])
        R.dma("sp", wuv, Wuv.rearrange("(a p) n -> p a n", p=128), [self.b_wg], (), [b_w])
        gq = A.alloc([512], F32)
        gkv = A.alloc([512], F32)
        R.dma("sp", gq, self.vecs[8:9, 0:512].partition_broadcast(128), (), (), [b_w])
        R.dma("sp", gkv, self.vecs[9:10, 0:512].partition_broadcast(128), (), (), [b_w])
        hT = A.alloc([16, 512], BF16)
        b_hT = Buf()
        cqT = A.alloc([4, 512], BF16)
        ckT = A.alloc([4, 512], BF16)
        kiT = A.alloc([512], BF16, parts=64)
        b_cT = Buf()
        junk = A.alloc([512], BF16)
        ssq = A.alloc([4], F32)
        b_ss = Buf()
        nrm = [A.alloc([512], BF16) for _ in range(2)]
        b_nrm = [Buf(), Buf()]
        kib = A.alloc([64], BF16)
        wsb = A.alloc([32], F32)
        b_ws = Buf()
        stg = [A.alloc([512], BF16) for _ in range(4)]
        b_stg = [Buf() for _ in range(4)]
        self.b_qi = Buf()
        self.b_ws = Buf()
        self.b_exl = Buf()
        self.b_qt0 = Buf()
        EXLk = self.EXL[:, OFF_K:OFF_K + 2048 * 512].rearrange("l (f c) -> l f c", c=512)
        EXLv = self.EXL[:, OFF_V:OFF_V + 2048 * 512].rearrange("l (h p c) -> l h p c", h=16, p=128)
        scnt = 0
        pcnt = 0
        for l in range(4):
            R.dma("sp", hT, self.H1T[:, l * 512:(l + 1) * 512].rearrange("(a p) t -> p a t", p=128), [self.b_h1t], [b_hT])
            for ts_ in range(4):
                tok = slice(ts_ * 128, (ts_ + 1) * 128)
                for gi, (c0, w) in enumerate([(0, 512), (512, 512), (1024, 80)]):
                    for dc in range(16):
                        self.mm(self.ps[gi][:, 0:w], hT[:, dc, tok], win[:, dc, c0:c0 + w], dc == 0, dc == 15, [b_hT, b_w], [self.pb[gi]])
                for gi, (gvec, dstT) in enumerate([(gq, cqT), (gkv, ckT)]):
                    self.memset("dve", ssq[:, 0:1], 0.0, [b_ss])
                    R.op("act", (lambda e, gi=gi: e.activation(out=junk, in_=self.ps[gi][:, :], func=AF.Square, accum_out=ssq[:, 0:1])),
                         [self.pb[gi], b_ss], [b_ss])
                    self.ts("dve", ssq[:, 1:2], ssq[:, 0:1], 1.0 / 512.0, 1e-6, ALU.mult, ALU.add, [b_ss], [b_ss])
                    self.act(ssq[:, 2:3], ssq[:, 1:2], AF.Sqrt, [b_ss], [b_ss])
                    R.op("dve", lambda e: e.reciprocal(out=ssq[:, 3:4], in_=ssq[:, 2:3]), [b_ss], [b_ss])
                    k = gi
                    self.stt("dve", nrm[k], self.ps[gi][:, :], ssq[:, 3:4], gvec, ALU.mult, ALU.mult, [self.pb[gi], b_ss, b_w], [b_nrm[k]])
                    pv = self.ps[6 + gi][:, :].bitcast(BF16)
                    for rc in range(4):
                        self.tr(pv[:, rc * 128:(rc + 1) * 128], nrm[k][:, rc * 128:(rc + 1) * 128], self.identb, [b_nrm[k], self.b_c], [self.pb[6 + gi]])
                    self.copy("act", dstT[:, :, tok], pv[:, 0:512].rearrange("p (a b) -> p a b", b=128), [self.pb[6 + gi]], (), [b_cT])
                self.copy("dve", kib, self.ps[2][:, 0:64], [self.pb[2]], [b_ws])
                self.ts("dve", wsb[:, 0:16], self.ps[2][:, 64:80], 1.0 / 32.0, None, ALU.mult, None, [self.pb[2]], [b_ws])
                R.op("act", lambda e: e.sign(out=wsb[:, 16:32], in_=wsb[:, 0:16]), [b_ws], [b_ws])
                self.tt("dve", wsb[:, 0:16], wsb[:, 0:16], wsb[:, 16:32], ALU.mult, [b_ws], [b_ws])
                row0 = l * 512 + ts_ * 128
                R.dma("pool", self.WS[row0:row0 + 128, :], wsb, [b_ws], (), [self.b_ws])
                pv = self.ps[5][:, :].bitcast(BF16)
                self.tr(pv[0:64, 0:128], kib, self.identb, [b_ws, self.b_c], [self.pb[5]])
                self.copy("dve", kiT[0:64, tok], pv[0:64, 0:128], [self.pb[5]], (), [b_cT])
            R.dma("pool", self.EXL[l, OFF_X:OFF_X + 32768].rearrange("(p c) -> p c", p=64), kiT, [b_cT], (), [self.b_exl])
            jobs = [("qi", mc) for mc in range(8)] + [("q", h) for h in range(16)] + [("k", h) for h in range(16)]
            for kind, idx in jobs:
                pi = pcnt % 4
                pcnt += 1
                if kind == "qi":
                    W_, src, cols = wqi, cqT, slice(idx * 128, (idx + 1) * 128)
                elif kind == "q":
                    W_, src, cols = wuq, cqT, slice(idx * 128, (idx + 1) * 128)
                else:
                    W_, src, cols = wuk, ckT, slice(idx * 128, (idx + 1) * 128)
                for rc in range(4):
                    self.mm(self.ps[pi][:, :], W_[:, rc, cols], src[:, rc, :], rc == 0, rc == 3, [b_w, b_cT], [self.pb[pi]])
                si = scnt % 4
                scnt += 1
                if kind == "q":
                    self.act(stg[si], self.ps[pi][:, :], AF.Copy, [self.pb[pi]], [b_stg[si]], scale=SCALE)
                    R.dma("pool", self.QT0[idx * 128:(idx + 1) * 128, l * 512:(l + 1) * 512], stg[si], [b_stg[si]], (), [self.b_qt0])
                elif kind == "qi":
                    self.copy("dve", stg[si], self.ps[pi][:, :], [self.pb[pi]], [b_stg[si]])
                    R.dma("pool", self.QI[idx * 128:(idx + 1) * 128, l * 512:(l + 1) * 512], stg[si], [b_stg[si]], (), [self.b_qi])
                else:
                    self.copy("dve", stg[si], self.ps[pi][:, :], [self.pb[pi]], [b_stg[si]])
                    R.dma("pool", EXLk[l, idx * 128:(idx + 1) * 128, :], stg[si], [b_stg[si]], (), [self.b_exl])
            for ts_ in range(4):
                for hb in range(0, 16, 4):
                    pi = pcnt % 4
                    pcnt += 1
                    for rc in range(4):
                        self.mm(self.ps[pi][:, :], ckT[:, rc, ts_ * 128:(ts_ + 1) * 128], wuv[:, rc, hb * 128:(hb + 4) * 128],
                                rc == 0, rc == 3, [b_w, b_cT], [self.pb[pi]])
                    si = scnt % 4
                    scnt += 1
                    self.copy("act", stg[si], self.ps[pi][:, :], [self.pb[pi]], [b_stg[si]])
                    R.dma("pool", EXLv[l, hb:hb + 4, :, ts_ * 128:(ts_ + 1) * 128].rearrange("h p d -> p h d"),
                          stg[si].rearrange("p (h d) -> p h d", d=128), [b_stg[si]], (), [self.b_exl])
        self.put_slots([self.b_exl])
        self.exchange()
        R.barrier()

    def l1_index(self):
        R, A = self.R, self.A
        A.reset()
        cst = A.alloc([CSTW], F32)
        b_cst = Buf()
        R.dma("sp", cst, self.cst, (), [b_cst])
        CM = A.alloc([4, 512], F32)
        for m in range(4):
            self.memset("pool", CM[:, m, :], 0.0, [b_cst])
            R.op("pool", (lambda e, m=m: e.affine_select(out=CM[:, m, :], in_=CM[:, m, :], pattern=[[-1, 512]],
                                                         compare_op=ALU.is_ge, fill=-1e30, base=m * 128, channel_multiplier=1)),
                 [b_cst], [b_cst])
        KI = A.alloc([T], BF16)
        b_ki = Buf()
        QIt = [A.alloc([8, 128], BF16) for _ in range(2)]
        wst = [A.alloc([32], F32) for _ in range(2)]
        b_q = [Buf(), Buf()]
        sc = A.alloc([T], F32)
        sc2 = A.alloc([512], F32)
        b_sc = Buf()
        b_sc2 = Buf()
        rl = [A.alloc([512], F32) for _ in range(4)]
        b_rl = [Buf() for _ in range(4)]
        junk = A.alloc([T], BF16)
        b_junk = Buf()
        bis = A.alloc([8], F32)
        b_bis = Buf()
        mk = A.alloc([T], F32)
        b_mk = Buf()
        mstg = [A.alloc([8, 128], BF16) for _ in range(2)]
        b_mstg = [Buf(), Buf()]
        self.b_mk = Buf()
        rcnt = 0
        tcnt = 0
        qn = 0
        for l in range(4):
            n = 4 * (l + 1)
            L = n * 512
            src = self.WIN[l][:, OFF_X:OFF_X + 32768].rearrange("a (p c) -> p a c", p=64)
            R.dma("sp", KI[0:64, 0:L].rearrange("p (a c) -> p a c", c=512), src, [self.b_win[l]], [b_ki])
            R.dma("sp", KI[64:128, 0:L].rearrange("p (a c) -> p a c", c=512), src, [self.b_win[l]], (), [b_ki])
            for ts_ in range(4):
                k = qn % 2
                qn += 1
                row0 = l * 512 + ts_ * 128
                R.dma("sp", QIt[k], self.QI[:, row0:row0 + 128].rearrange("(a p) t -> p a t", p=128), [self.b_qi], [b_q[k]])
                R.dma("sp", wst[k], self.WS[row0:row0 + 128, :], [self.b_ws], (), [b_q[k]])
                for a in range(n):
                    ksl = slice(a * 512, (a + 1) * 512)
                    virt = cst[:, 4928 + l * 16 + a:4928 + l * 16 + a + 1]
                    for hh in range(16):
                        mc, half = divmod(hh, 2)
                        pi = hh % 2
                        prt = slice(half * 64, (half + 1) * 64)
                        self.mm(self.ps[pi][:, :], QIt[k][prt, mc, :], KI[prt, ksl], True, True, [b_q[k], b_ki], [self.pb[pi]])
                        ri = rcnt % 4
                        rcnt += 1
                        self.act(rl[ri], self.ps[pi][:, :], AF.Relu, [self.pb[pi], b_q[k]], [b_rl[ri]], scale=wst[k][:, hh:hh + 1])
                        sg = wst[k][:, 16 + hh:17 + hh]
                        if hh == 0:
                            self.ts("dve", sc[:, ksl], rl[ri], sg, virt, ALU.mult, ALU.add, [b_rl[ri], b_q[k], b_cst], [b_sc])
                        else:
                            self.stt("dve", sc[:, ksl], rl[ri], sg, sc[:, ksl], ALU.mult, ALU.add, [b_rl[ri], b_q[k]], [b_sc])
                    if a == n - 1:
                        self.tt("dve", sc[:, ksl], sc[:, ksl], CM[:, ts_, :], ALU.add, [b_cst], [b_sc])
                self.memset("dve", bis[:, 0:1], -32.0, [b_bis])
                for it in range(24):
                    hstep = 32.0 / (2 ** it)
                    self.ts("dve", bis[:, 1:2], bis[:, 0:1], hstep, None, ALU.add, None, [b_bis], [b_bis])
                    self.memset("dve", bis[:, 2:3], 0.0, [b_bis])
                    self.ts("dve", junk[:, 0:L], sc[:, 0:L], bis[:, 1:2], 0.0, ALU.is_ge, ALU.add, [b_sc, b_bis], [b_junk],
                            accum_out=bis[:, 2:3])
                    self.ts("dve", bis[:, 3:4], bis[:, 2:3], 256.0, hstep, ALU.is_ge, ALU.mult, [b_junk, b_bis], [b_bis])
                    self.tt("dve", bis[:, 0:1], bis[:, 0:1], bis[:, 3:4], ALU.add, [b_bis], [b_bis])
                self.ts("dve", mk[:, 0:L], sc[:, 0:L], bis[:, 0:1], None, ALU.is_ge, None, [b_sc, b_bis], [b_mk])
                self.ts("pool", mk[:, 0:L], mk[:, 0:L], -NEG, NEG, ALU.mult, ALU.add, [b_mk], [b_mk])
                for g0 in range(0, 4 * n, 4):
                    pi = 4 + (tcnt % 2)
                    mi = tcnt % 2
                    tcnt += 1
                    for q in range(4):
                        kk = g0 + q
                        self.tr(self.ps[pi][:, q * 128:(q + 1) * 128], mk[:, kk * 128:(kk + 1) * 128], self.identf,
                                [b_mk, self.b_c], [self.pb[pi]])
                    self.copy("act", mstg[mi][:, 0:4, :], self.ps[pi][:, :].rearrange("p (a b) -> p a b", b=128), [self.pb[pi]], [b_mstg[mi]])
                    R.dma("pool", self.MK[l][g0:g0 + 4, :, ts_ * 128:(ts_ + 1) * 128].rearrange("k s q -> s k q"), mstg[mi][:, 0:4, :],
                          [b_mstg[mi]], (), [self.b_mk])
        R.barrier()

    def l1_attn(self):
        R, A = self.R, self.A
        A.reset()
        tbr = A.alloc([384], F32)
        TB = A.alloc([16, 384], BF16)
        b_tb = Buf()
        for h in range(16):
            R.dma("sp", tbr, self.tbraw[:, h * 384:(h + 1) * 384], (), [b_tb])
            self.ts("dve", TB[:, h, :], tbr, self.t31[:, h:h + 1], None, ALU.subtract, None, [b_tb, self.b_c], [b_tb])
        KT = [A.alloc([T], BF16) for _ in range(2)]
        VA = [A.alloc([64, 130], BF16) for _ in range(2)]
        QT = [A.alloc([512], BF16) for _ in range(2)]
        b_kv = [Buf(), Buf()]
        for k in range(2):
            self.memset("pool", VA[k][:, :, 128:130], 1.0, [b_kv[k]])
        MKs = A.alloc([64, 512], BF16)
        b_mks = Buf()
        PT = [A.alloc([512], BF16) for _ in range(2)]
        b_pt = [Buf(), Buf()]
        ost = [A.alloc([128], BF16) for _ in range(4)]
        b_ost = [Buf() for _ in range(4)]
        rec = A.alloc([4], F32)
        b_rec = [Buf() for _ in range(4)]
        self.b_att = Buf()
        iters = [(l, h) for l in range(4) for h in range(16)]

        def loads(it):
            l, h = iters[it]
            k = it % 2
            n = 4 * (l + 1)
            Wk = self.WIN[l][:, OFF_K:OFF_K + 2048 * 512].rearrange("a (f c) -> a f c", c=512)
            Wv = self.WIN[l][:, OFF_V:OFF_V + 2048 * 512].rearrange("a (h p c) -> a h p c", h=16, p=128)
            R.dma("sp", KT[k][:, 0:n * 512].rearrange("p (a c) -> p a c", c=512),
                  Wk[:, h * 128:(h + 1) * 128, :].rearrange("a p c -> p a c"), [self.b_win[l]], [b_kv[k]])
            for a in range(n):
                R.dma("sp", VA[k][:, 4 * a:4 * a + 4, 0:128], Wv[a, h, :, :].rearrange("p (s d) -> p s d", d=128),
                      [self.b_win[l]], (), [b_kv[k]])
            R.dma("sp", QT[k], self.QT0[h * 128:(h + 1) * 128, l * 512:(l + 1) * 512], [self.b_qt0], (), [b_kv[k]])
        scnt = 0
        ocnt = 0
        loads(0)
        for it, (l, h) in enumerate(iters):
            k = it % 2
            n = 4 * (l + 1)
            if h == 0:
                for g0 in range(0, 4 * n, 16):
                    R.dma("sp", MKs[:, g0:g0 + 16, :], self.MK[l][g0:g0 + 16, :, :].rearrange("k s q -> s k q"), [self.b_mk],
                          [b_mks] if g0 == 0 else (), () if g0 == 0 else [b_mks])
            if it + 1 < len(iters):
                loads(it + 1)
            kown = 4 * (n - 1)
            for kk in range(0, 4 * n):
                si = scnt % 2
                scnt += 1
                S = self.ps[si]
                m = kk - kown
                qa = 128 * m if m > 0 else 0
                extra = [(self.identb, MKs[:, kk, qa:512], qa, 512, [self.b_c, b_mks])]
                if kk == kown - 1:
                    extra.append((self.identb, TB[:, h, 0:128], 0, 128, [self.b_c, b_tb]))
                if m >= 0:
                    w = min(256, 512 - 128 * m)
                    extra.append((self.identb, TB[:, h, 128:128 + w], 128 * m, 128 * m + w, [self.b_c, b_tb]))
                self.mm(S[:, qa:512], KT[k][:, kk * 128:(kk + 1) * 128], QT[k][:, qa:512], True, False, [b_kv[k]], [self.pb[si]])
                for ei, (lt_, rh_, c0, c1, rd_) in enumerate(extra):
                    self.mm(S[:, c0:c1], lt_, rh_, False, ei == len(extra) - 1, rd_, [self.pb[si]])
                self.act(PT[si][:, qa:512], S[:, qa:512], AF.Exp, [self.pb[si], self.b_c], [b_pt[si]], bias=self.t31[:, h:h + 1])
                for v in range(4):
                    if m > v:
                        continue
                    self.mm(self.ps[2 + v][:, 0:130], PT[si][:, v * 128:(v + 1) * 128], VA[k][:, kk, :],
                            kk == 0, kk == kown + v, [b_pt[si], b_kv[k]], [self.pb[2 + v]])
            for v in range(4):
                oi = ocnt % 4
                ocnt += 1
                O = self.ps[2 + v]
                R.op("dve", (lambda e, O=O, oi=oi: e.reciprocal(out=rec[:, oi:oi + 1], in_=O[:, 128:129])),
                     [self.pb[2 + v]], [b_rec[oi]])
                self.act(ost[oi], O[:, 0:128], AF.Copy, [self.pb[2 + v], b_rec[oi]], [b_ost[oi]], scale=rec[:, oi:oi + 1])
                row0 = l * 512 + v * 128
                R.dma("pool", self.ATT[row0:row0 + 128, h * 128:(h + 1) * 128], ost[oi], [b_ost[oi]], (), [self.b_att])
        R.barrier()


    def l1_moe(self):
        R, A = self.R, self.A
        A.reset()
        FE = self.FE
        NF = FE // 128
        hz_zb = []
        self.zero_fill(self.HZ, 8 * D, NLOC, hz_zb)
        gz_zb = Buf()
        R.dma("sp", self.GZ.rearrange("(p a) c -> p (a c)", p=128), self.zt[:, 0:2048].bitcast(F32), [self.b_zt], [gz_zb])
        b_hz = Buf()
        b_gz = Buf()

        def put_h(e):
            r = self.dynreg(e, "R8")
            return e.dma_start(out=self.HZ.rearrange("(s r) c -> s r c", s=8)[bass.ds(r, 1), :, :].rearrange("s (a p) c -> p (s a) c", p=128),
                               in_=self.HMT.rearrange("(a p) c -> p a c", p=128))
        self.dyn("pool", put_h, [self.b_hmt] + hz_zb, (), [b_hz])

        def put_g(e):
            r = self.dynreg(e, "R8")
            return e.dma_start(out=self.GZ.rearrange("(s r) c -> s r c", s=8)[bass.ds(r, 1), :, :].rearrange("s (p a) c -> p (s a c)", p=128),
                               in_=self.GT.rearrange("(p a) c -> p (a c)", p=128))
        self.dyn("sp", put_g, [self.b_gt, gz_zb], (), [b_gz])
        b_hg = Buf()
        self.allreduce8(self.HZ, self.HT8, self.HG, [b_hz] + hz_zb, b_hg)
        b_gg = Buf()
        self.allreduce8(self.GZ, self.GT8, self.GG, [b_gz, gz_zb], b_gg, esize=4)
        b_ew = Buf()
        for src, dst, rows, cols in ((self.ew1, self.EW1, D, FE), (self.ew3, self.EW3, D, FE), (self.ew2, self.EW2, FE, D)):
            step = max(128, (1 << 20) // cols // 128 * 128)
            for r0 in range(0, rows, step):
                nrow = min(step, rows - r0)
                R.dma("pool", dst[r0:r0 + nrow, :], src[r0:r0 + nrow, :], (), (), [b_ew])
        cst = A.alloc([16], F32)
        b_cst = Buf()
        R.dma("sp", cst, self.cst[:, 4992:5008], (), [b_cst])
        xT = A.alloc([16, 512], BF16)
        b_xT = Buf()
        gts = A.alloc([4, 8], F32)
        gcol = A.alloc([4], F32)
        b_g = Buf()
        gT = A.alloc([NF, 512], BF16)
        b_gT = Buf()
        w13 = [A.alloc([2, 16, 128], BF16) for _ in range(2)]
        b_w13 = [Buf(), Buf()]
        w2s = [A.alloc([NF, 256], BF16) for _ in range(2)]
        b_w2 = [Buf(), Buf()]
        sA = [A.alloc([512], BF16) for _ in range(2)]
        b_sA = [Buf(), Buf()]
        ot = [A.alloc([256], F32) for _ in range(4)]
        b_ot = [Buf() for _ in range(4)]
        self.b_mo = Buf()
        wc = 0
        pc = 0
        w2c = 0
        oc = 0
        for s_ in range(8):
            for tq in range(4):
                r0 = s_ * NLOC + tq * 512
                R.dma("sp", xT, self.HG[s_ * D:(s_ + 1) * D, tq * 512:(tq + 1) * 512].rearrange("(a p) t -> p a t", p=128), [b_hg], [b_xT])
                R.dma("sp", gts, self.GG[r0:r0 + 512, :].rearrange("(a p) e -> p a e", p=128), [b_gg], [b_g])
                for a in range(4):
                    self.tt("dve", gts[:, a, :], gts[:, a, :], cst[:, 0:8], ALU.mult, [b_cst], [b_g])
                R.op("dve", lambda e: e.reduce_sum(out=gcol, in_=gts, axis=mybir.AxisListType.X), [b_g], [b_g])
                for f0 in range(0, FE, 128):
                    k = wc % 2
                    wc += 1
                    R.dma("sp", w13[k][:, 0, :, :], self.EW1[:, f0:f0 + 128].rearrange("(a p) n -> p a n", p=128), [b_ew], [b_w13[k]])
                    R.dma("sp", w13[k][:, 1, :, :], self.EW3[:, f0:f0 + 128].rearrange("(a p) n -> p a n", p=128), [b_ew], (), [b_w13[k]])
                    fidx = f0 // 128
                    pa = (pc % 2) * 2
                    sk = pc % 2
                    pc += 1
                    for dc in range(16):
                        self.mm(self.ps[pa][:, :], w13[k][:, 0, dc, :], xT[:, dc, :], dc == 0, dc == 15, [b_w13[k], b_xT], [self.pb[pa]])
                    for dc in range(16):
                        self.mm(self.ps[pa + 1][:, :], w13[k][:, 1, dc, :], xT[:, dc, :], dc == 0, dc == 15, [b_w13[k], b_xT], [self.pb[pa + 1]])
                    self.act(sA[sk], self.ps[pa][:, :], AF.Silu, [self.pb[pa]], [b_sA[sk]])
                    self.tt("dve", gT[:, fidx, :], sA[sk], self.ps[pa + 1][:, :], ALU.mult, [b_sA[sk], self.pb[pa + 1]],
                            [b_gT] if fidx == 0 else (), () if fidx == 0 else [b_gT])
                for c0 in range(0, D, 256):
                    k2 = w2c % 2
                    w2c += 1
                    R.dma("sp", w2s[k2], self.EW2[:, c0:c0 + 256].rearrange("(a p) n -> p a n", p=128), [b_ew], [b_w2[k2]])
                    for ts_ in range(4):
                        pi = 4 + (pc % 2)
                        pc += 1
                        for f in range(NF):
                            self.mm(self.ps[pi][:, 0:256], gT[:, f, ts_ * 128:(ts_ + 1) * 128], w2s[k2][:, f, :], f == 0, f == NF - 1,
                                    [b_gT, b_w2[k2]], [self.pb[pi]])
                        oi = oc % 4
                        oc += 1
                        self.ts("dve" if oi % 2 == 0 else "pool" if False else "dve", ot[oi], self.ps[pi][:, 0:256], gcol[:, ts_:ts_ + 1], None,
                                ALU.mult, None, [self.pb[pi], b_g], [b_ot[oi]])
                        R.dma("pool", self.MO[r0 + ts_ * 128:r0 + (ts_ + 1) * 128, c0:c0 + 256], ot[oi], [b_ot[oi]], (), [self.b_mo])
        R.barrier()
        self.b_ms = Buf()
        self.allreduce8(self.MO, self.MT8, self.MS, [self.b_mo], self.b_ms, esize=4)
        self.b_ff = Buf()

        def get_ff(e):
            r = self.dynreg(e, "R8")
            return e.dma_start(out=self.FF.rearrange("(p a) c -> p (a c)", p=128),
                               in_=self.MS.rearrange("(s r) c -> s r c", s=8)[bass.ds(r, 1), :, :].rearrange("s (p a) c -> p (s a c)", p=128))
        self.dyn("sp", get_ff, [self.b_ms], [self.b_ff])
        R.barrier()

    def l1_final(self):
        R, A = self.R, self.A
        A.reset()
        gt = A.alloc([D], F32)
        bt = A.alloc([D], F32)
        b_gb = Buf()
        R.dma("sp", gt, self.vecs[6:7, :].partition_broadcast(128), (), (), [b_gb])
        R.dma("sp", bt, self.vecs[7:8, :].partition_broadcast(128), (), (), [b_gb])
        hm = [A.alloc([D], F32) for _ in range(2)]
        ff = [A.alloc([D], F32) for _ in range(2)]
        b_h = [Buf(), Buf()]
        b_f = [Buf(), Buf()]
        st6 = A.alloc([4, 6], F32)
        mv = A.alloc([2], F32)
        rs = A.alloc([4], F32)
        last = []
        for tt in range(16):
            k = tt % 2
            R.dma("sp", hm[k], self.HM[tt * 128:(tt + 1) * 128, :], [self.b_hm], [b_h[k]])
            R.dma("sp", ff[k], self.FF[tt * 128:(tt + 1) * 128, :], [self.b_ff], [b_f[k]])
            self.stt("dve", ff[k], hm[k], ALPHA, ff[k], ALU.mult, ALU.add, [b_h[k]], [b_f[k]])
            self.ln_inplace(ff[k], gt, bt, b_f[k], (st6, mv, rs), b_gb)
            last.append(R.dma("pool", self.y[tt * 128:(tt + 1) * 128, :], ff[k], [b_f[k]], ()))
        return last


def _rel_bucket_np(d):
    d = np.maximum(d, 0)
    exact = 16
    nf = np.maximum(d, 1).astype(np.float32)
    large = exact + (np.log(nf / exact) / math.log(128 / exact) * (32 - exact)).astype(np.int32)
    large = np.minimum(large, 31)
    return np.where(d < exact, d, large)


def _consts_for(j, core):
    c = np.zeros((128, CSTW), np.float32)
    p = np.arange(128)
    c[:, 0:128] = (p[:, None] <= p[None, :])
    g = np.arange(64)
    c[0:64, 128:192] = (g[:, None] < g[None, :])
    c[127, 192:320] = 1.0
    c[0:64, 320:448] = 1.0
    for l in range(4):
        n = 4 * (l + 1)
        i = TS[j][l]
        ws = i + 1 - n
        nvirt = 2 * max(0, -ws)
        for u in range(2):
            wb_own = 2 * (n - 1) + u
            row = np.zeros(32, np.float32)
            row[wb_own:] = -1e30
            row[:nvirt] = -1e30
            c[:, 448 + (2 * l + u) * 32:448 + (2 * l + u + 1) * 32] = row[None, :]
    for wb in range(32):
        c[wb, 832 + wb * 128:832 + (wb + 1) * 128] = 1.0
    for l in range(4):
        n = 4 * (l + 1)
        ws = TS[j][l] + 1 - n
        for a in range(max(0, -ws)):
            c[:, 4928 + l * 16 + a] = -1e30
    c[:, 4992 + core] = 1.0
    return c


_CACHE = {}


def kernel(**inp):
    stop_after = inp.pop("_stop_after", None)
    x = np.asarray(inp["x"], np.float32)
    F0 = inp["ev_ffn_w1"].shape[-1]
    FE = inp["od_exp_w1"].shape[-1]
    key = (F0, FE, stop_after)
    if key not in _CACHE:
        pr = Prog(F0, FE, stop_after)
        pr.build()
        _CACHE[key] = pr
    pr = _CACHE[key]
    mats = {
        "w_in0": inp["ev_w_in"][0], "w_out0": inp["ev_w_out"][0], "w1": inp["ev_ffn_w1"][0], "w3": inp["ev_ffn_w3"][0],
        "w2": inp["ev_ffn_w2"][0], "w_in1": inp["od_w_in"][0], "w_uq": inp["od_w_uq"][0], "w_qidx": inp["od_w_qidx"][0],
        "w_uk": np.transpose(inp["od_w_uk"][0], (1, 0, 2)).reshape(512, D),
        "w_uv": np.transpose(inp["od_w_uv"][0], (1, 0, 2)).reshape(512, D),
        "w_out1": inp["od_w_out"][0], "router": inp["od_router"][0],
    }
    flat = np.zeros(pr.NR * 1024, np.float32)
    for name, K, N in pr.shapes:
        off = pr.lay[name][0]
        flat[off:off + K * N] = np.asarray(mats[name], np.float32).reshape(-1)
    flat = flat.reshape(pr.NR, 1024)
    vecs = np.zeros((16, D), np.float32)
    for i, nm in enumerate(["ev_ln1_g", "ev_ln1_b", "ev_ln2_g", "ev_ln2_b", "od_ln1_g", "od_ln1_b", "od_ln2_g", "od_ln2_b"]):
        vecs[i] = inp[nm][0]
    vecs[8, :512] = inp["od_q_norm_g"][0]
    vecs[9, :512] = inp["od_kv_norm_g"][0]
    table = np.asarray(inp["rel_table"], np.float32)
    tab31 = table[31:32, :].copy()
    s_l = np.arange(128)[:, None]
    yp = np.arange(384)[None, :]
    dd = yp - 128 - s_l
    bk = np.where(dd >= 0, _rel_bucket_np(dd), 31)
    tbraw = np.ascontiguousarray(np.transpose(table[bk, :], (0, 2, 1))).reshape(128, 16 * 384)
    in_maps = []
    for c in range(8):
        b, j = divmod(c, 4)
        xl = np.concatenate([x[b, t * 512:(t + 1) * 512] for t in TS[j]], 0)
        in_maps.append({
            "x_loc": np.ascontiguousarray(xl),
            "wsh": np.ascontiguousarray(flat[c * pr.NRS:(c + 1) * pr.NRS]),
            "vecs": vecs, "bfor": np.asarray(inp["ev_b_forget"], np.float32).reshape(1, 8),
            "tab31": tab31, "tbraw": tbraw, "cst": _consts_for(j, c),
            "ew1": np.ascontiguousarray(inp["od_exp_w1"][0, c]), "ew3": np.ascontiguousarray(inp["od_exp_w3"][0, c]),
            "ew2": np.ascontiguousarray(inp["od_exp_w2"][0, c]),
        })
    res = run_bass_kernel_spmd(pr.nc, in_maps, core_ids=list(range(8)))
    out = np.zeros((2, T, D), np.float32)
    for c in range(8):
        b, j = divmod(c, 4)
        yl = np.asarray(res.results[c]["y"], np.float32)
        for l, t in enumerate(TS[j]):
            out[b, t * 512:(t + 1) * 512] = yl[l * 512:(l + 1) * 512]
    return out
```

```python
import contextlib, math
import numpy as np
import concourse.bass as bass
import concourse.mybir as mybir
from concourse.bass_utils import run_bass_kernel_spmd

F32 = mybir.dt.float32
BF16 = mybir.dt.bfloat16
AF = mybir.ActivationFunctionType
ALU = mybir.AluOpType

D = 2048
T = 8192
NLOC = 2048
TS = [[j, 7 - j, 8 + j, 15 - j] for j in range(4)]
RANK_OF = {}
LIDX_OF = {}
for _j in range(4):
    for _l, _t in enumerate(TS[_j]):
        RANK_OF[_t] = _j
        LIDX_OF[_t] = _l
ALPHA = 4 ** 0.25
SCALE = 128 ** -0.5
NEG = -30000.0
GROUP4 = [[0, 1, 2, 3], [4, 5, 6, 7]]
GROUP8 = [[0, 1, 2, 3, 4, 5, 6, 7]]
PAIRS = [[0, 4], [1, 5], [2, 6], [3, 7]]
SL0 = 4224
SL1 = 4224


def gpos128(kt):
    i512, m = divmod(kt, 4)
    return RANK_OF[i512] * 16 + LIDX_OF[i512] * 4 + m


class Buf:
    __slots__ = ("w", "pw", "r")

    def __init__(self):
        self.w = None
        self.pw = []
        self.r = []


class Op:
    __slots__ = ("eng", "fn", "deps", "sig", "ev", "dma", "cc")


class Rec:
    EPOCH = 30000

    def __init__(self, nc, n_dma_sems=32):
        self.nc = nc
        self.ops = []
        self.engs = {"pe": nc.tensor, "dve": nc.vector, "act": nc.scalar, "pool": nc.gpsimd, "sp": nc.sync}
        self.n_dma_sems = n_dma_sems
        self.last = {}
        self.open_dma = []
        self.bar_deps = []
        self.bar_pending = set()

    def op(self, eng, fn, reads=(), writes=(), pwrites=(), dma=False, cc=False):
        ops = self.ops
        deps = set()
        for b in reads:
            if b.w is not None:
                deps.add(b.w)
            deps.update(b.pw)
        for b in writes:
            if b.w is not None:
                deps.add(b.w)
            deps.update(b.pw)
            deps.update(b.r)
        for b in pwrites:
            if b.w is not None:
                deps.add(b.w)
            deps.update(b.r)
        if eng in self.bar_pending:
            deps.update(self.bar_deps)
            self.bar_pending.discard(eng)
        i = len(ops)
        o = Op()
        o.eng = eng
        o.fn = fn
        o.dma = dma or cc
        o.cc = cc
        o.sig = False
        o.ev = None
        if eng == "pe" and not o.dma:
            o.deps = [d for d in deps if not (ops[d].eng == "pe" and not ops[d].dma)]
        else:
            o.deps = list(deps)
        ops.append(o)
        for b in writes:
            b.w = i
            b.pw = []
            b.r = []
        for b in pwrites:
            b.pw.append(i)
        for b in reads:
            if b.w == i:
                continue
            if not o.dma:
                b.r = [x for x in b.r if ops[x].dma or ops[x].eng != eng]
            b.r.append(i)
        if o.dma:
            self.open_dma.append(i)
        else:
            self.last[eng] = i
        return i

    def dma(self, eng, out, in_, reads=(), writes=(), pwrites=()):
        return self.op(eng, lambda e: e.dma_start(out=out, in_=in_), reads, writes, pwrites, dma=True)

    def barrier(self):
        self.bar_deps = list(self.last.values()) + list(self.open_dma)
        self.open_dma = []
        self.bar_pending = set(self.engs)

    def emit(self, st, final_wait_ops=()):
        nc = self.nc
        ops = self.ops
        for o in ops:
            for d in o.deps:
                ops[d].sig = True
        for d in final_wait_ops:
            ops[d].sig = True
        dma_sems = [st.enter_context(nc.semaphore(f"dq{i}")) for i in range(self.n_dma_sems)]
        dma_cnt = [0] * self.n_dma_sems
        rr = 0
        esem = {}
        ecnt = {}
        nep = {}
        for e in self.engs:
            esem[e] = st.enter_context(nc.semaphore(f"e_{e}_0"))
            ecnt[e] = 0
            nep[e] = 0
        waited = {e: {} for e in self.engs}
        for o in ops:
            E = self.engs[o.eng]
            need = {}
            for d in o.deps:
                s, v = ops[d].ev
                k = id(s)
                if k not in need or need[k][1] < v:
                    need[k] = (s, v)
            wd = waited[o.eng]
            for k, (s, v) in need.items():
                if wd.get(k, 0) >= v:
                    continue
                E.wait_ge(s, v)
                wd[k] = v
            ins = o.fn(E)
            if o.cc:
                if not hasattr(self, "_ccs"):
                    self._ccs = [st.enter_context(nc.semaphore(f"cc{i}")) for i in range(12)]
                    self._ccn = [0] * 12
                    self._ccr = 0
                q = self._ccr
                self._ccr = (q + 1) % 12
                self._ccn[q] += 1
                ins.then_inc(self._ccs[q])
                o.ev = (self._ccs[q], self._ccn[q])
            elif o.sig or o.dma:
                if o.dma:
                    q = rr
                    rr = (rr + 1) % self.n_dma_sems
                    dma_cnt[q] += 16
                    ins.then_inc(dma_sems[q], 16)
                    o.ev = (dma_sems[q], dma_cnt[q])
                else:
                    if ecnt[o.eng] >= self.EPOCH:
                        nep[o.eng] += 1
                        esem[o.eng] = st.enter_context(nc.semaphore(f"e_{o.eng}_{nep[o.eng]}"))
                        ecnt[o.eng] = 0
                    ecnt[o.eng] += 1
                    ins.then_inc(esem[o.eng], 1)
                    o.ev = (esem[o.eng], ecnt[o.eng])
        for d in final_wait_ops:
            s, v = ops[d].ev
            nc.sync.wait_ge(s, v)


class Arena:
    def __init__(self, t, cap16):
        self.t = t
        self.cap = cap16
        self.off = 0

    def reset(self):
        self.off = 0

    def alloc(self, shape, dt, parts=128):
        n = 1
        for s in shape:
            n *= s
        n16 = n * (2 if dt == F32 else 1)
        n16 = (n16 + 1) // 2 * 2
        o = self.off
        self.off += n16
        assert self.off <= self.cap, (self.off, self.cap)
        ap = self.t[0:parts, o:o + n16]
        if dt == F32:
            ap = ap.bitcast(F32)
        if dt != F32 and n16 != n:
            ap = ap[:, 0:n]
        if len(shape) == 2:
            ap = ap.rearrange("p (a b) -> p a b", b=shape[1])
        elif len(shape) == 3:
            ap = ap.rearrange("p (a b c) -> p a b c", b=shape[1], c=shape[2])
        return ap


def pack_layout(shapes):
    off = 0
    lay = {}
    for name, K, N in shapes:
        lay[name] = (off, K, N)
        off += K * N
        off = (off + 1023) // 1024 * 1024
    rows = off // 1024
    rows = (rows + 8 * 1024 - 1) // (8 * 1024) * (8 * 1024)
    return lay, rows


class Builder:
    def __init__(self, F0, FE, stop_after=None):
        self.F0 = F0
        self.FE = FE
        self.stop_after = stop_after
        self.shapes = [
            ("w_in0", D, 6152), ("w_out0", D, D), ("w1", D, F0), ("w3", D, F0), ("w2", F0, D),
            ("w_in1", D, 1104), ("w_uq", 512, D), ("w_qidx", 512, 1024), ("w_uk", 512, D), ("w_uv", 512, D),
            ("w_out1", D, D), ("router", D, 8),
        ]
        self.lay, self.NR = pack_layout(self.shapes)
        self.NRS = self.NR // 8

    def wview(self, name):
        off, K, N = self.lay[name]
        flat = self.WG.rearrange("r c -> (r c)")
        return flat[off:off + K * N].rearrange("(k n) -> k n", n=N)

    def mm(self, out, lhsT, rhs, start, stop, reads, writes, pw=()):
        self.R.op("pe", lambda e: e.matmul(out, lhsT=lhsT, rhs=rhs, start=start, stop=stop), reads, writes, pw)

    def tr(self, out, in_, ident, reads, writes, pw=()):
        self.R.op("pe", lambda e: e.transpose(out=out, in_=in_, identity=ident), reads, writes, pw)

    def act(self, out, in_, func, reads, writes, bias=None, scale=None, eng="act", pw=()):
        kw = {}
        if bias is not None:
            kw["bias"] = bias
        if scale is not None:
            kw["scale"] = scale
        self.R.op(eng, lambda e: e.activation(out=out, in_=in_, func=func, **kw), reads, writes, pw)

    def copy(self, eng, out, in_, reads, writes, pw=()):
        if eng == "act":
            self.R.op("act", lambda e: e.copy(out=out, in_=in_), reads, writes, pw)
        else:
            self.R.op(eng, lambda e: e.tensor_copy(out=out, in_=in_), reads, writes, pw)

    def tt(self, eng, out, in0, in1, op, reads, writes, pw=()):
        self.R.op(eng, lambda e: e.tensor_tensor(out=out, in0=in0, in1=in1, op=op), reads, writes, pw)

    def ts(self, eng, out, in0, s1, s2, op0, op1, reads, writes, accum_out=None, pw=()):
        if op1 is None:
            self.R.op(eng, lambda e: e.tensor_scalar(out=out, in0=in0, scalar1=s1, scalar2=None, op0=op0), reads, writes, pw)
        elif accum_out is None:
            self.R.op(eng, lambda e: e.tensor_scalar(out=out, in0=in0, scalar1=s1, scalar2=s2, op0=op0, op1=op1), reads, writes, pw)
        else:
            self.R.op(eng, lambda e: e.tensor_scalar(out=out, in0=in0, scalar1=s1, scalar2=s2, op0=op0, op1=op1, accum_out=accum_out), reads, writes, pw)

    def stt(self, eng, out, in0, scalar, in1, op0, op1, reads, writes, pw=()):
        self.R.op(eng, lambda e: e.scalar_tensor_tensor(out=out, in0=in0, scalar=scalar, in1=in1, op0=op0, op1=op1), reads, writes, pw)

    def dyn(self, eng, fn, reads, writes, pw=()):
        self.R.op(eng, fn, reads, writes, pw, dma=True)

    def memset(self, eng, ap, val, writes):
        self.R.op(eng, lambda e: e.memset(ap, val), (), writes)

    def dyn_put(self, eng, dram, rows_per_slot, row0, nrows, cols, src, group, reads, pwrites):
        c0, c1 = cols

        def fn(e):
            r = e.partition_id() % group
            return e.dma_start(out=dram[bass.ds(r * rows_per_slot + row0, nrows), c0:c1], in_=src)
        self.R.op(eng, fn, reads, (), pwrites, dma=True)

    def allreduce(self, groups, src, dst, reads, outbuf, esize=2):
        rows, cols = src.shape[0], src.shape[1]
        per = max(1, (4 * 1024 * 1024 // esize) // cols)
        for r0 in range(0, rows, per):
            n = min(per, rows - r0)
            self.R.op("pool", (lambda e, r0=r0, n=n: e.collective_compute(
                "AllReduce", ALU.add, replica_groups=groups, ins=[src[r0:r0 + n, :].opt()], outs=[dst[r0:r0 + n, :].opt()])),
                reads, (), (), cc=True)
            outbuf.pw.append(len(self.R.ops) - 1)

    def allreduce8(self, src, tmp, dst, reads, outbuf, esize=2, row_lo=0, row_hi=None):
        rows, cols = src.shape[0], src.shape[1]
        if row_hi is None:
            row_hi = rows
        per = max(1, (4 * 1024 * 1024 // esize) // cols)
        for r0 in range(row_lo, row_hi, per):
            n = min(per, row_hi - r0)
            self.R.op("pool", (lambda e, r0=r0, n=n: e.collective_compute(
                "AllReduce", ALU.add, replica_groups=GROUP4, ins=[src[r0:r0 + n, :].opt()], outs=[tmp[r0:r0 + n, :].opt()])),
                reads, (), (), cc=True)
            mid = Buf()
            mid.pw.append(len(self.R.ops) - 1)
            self.R.op("pool", (lambda e, r0=r0, n=n: e.collective_compute(
                "AllReduce", ALU.add, replica_groups=PAIRS, ins=[tmp[r0:r0 + n, :].opt()], outs=[dst[r0:r0 + n, :].opt()])),
                [mid], (), (), cc=True)
            outbuf.pw.append(len(self.R.ops) - 1)

    def zero_fill(self, dram, rows, cols, bufs):
        if cols > 8192:
            for r in range(0, rows, 128):
                for c in range(0, cols, 8192):
                    w = min(8192, cols - c)
                    b = Buf()
                    self.R.dma("sp", dram[r:r + 128, c:c + w], self.zt[:, 0:w], reads=[self.b_zt], writes=[b])
                    bufs.append(b)
            return
        per = 8192 // cols * 128
        r = 0
        while r < rows:
            n = min(per, rows - r)
            b = Buf()
            a = n // 128
            self.R.dma("sp", dram[r:r + n, :].rearrange("(p a) c -> p (a c)", p=128), self.zt[:, 0:a * cols],
                       reads=[self.b_zt], writes=[b])
            bufs.append(b)
            r += n

    def layer_norm(self, z, gt, bt, out, bz, bout, tmpst, reads_extra=()):
        st6, mv, rs = tmpst
        bs = Buf()
        for c in range(4):
            self.R.op("dve", (lambda e, c=c: e.bn_stats(out=st6[:, c, :], in_=z[:, c * 512:(c + 1) * 512])), [bz], [bs])
        self.R.op("dve", lambda e: e.bn_aggr(out=mv, in_=st6), [bs], [bs])
        self.ts("dve", rs[:, 0:1], mv[:, 1:2], 1e-5, None, ALU.add, None, [bs], [bs])
        self.act(rs[:, 1:2], rs[:, 0:1], AF.Sqrt, [bs], [bs])
        self.R.op("dve", lambda e: e.reciprocal(out=rs[:, 2:3], in_=rs[:, 1:2]), [bs], [bs])
        self.ts("dve", out, z, mv[:, 0:1], rs[:, 2:3], ALU.subtract, ALU.mult, [bz, bs], [bout])
        self.tt("pool", out, out, gt, ALU.mult, [bout] + list(reads_extra), [bout])
        self.tt("dve", out, out, bt, ALU.add, [bout] + list(reads_extra), [bout])


PADT = 1536
NT = 20
PT3 = 3
OFF_K = 0
OFF_V = 2048 * 512
OFF_X = 2 * 2048 * 512
SLOT = 2146304
PUTB = [(3, 0), (7, 1), (11, 0), (15, 1)]
TSC = [(0, 1), (7, -1), (8, 1), (15, -1)]
CSTW = 5376
import os
MOE_STOP = int(os.environ.get('MOE_STOP', '0'))


class Prog(Builder):
    def build(self):
        nc = bass.Bass("TRN2", target_bir_lowering=False)
        self.nc = nc
        F0, FE = self.F0, self.FE
        dt = nc.dram_tensor
        self.x_loc = dt("x_loc", [NLOC, D], F32, kind="ExternalInput").ap()
        self.wsh = dt("wsh", [self.NRS, 1024], F32, kind="ExternalInput").ap()
        self.vecs = dt("vecs", [16, D], F32, kind="ExternalInput").ap()
        self.bfor = dt("bfor", [1, 8], F32, kind="ExternalInput").ap()
        self.tab31 = dt("tab31", [1, 16], F32, kind="ExternalInput").ap()
        self.tbraw = dt("tbraw", [128, 16 * 384], F32, kind="ExternalInput").ap()
        self.cst = dt("cst", [128, CSTW], F32, kind="ExternalInput").ap()
        self.ew1 = dt("ew1", [D, FE], F32, kind="ExternalInput").ap()
        self.ew3 = dt("ew3", [D, FE], F32, kind="ExternalInput").ap()
        self.ew2 = dt("ew2", [FE, D], F32, kind="ExternalInput").ap()
        self.y = dt("y", [NLOC, D], F32, kind="ExternalOutput").ap()
        self.WZ = dt("WZ", [self.NR, 1024], BF16).ap()
        self.WG = dt("WG", [self.NR, 1024], BF16).ap()
        self.WT = dt("WT", [self.NR, 1024], BF16).ap()
        self.EXL = dt("EXL", [4, SLOT], BF16).ap()
        self.EXZ = dt("EXZ", [NT, SLOT], BF16).ap()
        self.EXG = dt("EXG", [NT, SLOT], BF16).ap()
        self.WIN = [dt(f"WIN{l}", [4 * (l + 1), SLOT], BF16).ap() for l in range(4)]
        self.CTS = dt("CTS", [NT, SLOT], BF16).ap()
        self.WZ3 = self.WZ.rearrange("(s r) c -> s r c", s=8)
        self.QI = dt("QI", [1024, NLOC], BF16).ap()
        self.WS = dt("WS", [NLOC, 32], F32).ap()
        self.MK = [dt(f"MK{l}", [16 * (l + 1), 128, 512], BF16).ap() for l in range(4)]
        self.GT = dt("GT", [NLOC, 8], F32).ap()
        self.HZ = dt("HZ", [8 * D, NLOC], BF16).ap()
        self.HT8 = dt("HT8", [8 * D, NLOC], BF16).ap()
        self.HG = dt("HG", [8 * D, NLOC], BF16).ap()
        self.GZ = dt("GZ", [8 * NLOC, 8], F32).ap()
        self.GT8 = dt("GT8", [8 * NLOC, 8], F32).ap()
        self.GG = dt("GG", [8 * NLOC, 8], F32).ap()
        self.EW1 = dt("EW1", [D, FE], BF16).ap()
        self.EW3 = dt("EW3", [D, FE], BF16).ap()
        self.EW2 = dt("EW2", [FE, D], BF16).ap()
        self.MO = dt("MO", [8 * NLOC, D], F32).ap()
        self.MT8 = dt("MT8", [8 * NLOC, D], F32).ap()
        self.MS = dt("MS", [8 * NLOC, D], F32).ap()
        self.FF = dt("FF", [NLOC, D], F32).ap()
        self.QT0 = dt("QT0", [2048, NLOC], BF16).ap()
        self.ATT = dt("ATT", [NLOC, 2048], BF16).ap()
        self.HM = dt("HM", [NLOC, D], F32).ap()
        self.HMT = dt("HMT", [D, NLOC], BF16).ap()
        self.H1 = dt("H1", [NLOC, D], F32).ap()
        self.H1T = dt("H1T", [D, NLOC], BF16).ap()
        with contextlib.ExitStack() as st:
            self.st = st
            ar_t = st.enter_context(nc.sbuf_tensor("arena", [128, 84 * 1024], BF16))
            cn_t = st.enter_context(nc.sbuf_tensor("consts", [128, 10 * 1024], BF16))
            self.A = Arena(ar_t, 84 * 1024)
            self.C = Arena(cn_t, 10 * 1024)
            self.ps = [st.enter_context(nc.psum_tensor(f"ps{i}", [128, 512], F32)) for i in range(8)]
            self.pb = [Buf() for _ in range(8)]
            self.R = Rec(nc)
            self.consts()
            self.phase_w()
            sa = self.stop_after
            if sa == "w":
                last = self.copy_out(self.WG[0:NLOC, :].bitcast(F32).rearrange("r (a c) -> (r a) c", c=2048) if False else self.x_loc)
            else:
                self.phase_a()
                if sa == "a":
                    last = self.copy_out(self.x_loc)
                else:
                    self.phase_b()
                    if sa == "b":
                        last = self.copy_out(self.x_loc)
                    else:
                        self.phase_c1("w_out0", self.x_loc, 0, self.HM, self.HMT)
                        self.phase_c2()
                        if sa == 0:
                            last = self.copy_out(self.H1)
                        else:
                            last = self.layer1()
            self.R.emit(st, final_wait_ops=last)
        return nc

    def copy_out(self, src):
        R, A = self.R, self.A
        R.barrier()
        A.reset()
        t = [A.alloc([D], F32) for _ in range(2)]
        b = [Buf(), Buf()]
        last = []
        for tt in range(16):
            k = tt % 2
            R.dma("sp", t[k], src[tt * 128:(tt + 1) * 128, :], (), [b[k]])
            last.append(R.dma("sp", self.y[tt * 128:(tt + 1) * 128, :], t[k], [b[k]], ()))
        return last

    def consts(self):
        C, R = self.C, self.R
        self.zt = C.alloc([8192], BF16)
        self.b_zt = Buf()
        self.memset("pool", self.zt, 0.0, [self.b_zt])
        self.b_c = Buf()
        self.identb = C.alloc([128], BF16)
        self.identf = C.alloc([128], F32)
        self.memset("pool", self.identb, 0.0, [self.b_c])
        R.op("pool", lambda e: e.affine_select(out=self.identb, in_=self.identb, pattern=[[-1, 128]],
                                               compare_op=ALU.not_equal, fill=1.0, base=0, channel_multiplier=1),
             [self.b_c], [self.b_c])
        self.copy("dve", self.identf, self.identb, [self.b_c], [self.b_c])
        self.ones1 = C.alloc([1], F32)
        self.memset("dve", self.ones1, 1.0, [self.b_c])
        self.t31 = C.alloc([16], F32)
        R.dma("sp", self.t31, self.tab31.partition_broadcast(128), (), [self.b_c])
        self.bf_t = C.alloc([8], F32)
        R.dma("sp", self.bf_t, self.bfor.partition_broadcast(128), (), [self.b_c])
        self.CA = C.alloc([512], BF16)
        self.memset("pool", self.CA, 0.0, [self.b_c])
        R.op("pool", lambda e: e.affine_select(out=self.CA, in_=self.CA, pattern=[[1, 512]],
                                               compare_op=ALU.is_ge, fill=NEG, base=0, channel_multiplier=-1),
             [self.b_c], [self.b_c])

    def dynreg(self, e, kind):
        if not hasattr(self, "_dr"):
            self._dr = {}
        key = (id(e), kind)
        if key not in self._dr:
            if kind == "R8":
                v = e.partition_id()
            elif kind == "R":
                v = e.partition_id() % 4
            else:
                v = self.dynreg(e, "R") * (-1) + 3
            self._dr[key] = e.snap(v)
        return self._dr[key]

    def put_slots(self, reads):
        self.b_exz = Buf()
        for l in range(4):
            base, sel = PUTB[l]

            def fn(e, l=l, base=base, sel=sel):
                reg = self.dynreg(e, "R" if sel == 0 else "RP")
                return e.dma_start(out=self.EXZ[base:NT, :][bass.ds(reg, 1), :].rearrange("a (p c) -> p (a c)", p=128),
                                   in_=self.EXL[l:l + 1, :].rearrange("a (p c) -> p (a c)", p=128))
            self.dyn("pool", fn, reads + self.ex_zb, (), [self.b_exz])

    def exchange(self):
        R = self.R
        self.b_exg = Buf()
        self.allreduce(GROUP4, self.EXZ[PT3:PT3 + 16, :].rearrange("a (p c) -> (a p) c", p=128),
                       self.EXG[PT3:PT3 + 16, :].rearrange("a (p c) -> (a p) c", p=128), [self.b_exz] + self.ex_zb, self.b_exg)
        self.b_win = [Buf() for _ in range(4)]
        for l in range(4):
            n = 4 * (l + 1)

            def fn(e, l=l, n=n):
                reg = self.dynreg(e, "R" if l % 2 == 0 else "RP")
                return e.dma_start(out=self.WIN[l].rearrange("a (p c) -> p a c", p=128),
                                   in_=self.EXG[bass.ds(reg, n), :].rearrange("a (p c) -> p a c", p=128))
            self.dyn("sp", fn, [self.b_exg] + self.exg_zb, [self.b_win[l]])

    def phase_w(self):
        R = self.R
        zb = []
        self.zero_fill(self.WZ, self.NR, 1024, zb)
        self.ex_zb = []
        exz2 = self.EXZ.rearrange("a (p c) -> (a p) c", p=128)
        self.zero_fill(exz2, NT * 128, SLOT // 128, self.ex_zb)
        bput = Buf()
        NRS = self.NRS
        def fn(e):
            r = self.dynreg(e, "R8")
            return e.dma_start(out=self.WZ3[bass.ds(r, 1), :, :].rearrange("s r c -> (s r c)").rearrange("(p x) -> p x", p=128),
                               in_=self.wsh.rearrange("r c -> (r c)").rearrange("(p x) -> p x", p=128))
        self.dyn("pool", fn, zb, (), [bput])
        self.b_wg = Buf()
        self.b_wg0 = Buf()
        off, K, N = self.lay["w_in0"]
        self.w_split = ((off + K * N + 1023) // 1024 + 2047) // 2048 * 2048
        self.w_reads = [bput] + zb
        self.allreduce8(self.WZ, self.WT, self.WG, self.w_reads, self.b_wg0, row_hi=self.w_split)
        self.b_wg.pw.extend(self.b_wg0.pw)
        self.exg_zb = []
        exg2 = self.EXG.rearrange("a (p c) -> (a p) c", p=128)
        self.zero_fill(exg2[0:PT3 * 128, :], PT3 * 128, SLOT // 128, self.exg_zb)
        self.zero_fill(exg2[(PT3 + 16) * 128:NT * 128, :], (NT - PT3 - 16) * 128, SLOT // 128, self.exg_zb)

    def rest_weights(self):
        self.allreduce8(self.WZ, self.WT, self.WG, self.w_reads, self.b_wg, row_lo=self.w_split)

    def transpose16(self, src_bf, b_src, dstT, col0, b_dst):
        for half in range(2):
            pi = 6 + half
            pv = self.ps[pi][:, :].bitcast(BF16)
            for q in range(8):
                dc = half * 8 + q
                self.tr(pv[:, q * 128:(q + 1) * 128], src_bf[:, dc * 128:(dc + 1) * 128], self.identb,
                        [b_src, self.b_c], [self.pb[pi]])
            dst = dstT[:, half * 8:(half + 1) * 8, col0:col0 + 128]
            src = pv.rearrange("p (a b) -> p a b", b=128)
            if half == 0:
                self.copy("dve", dst, src, [self.pb[pi]], [b_dst])
            else:
                self.copy("act", dst, src, [self.pb[pi]], (), [b_dst])

    def phase_a(self):
        R, A = self.R, self.A
        A.reset()
        xT = A.alloc([16, NLOC], BF16)
        b_xT = [Buf() for _ in range(16)]
        xs = [A.alloc([D], F32) for _ in range(2)]
        b_xs = [Buf(), Buf()]
        xb = [A.alloc([D], BF16) for _ in range(2)]
        b_xb = [Buf(), Buf()]
        for tt in range(16):
            k = tt % 2
            R.dma("sp", xs[k], self.x_loc[tt * 128:(tt + 1) * 128, :], (), [b_xs[k]])
            self.copy("act", xb[k], xs[k], [b_xs[k]], [b_xb[k]])
            self.transpose16(xb[k], b_xb[k], xT, tt * 128, b_xT[tt])
        W = self.wview("w_in0")
        slab = [A.alloc([16, 512], BF16) for _ in range(2)]
        b_slab = [Buf(), Buf()]
        stg = [A.alloc([512], BF16) for _ in range(4)]
        b_stg = [Buf() for _ in range(4)]
        kms = A.alloc([8, 8], F32)
        kmb = A.alloc([8, 8], BF16)
        b_km = Buf()
        lfs = A.alloc([16, 8], F32)
        lft = A.alloc([2, 8], F32)
        b_lf = Buf()
        b_lft = Buf()
        self.b_exl = Buf()
        self.b_qt0 = Buf()
        EXLk = self.EXL[:, OFF_K:OFF_K + 2048 * 512].rearrange("l (f c) -> l f c", c=512)
        EXLv = self.EXL[:, OFF_V:OFF_V + 2048 * 512].rearrange("l (h p c) -> l h p c", h=16, p=128)
        lf3 = A.alloc([3, 16, 8], BF16)
        lfr = A.alloc([16, 8], F32)
        fm = [(0, "q", 0), (512, "q", 4), (1024, "k", 0), (1536, "k", 4),
              (3072, "q", 8), (3584, "q", 12), (4096, "k", 8), (4608, "k", 12)]
        tm = [(2048, 0), (2560, 4), (5120, 8), (5632, 12)]
        slabs = [(c, "fm", kind, hb) for c, kind, hb in fm] + [(c, "tm", None, hb) for c, hb in tm] + [(6144, "f", None, 0)]

        def load_slab(i):
            col0, typ = slabs[i][0], slabs[i][1]
            k = i % 2
            if typ == "f":
                R.dma("sp", slab[k][:, :, 0:8], W[:, 6144:6152].rearrange("(a p) n -> p a n", p=128), [self.b_wg0], [b_slab[k]])
            else:
                R.dma("sp", slab[k], W[:, col0:col0 + 512].rearrange("(a p) n -> p a n", p=128), [self.b_wg0], [b_slab[k]])
        load_slab(0)
        pcnt = 0
        scnt = 0
        for i, (col0, typ, kind, hb) in enumerate(slabs):
            k = i % 2
            if i + 1 < len(slabs):
                load_slab(i + 1)
            if typ == "fm":
                for l in range(4):
                    for hc in range(4):
                        pi = pcnt % 4
                        pcnt += 1
                        for dc in range(16):
                            self.mm(self.ps[pi][:, :], slab[k][:, dc, hc * 128:(hc + 1) * 128], xT[:, dc, l * 512:(l + 1) * 512],
                                    dc == 0, dc == 15, [b_slab[k]] + b_xT[l * 4:l * 4 + 4], [self.pb[pi]])
                        si = scnt % 4
                        scnt += 1
                        h = hb + hc
                        feat0 = h * 128
                        if kind == "q":
                            self.act(stg[si], self.ps[pi][:, :], AF.Copy, [self.pb[pi]], [b_stg[si]], scale=SCALE)
                            R.dma("pool", self.QT0[feat0:feat0 + 128, l * 512:(l + 1) * 512], stg[si], [b_stg[si]], (), [self.b_qt0])
                        else:
                            self.copy("dve", stg[si], self.ps[pi][:, :], [self.pb[pi]], [b_stg[si]])
                            if h < 8:
                                R.op("dve", (lambda e, h=h, l=l, pi=pi: e.reduce_sum(
                                    out=kms[:, h, 2 * l:2 * l + 2],
                                    in_=self.ps[pi][:, :].rearrange("p (a b) -> p a b", b=256),
                                    axis=mybir.AxisListType.X)), [self.pb[pi]], (), [b_km])

                            R.dma("pool", EXLk[l, feat0:feat0 + 128, :], stg[si], [b_stg[si]], (), [self.b_exl])
            elif typ == "tm":
                for tt in range(16):
                    pi = pcnt % 4
                    pcnt += 1
                    for dc in range(16):
                        self.mm(self.ps[pi][:, :], xT[:, dc, tt * 128:(tt + 1) * 128], slab[k][:, dc, :],
                                dc == 0, dc == 15, [b_slab[k], b_xT[tt]], [self.pb[pi]])
                    si = scnt % 4
                    scnt += 1
                    self.copy("act" if tt % 2 else "dve", stg[si], self.ps[pi][:, :], [self.pb[pi]], [b_stg[si]])

                    sub = tt % 4
                    R.dma("pool", EXLv[tt // 4, hb:hb + 4, :, sub * 128:(sub + 1) * 128].rearrange("h p d -> p h d"),
                          stg[si].rearrange("p (h d) -> p h d", d=128), [b_stg[si]], (), [self.b_exl])
            else:
                for tt in range(16):
                    pi = pcnt % 4
                    pcnt += 1
                    for dc in range(16):
                        self.mm(self.ps[pi][:, 0:8], xT[:, dc, tt * 128:(tt + 1) * 128], slab[k][:, dc, 0:8],
                                dc == 0, dc == 15, [b_slab[k], b_xT[tt]], [self.pb[pi]])
                    self.tt("dve", lft[:, 0, :], self.ps[pi][:, 0:8], self.bf_t, ALU.add, [self.pb[pi], self.b_c], [b_lft])
                    self.act(lft[:, 1, :], lft[:, 0, :], AF.Exp, [b_lft], [b_lft], scale=-1.0)
                    self.act(lft[:, 0, :], lft[:, 1, :], AF.Ln, [b_lft, self.b_c], [b_lft], bias=self.ones1)
                    self.ts("dve", lfs[:, tt, :], lft[:, 0, :], -1.0, None, ALU.mult, None, [b_lft], (), pw=[b_lf])
        self.ts("dve", kmb, kms, 1.0 / 256.0, None, ALU.mult, None, [b_km], [b_km])
        self.copy("dve", lf3[:, 0, :, :], lfs, [b_lf], [b_lf])
        self.tt("dve", lfr, lfs, lf3[:, 0, :, :], ALU.subtract, [b_lf], [b_lf])
        self.copy("dve", lf3[:, 1, :, :], lfr, [b_lf], [b_lf])
        self.tt("dve", lfr, lfr, lf3[:, 1, :, :], ALU.subtract, [b_lf], [b_lf])
        self.copy("dve", lf3[:, 2, :, :], lfr, [b_lf], [b_lf])
        for l in range(4):
            R.dma("pool", self.EXL[l, OFF_X:OFF_X + 2048].rearrange("(p h u) -> p h u", p=128, u=2),
                  kmb[:, :, 2 * l:2 * l + 2], [b_km], (), [self.b_exl])
            R.dma("pool", self.EXL[l, OFF_X + 2048:OFF_X + 2048 + 12288].rearrange("(p k a h) -> p k a h", p=128, k=3, h=8),
                  lf3[:, :, 4 * l:4 * l + 4, :], [b_lf], (), [self.b_exl])
        self.put_slots([self.b_exl])
        self.exchange()
        R.barrier()

    def phase_b(self):
        R, A = self.R, self.A
        A.reset()
        self.rest_weights()
        cst = A.alloc([CSTW], F32)
        b_cst = Buf()
        R.dma("sp", cst, self.cst, (), [b_cst])
        U = cst[:, 0:128]
        LT = cst[0:64, 128:192]
        SEL127 = cst[:, 192:320]
        ones64 = cst[0:64, 320:448]
        ESELf = cst[0:32, 832:832 + 4096]
        lf = A.alloc([64, 8], F32)
        ccb = A.alloc([NT, 64], BF16)
        cc = ccb.rearrange("p a c -> p (a c)").bitcast(F32).rearrange("p (g h) -> p g h", h=8)
        tot = A.alloc([8], F32)
        rhsM = A.alloc([64, 8], F32)
        b_lf = Buf()
        b_cc = Buf()
        lf3g = A.alloc([16, 3, 32], BF16)
        R.dma("sp", lf3g, self.EXG[PT3:PT3 + 16, OFF_X + 2048:OFF_X + 2048 + 12288].rearrange("a (p k c) -> p a k c", p=128, k=3),
              [self.b_exg], [b_lf])
        lfv = lf.rearrange("p (a s) h -> p a (s h)", s=4)
        self.tt("dve", lfv, lf3g[:, :, 0, :], lf3g[:, :, 1, :], ALU.add, [b_lf], [b_lf])
        self.tt("dve", lfv, lfv, lf3g[:, :, 2, :], ALU.add, [b_lf], [b_lf])
        lf2 = lf.rearrange("p g h -> p (g h)")
        p0 = self.ps[6]
        for h in range(8):
            self.mm(p0[0:64, h:h + 1], lf[:, :, h], self.ones1, True, True, [b_lf, self.b_c], [self.pb[6]])
        self.copy("dve", tot[0:64, :], p0[0:64, 0:8], [self.pb[6]], [b_cc])
        for h in range(8):
            self.ts("dve", rhsM[0:64, :, h], LT, tot[0:64, h:h + 1], None, ALU.mult, None, [b_cst, b_cc], [b_cc])
        p1 = self.ps[7]
        self.mm(p1[:, :], U, lf2, True, False, [b_cst, b_lf], [self.pb[7]])
        self.mm(p1[:, :], ones64, rhsM[0:64, :, :].rearrange("p g h -> p (g h)"), False, True, [b_cst, b_cc], [self.pb[7]])
        self.memset("dve", cc, 30000.0, [b_cc])
        self.copy("dve", cc[:, PT3 * 4:PT3 * 4 + 64, :].rearrange("p g h -> p (g h)"), p1[:, :], [self.pb[7]], [b_cc])
        b_ct = Buf()
        R.dma("act", self.CTS[:, 0:8192].rearrange("a (p c) -> p a c", p=128), ccb, [b_cc], [b_ct])
        tbr = A.alloc([8, 384], F32)
        TB = A.alloc([8, 384], BF16)
        OWN = A.alloc([8, 256], BF16)
        eselb = A.alloc([32 * 128], BF16, parts=32)
        b_tb = Buf()
        R.dma("sp", tbr, self.tbraw[:, 0:8 * 384].rearrange("p (h w) -> p h w", w=384), (), [b_tb])
        for h in range(8):
            self.ts("dve", TB[:, h, :], tbr[:, h, :], self.t31[:, h:h + 1], None, ALU.subtract, None, [b_tb, self.b_c], [b_tb])
        for h in range(8):
            self.tt("dve", OWN[:, h, :], TB[:, h, 128:384], self.CA[:, 0:256], ALU.add, [b_tb, self.b_c], [b_tb])
        self.copy("dve", eselb, ESELf, [b_cst], [b_tb])
        KT = [A.alloc([T], BF16) for _ in range(2)]
        VA = [A.alloc([64, 130], BF16) for _ in range(2)]
        QT = [A.alloc([512], BF16) for _ in range(2)]
        KM = [A.alloc([32], BF16) for _ in range(2)]
        b_kv = [Buf(), Buf()]
        for k in range(2):
            self.memset("dve", VA[k][:, :, 128:130], 1.0, [b_kv[k]])
            self.memset("dve", KM[k], 0.0, [b_kv[k]])
        CWb = A.alloc([16, 64], BF16)
        CW = CWb.rearrange("p a c -> p (a c)").bitcast(F32).rearrange("p (g h) -> p g h", h=8)
        CLB = A.alloc([64, 8], F32)
        b_cw = Buf()
        cb = [A.alloc([64], F32) for _ in range(2)]
        b_cb = [Buf(), Buf()]
        PT = [A.alloc([256], BF16) for _ in range(2)]
        b_pt = [Buf(), Buf()]
        gm = A.alloc([2, 32], F32)
        g2 = A.alloc([2, 32], F32)
        m8 = A.alloc([2, 8], F32)
        maskT = A.alloc([256], BF16, parts=32)
        b_gm = Buf()
        b_mask = Buf()
        ost = [A.alloc([128], BF16) for _ in range(4)]
        b_ost = [Buf() for _ in range(4)]
        rec = A.alloc([4], F32)
        b_rec = [Buf() for _ in range(4)]
        self.b_att = Buf()
        iters = [(l, h) for l in range(4) for h in range(16)]

        def loads(it):
            l, h = iters[it]
            k = it % 2
            n = 4 * (l + 1)
            Wk = self.WIN[l][:, OFF_K:OFF_K + 2048 * 512].rearrange("a (f c) -> a f c", c=512)
            Wv = self.WIN[l][:, OFF_V:OFF_V + 2048 * 512].rearrange("a (h p c) -> a h p c", h=16, p=128)
            Wm = self.WIN[l][:, OFF_X:OFF_X + 2048].rearrange("a (p c) -> a p c", p=128)
            R.dma("sp", KT[k][:, 0:n * 512].rearrange("p (a c) -> p a c", c=512),
                  Wk[:, h * 128:(h + 1) * 128, :].rearrange("a p c -> p a c"), [self.b_win[l]], [b_kv[k]])
            for a in range(n):
                R.dma("sp", VA[k][:, 4 * a:4 * a + 4, 0:128], Wv[a, h, :, :].rearrange("p (s d) -> p s d", d=128),
                      [self.b_win[l]], (), [b_kv[k]])
            R.dma("sp", QT[k], self.QT0[h * 128:(h + 1) * 128, l * 512:(l + 1) * 512], [self.b_qt0], (), [b_kv[k]])
            if h < 8:
                R.dma("sp", KM[k][:, 0:2 * n].rearrange("p (a u) -> p a u", u=2),
                      Wm[:, :, 2 * h:2 * h + 2].rearrange("a p u -> p a u"), [self.b_win[l]], (), [b_kv[k]])
        scnt = 0
        ocnt = 0
        cbn = 0
        loads(0)
        for it, (l, h) in enumerate(iters):
            k = it % 2
            n = 4 * (l + 1)
            moba = h < 8
            if h == 0:
                self.dyn("sp", (lambda e, n=n, l=l: e.dma_start(
                    out=CWb[:, 0:n, :],
                    in_=self.CTS[bass.ds(self.dynreg(e, "R" if l % 2 == 0 else "RP"), n), 0:8192].rearrange("a (p c) -> p a c", p=128))),
                    [b_ct], [b_cw])
                for c in range(0, 32 * n, 512):
                    self.mm(self.ps[6][:, :], SEL127, CW.rearrange("p g h -> p (g h)")[:, c:c + 512], True, True,
                            [b_cst, b_cw], [self.pb[6]])
                    self.copy("dve", CLB.rearrange("p g h -> p (g h)")[:, c:c + 512], self.ps[6][:, :], [self.pb[6]], (), [b_cw])
            if it + 1 < len(iters):
                loads(it + 1)
            for u in range(2):
                kd0 = 4 * (n - 1) + 2 * u
                kd1 = kd0 + 1
                kp = kd0 - 1
                q0 = u * 256
                if moba:
                    bv = cst[:, 448 + (2 * l + u) * 32:448 + (2 * l + u + 1) * 32]
                    for v in range(2):
                        self.mm(self.ps[4][:, v * 32:v * 32 + 32], QT[k][:, q0 + v * 128:q0 + (v + 1) * 128], KM[k][:, 0:32],
                                True, True, [b_kv[k]], [self.pb[4]])
                    for v in range(2):
                        self.tt("dve", gm[:, v, :], self.ps[4][:, v * 32:v * 32 + 32], bv, ALU.add, [self.pb[4], b_cst],
                                [b_gm] if v == 0 else (), () if v == 0 else [b_gm])
                    for v in range(2):
                        R.op("dve", (lambda e, v=v: e.max(out=m8[:, v, :], in_=gm[:, v, :])), [b_gm], [b_gm])
                    for v in range(2):
                        self.ts("dve", g2[:, v, :], gm[:, v, :], m8[:, v, 2:3], None, ALU.is_ge, None, [b_gm], [b_gm])
                    self.ts("dve", gm, gm, -1e29, None, ALU.is_gt, None, [b_gm], [b_gm])
                    self.tt("dve", g2, g2, gm, ALU.mult, [b_gm], [b_gm])
                    self.ts("dve", g2, g2, -NEG, NEG, ALU.mult, ALU.add, [b_gm], [b_gm])
                    for v in range(2):
                        self.tr(self.ps[5][0:32, v * 128:(v + 1) * 128], g2[:, v, :], self.identf, [b_gm, self.b_c], [self.pb[5]])
                    self.copy("dve", maskT, self.ps[5][0:32, 0:256], [self.pb[5]], [b_mask])
                    bias_ap = self.t31[:, h:h + 1]
                else:
                    hh = h - 8
                    cbk = cbn % 2
                    cbn += 1
                    self.ts("dve", cb[cbk][:, 0:kd1 + 1], CW[:, 0:kd1 + 1, hh], -1.0, CLB[:, kd1, hh:hh + 1], ALU.mult, ALU.add,
                            [b_cw], [b_cb[cbk]])
                for kk in range(0, kd1 + 1):
                    si = scnt % 2
                    scnt += 1
                    S = self.ps[si]
                    qa, qb = (128, 256) if kk == kd1 else (0, 256)
                    extra = []
                    if moba:
                        if kk < kd0:
                            wb = kk // 2
                            extra.append((eselb[0:32, wb * 128:(wb + 1) * 128], maskT[0:32, qa:qb], [b_tb, b_mask]))
                        if kk == kp:
                            extra.append((self.identb, TB[:, h, 0:256], [self.b_c, b_tb]))
                        if kk == kd0:
                            extra.append((self.identb, OWN[:, h, 0:256], [self.b_c, b_tb]))
                        if kk == kd1:
                            extra.append((self.identb, OWN[:, h, 0:128], [self.b_c, b_tb]))
                    else:
                        if kk == kd0:
                            extra.append((self.identb, self.CA[:, 0:256], [self.b_c]))
                        if kk == kd1:
                            extra.append((self.identb, self.CA[:, 0:128], [self.b_c]))
                    self.mm(S[:, qa:qb], KT[k][:, kk * 128:(kk + 1) * 128], QT[k][:, q0 + qa:q0 + qb],
                            True, len(extra) == 0, [b_kv[k]], [self.pb[si]])
                    for ei, (lt_, rh_, rd_) in enumerate(extra):
                        self.mm(S[:, qa:qb], lt_, rh_, False, ei == len(extra) - 1, rd_, [self.pb[si]])
                    if moba:
                        self.act(PT[si][:, qa:qb], S[:, qa:qb], AF.Exp, [self.pb[si], self.b_c], [b_pt[si]], bias=bias_ap)
                    else:
                        self.act(PT[si][:, qa:qb], S[:, qa:qb], AF.Exp, [self.pb[si], b_cb[cbk]], [b_pt[si]],
                                 bias=cb[cbk][:, kk:kk + 1])
                    for v in range(2):
                        if kk == kd1 and v == 0:
                            continue
                        lastk = kd0 if v == 0 else kd1
                        self.mm(self.ps[2 + v][:, 0:130], PT[si][:, v * 128:(v + 1) * 128], VA[k][:, kk, :],
                                kk == 0, kk == lastk, [b_pt[si], b_kv[k]], [self.pb[2 + v]])
                for v in range(2):
                    oi = ocnt % 4
                    ocnt += 1
                    O = self.ps[2 + v]
                    R.op("dve", (lambda e, O=O, oi=oi: e.reciprocal(out=rec[:, oi:oi + 1], in_=O[:, 128:129])),
                         [self.pb[2 + v]], [b_rec[oi]])
                    self.act(ost[oi], O[:, 0:128], AF.Copy, [self.pb[2 + v], b_rec[oi]], [b_ost[oi]], scale=rec[:, oi:oi + 1])
                    row0 = l * 512 + q0 + v * 128
                    R.dma("act", self.ATT[row0:row0 + 128, h * 128:(h + 1) * 128], ost[oi], [b_ost[oi]], (), [self.b_att])
        R.barrier()

    def phase_c1(self, wname, res_src, vrow, HMdst, HMTdst, router=False):
        R, A = self.R, self.A
        A.reset()
        Wv = self.wview(wname)
        Wo = A.alloc([16, D], BF16)
        b_wo = Buf()
        for c in range(4):
            R.dma("sp", Wo[:, :, c * 512:(c + 1) * 512], Wv[:, c * 512:(c + 1) * 512].rearrange("(a p) n -> p a n", p=128),
                  [self.b_wg], (), [b_wo])
        gt = A.alloc([D], F32)
        bt = A.alloc([D], F32)
        b_gb = Buf()
        R.dma("sp", gt, self.vecs[vrow:vrow + 1, :].partition_broadcast(128), (), (), [b_gb])
        R.dma("sp", bt, self.vecs[vrow + 1:vrow + 2, :].partition_broadcast(128), (), (), [b_gb])
        at = [A.alloc([D], BF16) for _ in range(2)]
        b_at = [Buf(), Buf()]
        aT = [A.alloc([16, 128], BF16) for _ in range(2)]
        b_aT = [Buf(), Buf()]
        xr = [A.alloc([D], F32) for _ in range(2)]
        b_xr = [Buf(), Buf()]
        z = [A.alloc([D], F32) for _ in range(2)]
        b_z = [Buf(), Buf()]
        hb = [A.alloc([D], BF16) for _ in range(2)]
        b_hb = [Buf(), Buf()]
        hT = [A.alloc([16, 128], BF16) for _ in range(2)]
        b_hT = [Buf(), Buf()]
        st6 = A.alloc([4, 6], F32)
        mv = A.alloc([2], F32)
        rs = A.alloc([4], F32)
        self.b_hm = Buf()
        self.b_hmt = Buf()
        if router:
            wr = A.alloc([16, 8], BF16)
            R.dma("sp", wr, self.wview("router").rearrange("(a p) n -> p a n", p=128), [self.b_wg], (), [b_wo])
            rt = A.alloc([64], F32)
            b_rt = Buf()
            self.b_gt = Buf()
        for tt in range(16):
            k = tt % 2
            R.dma("sp", at[k], self.ATT[tt * 128:(tt + 1) * 128, :], [self.b_att], [b_at[k]])
            R.dma("sp", xr[k], res_src[tt * 128:(tt + 1) * 128, :], [self.b_hm] if res_src is not self.x_loc else [], [b_xr[k]])
            self.transpose16(at[k], b_at[k], aT[k], 0, b_aT[k])
            for c in range(4):
                for f in range(16):
                    self.mm(self.ps[c][:, :], aT[k][:, f, :], Wo[:, f, c * 512:(c + 1) * 512], f == 0, f == 15,
                            [b_aT[k], b_wo], [self.pb[c]])
                self.stt("dve", z[k][:, c * 512:(c + 1) * 512], xr[k][:, c * 512:(c + 1) * 512], ALPHA, self.ps[c][:, :],
                         ALU.mult, ALU.add, [b_xr[k], self.pb[c]], [b_z[k]] if c == 0 else (), () if c == 0 else [b_z[k]])
            self.ln_inplace(z[k], gt, bt, b_z[k], (st6, mv, rs), b_gb)
            R.dma("pool", HMdst[tt * 128:(tt + 1) * 128, :], z[k], [b_z[k]], (), [self.b_hm])
            self.copy("act", hb[k], z[k], [b_z[k]], [b_hb[k]])
            self.transpose16(hb[k], b_hb[k], hT[k], 0, b_hT[k])
            R.dma("pool", HMTdst[:, tt * 128:(tt + 1) * 128].rearrange("(a p) t -> p a t", p=128), hT[k], [b_hT[k]], (), [self.b_hmt])
            if router:
                for dc in range(16):
                    self.mm(self.ps[5][:, 0:8], hT[k][:, dc, :], wr[:, dc, :], dc == 0, dc == 15, [b_hT[k], b_wo], [self.pb[5]])
                lg16, m8, ex, g1, g2, ga, gb = rt[:, 40:56], rt[:, 8:16], rt[:, 16:17], rt[:, 17:18], rt[:, 18:19], rt[:, 24:32], rt[:, 32:40]
                lg = lg16[:, 0:8]
                nl1 = rt[:, 19:20]
                self.memset("dve", lg16, -1e30, [b_rt])
                self.copy("dve", lg, self.ps[5][:, 0:8], [self.pb[5]], [b_rt])
                R.op("dve", lambda e: e.max(out=m8, in_=lg16), [b_rt], [b_rt])
                self.ts("dve", nl1, m8[:, 0:1], -1.0, None, ALU.mult, None, [b_rt], [b_rt])
                self.act(ex, m8[:, 1:2], AF.Exp, [b_rt], [b_rt], bias=nl1)
                self.ts("dve", g1, ex, 1.0, None, ALU.add, None, [b_rt], [b_rt])
                R.op("dve", lambda e: e.reciprocal(out=g2, in_=g1), [b_rt], [b_rt])
                self.copy("dve", g1, g2, [b_rt], [b_rt])
                self.tt("dve", g2, ex, g1, ALU.mult, [b_rt], [b_rt])
                self.ts("dve", ga, lg, m8[:, 0:1], None, ALU.is_equal, None, [b_rt], [b_rt])
                self.ts("dve", ga, ga, g1, None, ALU.mult, None, [b_rt], [b_rt])
                self.ts("dve", gb, lg, m8[:, 1:2], None, ALU.is_equal, None, [b_rt], [b_rt])
                self.ts("dve", gb, gb, g2, None, ALU.mult, None, [b_rt], [b_rt])
                self.tt("dve", ga, ga, gb, ALU.add, [b_rt], [b_rt])
                R.dma("pool", self.GT[tt * 128:(tt + 1) * 128, :], ga, [b_rt], (), [self.b_gt])
        R.barrier()

    def ln_inplace(self, z, gt, bt, bz, tmpst, b_gb):
        st6, mv, rs = tmpst
        bs = Buf()
        R = self.R
        for c in range(4):
            R.op("dve", (lambda e, c=c: e.bn_stats(out=st6[:, c, :], in_=z[:, c * 512:(c + 1) * 512])), [bz],
                 [bs] if c == 0 else (), () if c == 0 else [bs])
        R.op("dve", lambda e: e.bn_aggr(out=mv, in_=st6), [bs], [bs])
        self.ts("dve", rs[:, 0:1], mv[:, 1:2], 1e-5, None, ALU.add, None, [bs], [bs])
        self.act(rs[:, 1:2], rs[:, 0:1], AF.Sqrt, [bs], [bs])
        R.op("dve", lambda e: e.reciprocal(out=rs[:, 2:3], in_=rs[:, 1:2]), [bs], [bs])
        self.ts("dve", z, z, mv[:, 0:1], rs[:, 2:3], ALU.subtract, ALU.mult, [bs], [bz])
        self.tt("pool", z, z, gt, ALU.mult, [b_gb], [bz])
        self.tt("dve", z, z, bt, ALU.add, [b_gb], [bz])

    def phase_c2(self):
        R, A = self.R, self.A
        A.reset()
        F0 = self.F0
        NF = F0 // 128
        W1 = self.wview("w1")
        W3 = self.wview("w3")
        W2 = self.wview("w2")
        gt = A.alloc([D], F32)
        bt = A.alloc([D], F32)
        b_gb = Buf()
        R.dma("sp", gt, self.vecs[2:3, :].partition_broadcast(128), (), (), [b_gb])
        R.dma("sp", bt, self.vecs[3:4, :].partition_broadcast(128), (), (), [b_gb])
        hT = A.alloc([16, 512], BF16)
        b_hT = Buf()
        hm4 = A.alloc([4, D], F32)
        b_hm4 = [Buf() for _ in range(4)]
        gT = A.alloc([NF, 512], BF16)
        b_gT = Buf()
        w13 = [A.alloc([2, 16, 128], BF16) for _ in range(2)]
        b_w13 = [Buf(), Buf()]
        w2s = A.alloc([NF, 256], BF16)
        b_w2 = Buf()
        sA = [A.alloc([512], BF16) for _ in range(2)]
        b_sA = [Buf(), Buf()]
        hb = A.alloc([D], BF16)
        b_hb = Buf()
        oT = A.alloc([16, 128], BF16)
        b_oT = Buf()
        st6 = A.alloc([4, 6], F32)
        mv = A.alloc([2], F32)
        rs = A.alloc([4], F32)
        self.b_h1 = Buf()
        self.b_h1t = Buf()
        wc = 0
        pc = 0
        for l in range(4):
            R.dma("sp", hT, self.HMT[:, l * 512:(l + 1) * 512].rearrange("(a p) t -> p a t", p=128), [self.b_hmt], [b_hT])
            for ts_ in range(4):
                R.dma("sp", hm4[:, ts_, :], self.HM[l * 512 + ts_ * 128: l * 512 + (ts_ + 1) * 128, :], [self.b_hm], [b_hm4[ts_]])
            for f0 in range(0, F0, 128):
                k = wc % 2
                wc += 1
                R.dma("sp", w13[k][:, 0, :, :], W1[:, f0:f0 + 128].rearrange("(a p) n -> p a n", p=128), [self.b_wg], [b_w13[k]])
                R.dma("sp", w13[k][:, 1, :, :], W3[:, f0:f0 + 128].rearrange("(a p) n -> p a n", p=128), [self.b_wg], (), [b_w13[k]])
                for fc in range(1):
                    fidx = f0 // 128 + fc
                    pa = (pc % 2) * 2
                    sk = pc % 2
                    pc += 1
                    for dc in range(16):
                        self.mm(self.ps[pa][:, :], w13[k][:, 0, dc, fc * 128:(fc + 1) * 128], hT[:, dc, :], dc == 0, dc == 15,
                                [b_w13[k], b_hT], [self.pb[pa]])
                    for dc in range(16):
                        self.mm(self.ps[pa + 1][:, :], w13[k][:, 1, dc, fc * 128:(fc + 1) * 128], hT[:, dc, :], dc == 0, dc == 15,
                                [b_w13[k], b_hT], [self.pb[pa + 1]])
                    self.act(sA[sk], self.ps[pa][:, :], AF.Silu, [self.pb[pa]], [b_sA[sk]])
                    self.tt("dve", gT[:, fidx, :], sA[sk], self.ps[pa + 1][:, :], ALU.mult, [b_sA[sk], self.pb[pa + 1]],
                            [b_gT] if fidx == 0 else (), () if fidx == 0 else [b_gT])
            for c0 in range(0, D, 256):
                R.dma("sp", w2s, W2[:, c0:c0 + 256].rearrange("(a p) n -> p a n", p=128), [self.b_wg], [b_w2])
                for ts_ in range(4):
                    pi = 4 + (pc % 2)
                    pc += 1
                    for f in range(NF):
                        self.mm(self.ps[pi][:, 0:256], gT[:, f, ts_ * 128:(ts_ + 1) * 128], w2s[:, f, :], f == 0, f == NF - 1,
                                [b_gT, b_w2], [self.pb[pi]])
                    self.stt("dve", hm4[:, ts_, c0:c0 + 256], hm4[:, ts_, c0:c0 + 256], ALPHA, self.ps[pi][:, 0:256],
                             ALU.mult, ALU.add, [self.pb[pi]], (), [b_hm4[ts_]])
            for ts_ in range(4):
                self.ln_inplace(hm4[:, ts_, :], gt, bt, b_hm4[ts_], (st6, mv, rs), b_gb)
                row0 = l * 512 + ts_ * 128
                R.dma("pool", self.H1[row0:row0 + 128, :], hm4[:, ts_, :], [b_hm4[ts_]], (), [self.b_h1])
                self.copy("act", hb, hm4[:, ts_, :], [b_hm4[ts_]], [b_hb])
                self.transpose16(hb, b_hb, oT, 0, b_oT)
                R.dma("pool", self.H1T[:, row0:row0 + 128].rearrange("(a p) t -> p a t", p=128), oT, [b_oT], (), [self.b_h1t])
        R.barrier()

    def layer1(self):
        self.l1_proj()
        if self.stop_after == "p":
            return self.copy_out(self.H1)
        self.l1_index()
        if self.stop_after == "i":
            return self.copy_out(self.H1)
        self.l1_attn()
        if self.stop_after == "t":
            return self.copy_out(self.H1)
        self.phase_c1("w_out1", self.H1, 4, self.HM, self.HMT, router=True)
        if self.stop_after == "d":
            return self.copy_out(self.HM)
        self.l1_moe()
        return self.l1_final()

    def l1_proj(self):
        R, A = self.R, self.A
        A.reset()
        nc = self.nc
        Win = self.wview("w_in1")
        Wqi = self.wview("w_qidx")
        Wuq = self.wview("w_uq")
        Wuk = self.wview("w_uk")
        Wuv = self.wview("w_uv")
        win = A.alloc([16, 1104], BF16)
        wqi = A.alloc([4, 1024], BF16)
        wuq = A.alloc([4, D], BF16)
        wuk = A.alloc([4, D], BF16)
        wuv = A.alloc([4, D], BF16)
        b_w = Buf()
        R.dma("sp", win, Win.rearrange("(a p) n -> p a n", p=128), [self.b_wg], [b_w])
        R.dma("sp", wqi, Wqi.rearrange("(a p) n -> p a n", p=128), [self.b_wg], (), [b_w])
        R.dma("sp", wuq, Wuq.rearrange("(a p) n -> p a n", p=128), [self.b_wg], (), [b_w])
        R.dma("sp", wuk, Wuk.rearrange("(a p) n -> p a n", p=128), [self.b_wg], (), [b_w])
        R.dma("sp", wuv, Wuv.rearrange("(a p) n -> p a n", p=128), [self.b_wg], (), [b_w])
        gq = A.alloc([512], F32)
        gkv = A.alloc([512], F32)
        R.dma("sp", gq, self.vecs[8:9, 0:512].partition_broadcast(128), (), (), [b_w])
        R.dma("sp", gkv, self.vecs[9:10, 0:512].partition_broadcast(128), (), (), [b_w])
        hT = A.alloc([16, 512], BF16)
        b_hT = Buf()
        cqT = A.alloc([4, 512], BF16)
        ckT = A.alloc([4, 512], BF16)
        kiT = A.alloc([512], BF16, parts=64)
        b_cT = Buf()
        junk = A.alloc([512], BF16)
        ssq = A.alloc([4], F32)
        b_ss = Buf()
        nrm = [A.alloc([512], BF16) for _ in range(2)]
        b_nrm = [Buf(), Buf()]
        kib = A.alloc([64], BF16)
        wsb = A.alloc([32], F32)
        b_ws = Buf()
        stg = [A.alloc([512], BF16) for _ in range(4)]
        b_stg = [Buf() for _ in range(4)]
        self.b_qi = Buf()
        self.b_ws = Buf()
        self.b_exl = Buf()
        self.b_qt0 = Buf()
        EXLk = self.EXL[:, OFF_K:OFF_K + 2048 * 512].rearrange("l (f c) -> l f c", c=512)
        EXLv = self.EXL[:, OFF_V:OFF_V + 2048 * 512].rearrange("l (h p c) -> l h p c", h=16, p=128)
        scnt = 0
        pcnt = 0
        for l in range(4):
            R.dma("sp", hT, self.H1T[:, l * 512:(l + 1) * 512].rearrange("(a p) t -> p a t", p=128), [self.b_h1t], [b_hT])
            for ts_ in range(4):
                tok = slice(ts_ * 128, (ts_ + 1) * 128)
                for gi, (c0, w) in enumerate([(0, 512), (512, 512), (1024, 80)]):
                    for dc in range(16):
                        self.mm(self.ps[gi][:, 0:w], hT[:, dc, tok], win[:, dc, c0:c0 + w], dc == 0, dc == 15, [b_hT, b_w], [self.pb[gi]])
                for gi, (gvec, dstT) in enumerate([(gq, cqT), (gkv, ckT)]):
                    self.memset("dve", ssq[:, 0:1], 0.0, [b_ss])
                    R.op("act", (lambda e, gi=gi: e.activation(out=junk, in_=self.ps[gi][:, :], func=AF.Square, accum_out=ssq[:, 0:1])),
                         [self.pb[gi], b_ss], [b_ss])
                    self.ts("dve", ssq[:, 1:2], ssq[:, 0:1], 1.0 / 512.0, 1e-6, ALU.mult, ALU.add, [b_ss], [b_ss])
                    self.act(ssq[:, 2:3], ssq[:, 1:2], AF.Sqrt, [b_ss], [b_ss])
                    R.op("dve", lambda e: e.reciprocal(out=ssq[:, 3:4], in_=ssq[:, 2:3]), [b_ss], [b_ss])
                    k = gi
                    self.stt("dve", nrm[k], self.ps[gi][:, :], ssq[:, 3:4], gvec, ALU.mult, ALU.mult, [self.pb[gi], b_ss, b_w], [b_nrm[k]])
                    pv = self.ps[6 + gi][:, :].bitcast(BF16)
                    for rc in range(4):
                        self.tr(pv[:, rc * 128:(rc + 1) * 128], nrm[k][:, rc * 128:(rc + 1) * 128], self.identb, [b_nrm[k], self.b_c], [self.pb[6 + gi]])
                    self.copy("act", dstT[:, :, tok], pv[:, 0:512].rearrange("p (a b) -> p a b", b=128), [self.pb[6 + gi]], (), [b_cT])
                self.copy("dve", kib, self.ps[2][:, 0:64], [self.pb[2]], [b_ws])
                self.ts("dve", wsb[:, 0:16], self.ps[2][:, 64:80], 1.0 / 32.0, None, ALU.mult, None, [self.pb[2]], [b_ws])
                R.op("act", lambda e: e.sign(out=wsb[:, 16:32], in_=wsb[:, 0:16]), [b_ws], [b_ws])
                self.tt("dve", wsb[:, 0:16], wsb[:, 0:16], wsb[:, 16:32], ALU.mult, [b_ws], [b_ws])
                row0 = l * 512 + ts_ * 128
                R.dma("pool", self.WS[row0:row0 + 128, :], wsb, [b_ws], (), [self.b_ws])
                pv = self.ps[5][:, :].bitcast(BF16)
                self.tr(pv[0:64, 0:128], kib, self.identb, [b_ws, self.b_c], [self.pb[5]])
                self.copy("dve", kiT[0:64, tok], pv[0:64, 0:128], [self.pb[5]], (), [b_cT])
            R.dma("pool", self.EXL[l, OFF_X:OFF_X + 32768].rearrange("(p c) -> p c", p=64), kiT, [b_cT], (), [self.b_exl])
            jobs = [("qi", mc) for mc in range(8)] + [("q", h) for h in range(16)] + [("k", h) for h in range(16)]
            for kind, idx in jobs:
                pi = pcnt % 4
                pcnt += 1
                if kind == "qi":
                    W_, src, cols = wqi, cqT, slice(idx * 128, (idx + 1) * 128)
                elif kind == "q":
                    W_, src, cols = wuq, cqT, slice(idx * 128, (idx + 1) * 128)
                else:
                    W_, src, cols = wuk, ckT, slice(idx * 128, (idx + 1) * 128)
                for rc in range(4):
                    self.mm(self.ps[pi][:, :], W_[:, rc, cols], src[:, rc, :], rc == 0, rc == 3, [b_w, b_cT], [self.pb[pi]])
                si = scnt % 4
                scnt += 1
                if kind == "q":
                    self.act(stg[si], self.ps[pi][:, :], AF.Copy, [self.pb[pi]], [b_stg[si]], scale=SCALE)
                    R.dma("pool", self.QT0[idx * 128:(idx + 1) * 128, l * 512:(l + 1) * 512], stg[si], [b_stg[si]], (), [self.b_qt0])
                elif kind == "qi":
                    self.copy("dve", stg[si], self.ps[pi][:, :], [self.pb[pi]], [b_stg[si]])
                    R.dma("pool", self.QI[idx * 128:(idx + 1) * 128, l * 512:(l + 1) * 512], stg[si], [b_stg[si]], (), [self.b_qi])
                else:
                    self.copy("dve", stg[si], self.ps[pi][:, :], [self.pb[pi]], [b_stg[si]])
                    R.dma("pool", EXLk[l, idx * 128:(idx + 1) * 128, :], stg[si], [b_stg[si]], (), [self.b_exl])
            for ts_ in range(4):
                for hb in range(0, 16, 4):
                    pi = pcnt % 4
                    pcnt += 1
                    for rc in range(4):
                        self.mm(self.ps[pi][:, :], ckT[:, rc, ts_ * 128:(ts_ + 1) * 128], wuv[:, rc, hb * 128:(hb + 4) * 128],
                                rc == 0, rc == 3, [b_w, b_cT], [self.pb[pi]])
                    si = scnt % 4
                    scnt += 1
                    self.copy("act", stg[si], self.ps[pi][:, :], [self.pb[pi]], [b_stg[si]])
                    R.dma("pool", EXLv[l, hb:hb + 4, :, ts_ * 128:(ts_ + 1) * 128].rearrange("h p d -> p h d"),
                          stg[si].rearrange("p (h d) -> p h d", d=128), [b_stg[si]], (), [self.b_exl])
        self.put_slots([self.b_exl])
        self.exchange()
        R.barrier()

    def l1_index(self):
        R, A = self.R, self.A
        A.reset()
        cst = A.alloc([CSTW], F32)
        b_cst = Buf()
        R.dma("sp", cst, self.cst, (), [b_cst])
        CM = A.alloc([4, 512], F32)
        for m in range(4):
            self.memset("pool", CM[:, m, :], 0.0, [b_cst])
            R.op("pool", (lambda e, m=m: e.affine_select(out=CM[:, m, :], in_=CM[:, m, :], pattern=[[-1, 512]],
                                                         compare_op=ALU.is_ge, fill=-1e30, base=m * 128, channel_multiplier=1)),
                 [b_cst], [b_cst])
        KI = A.alloc([T], BF16)
        b_ki = Buf()
        QIt = [A.alloc([8, 128], BF16) for _ in range(2)]
        wst = [A.alloc([32], F32) for _ in range(2)]
        b_q = [Buf(), Buf()]
        sc = A.alloc([T], F32)
        sc2 = A.alloc([512], F32)
        b_sc = Buf()
        b_sc2 = Buf()
        rl = [A.alloc([512], F32) for _ in range(4)]
        b_rl = [Buf() for _ in range(4)]
        junk = A.alloc([T], BF16)
        b_junk = Buf()
        bis = A.alloc([8], F32)
        b_bis = Buf()
        mk = A.alloc([T], F32)
        b_mk = Buf()
        mstg = [A.alloc([8, 128], BF16) for _ in range(2)]
        b_mstg = [Buf(), Buf()]
        self.b_mk = Buf()
        rcnt = 0
        tcnt = 0
        qn = 0
        for l in range(4):
            n = 4 * (l + 1)
            L = n * 512
            src = self.WIN[l][:, OFF_X:OFF_X + 32768].rearrange("a (p c) -> p a c", p=64)
            R.dma("sp", KI[0:64, 0:L].rearrange("p (a c) -> p a c", c=512), src, [self.b_win[l]], [b_ki])
            R.dma("sp", KI[64:128, 0:L].rearrange("p (a c) -> p a c", c=512), src, [self.b_win[l]], (), [b_ki])
            for ts_ in range(4):
                k = qn % 2
                qn += 1
                row0 = l * 512 + ts_ * 128
                R.dma("sp", QIt[k], self.QI[:, row0:row0 + 128].rearrange("(a p) t -> p a t", p=128), [self.b_qi], [b_q[k]])
                R.dma("sp", wst[k], self.WS[row0:row0 + 128, :], [self.b_ws], (), [b_q[k]])
                for a in range(n):
                    ksl = slice(a * 512, (a + 1) * 512)
                    virt = cst[:, 4928 + l * 16 + a:4928 + l * 16 + a + 1]
                    for hh in range(16):
                        mc, half = divmod(hh, 2)
                        pi = hh % 2
                        prt = slice(half * 64, (half + 1) * 64)
                        self.mm(self.ps[pi][:, :], QIt[k][prt, mc, :], KI[prt, ksl], True, True, [b_q[k], b_ki], [self.pb[pi]])
                        ri = rcnt % 4
                        rcnt += 1
                        self.act(rl[ri], self.ps[pi][:, :], AF.Relu, [self.pb[pi], b_q[k]], [b_rl[ri]], scale=wst[k][:, hh:hh + 1])
                        sg = wst[k][:, 16 + hh:17 + hh]
                        if hh == 0:
                            self.ts("dve", sc[:, ksl], rl[ri], sg, virt, ALU.mult, ALU.add, [b_rl[ri], b_q[k], b_cst], [b_sc])
                        else:
                            self.stt("dve", sc[:, ksl], rl[ri], sg, sc[:, ksl], ALU.mult, ALU.add, [b_rl[ri], b_q[k]], [b_sc])
                    if a == n - 1:
                        self.tt("dve", sc[:, ksl], sc[:, ksl], CM[:, ts_, :], ALU.add, [b_cst], [b_sc])
                self.memset("dve", bis[:, 0:1], -32.0, [b_bis])
                for it in range(24):
                    hstep = 32.0 / (2 ** it)
                    self.ts("dve", bis[:, 1:2], bis[:, 0:1], hstep, None, ALU.add, None, [b_bis], [b_bis])
                    self.memset("dve", bis[:, 2:3], 0.0, [b_bis])
                    self.ts("dve", junk[:, 0:L], sc[:, 0:L], bis[:, 1:2], 0.0, ALU.is_ge, ALU.add, [b_sc, b_bis], [b_junk],
                            accum_out=bis[:, 2:3])
                    self.ts("dve", bis[:, 3:4], bis[:, 2:3], 256.0, hstep, ALU.is_ge, ALU.mult, [b_junk, b_bis], [b_bis])
                    self.tt("dve", bis[:, 0:1], bis[:, 0:1], bis[:, 3:4], ALU.add, [b_bis], [b_bis])
                self.ts("dve", mk[:, 0:L], sc[:, 0:L], bis[:, 0:1], None, ALU.is_ge, None, [b_sc, b_bis], [b_mk])
                self.ts("pool", mk[:, 0:L], mk[:, 0:L], -NEG, NEG, ALU.mult, ALU.add, [b_mk], [b_mk])
                for g0 in range(0, 4 * n, 4):
                    pi = 4 + (tcnt % 2)
                    mi = tcnt % 2
                    tcnt += 1
                    for q in range(4):
                        kk = g0 + q
                        self.tr(self.ps[pi][:, q * 128:(q + 1) * 128], mk[:, kk * 128:(kk + 1) * 128], self.identf,
                                [b_mk, self.b_c], [self.pb[pi]])
                    self.copy("act", mstg[mi][:, 0:4, :], self.ps[pi][:, :].rearrange("p (a b) -> p a b", b=128), [self.pb[pi]], [b_mstg[mi]])
                    R.dma("pool", self.MK[l][g0:g0 + 4, :, ts_ * 128:(ts_ + 1) * 128].rearrange("k s q -> s k q"), mstg[mi][:, 0:4, :],
                          [b_mstg[mi]], (), [self.b_mk])
        R.barrier()

    def l1_attn(self):
        R, A = self.R, self.A
        A.reset()
        tbr = A.alloc([384], F32)
        TB = A.alloc([16, 384], BF16)
        b_tb = Buf()
        for h in range(16):
            R.dma("sp", tbr, self.tbraw[:, h * 384:(h + 1) * 384], (), [b_tb])
            self.ts("dve", TB[:, h, :], tbr, self.t31[:, h:h + 1], None, ALU.subtract, None, [b_tb, self.b_c], [b_tb])
        KT = [A.alloc([T], BF16) for _ in range(2)]
        VA = [A.alloc([64, 130], BF16) for _ in range(2)]
        QT = [A.alloc([512], BF16) for _ in range(2)]
        b_kv = [Buf(), Buf()]
        for k in range(2):
            self.memset("pool", VA[k][:, :, 128:130], 1.0, [b_kv[k]])
        MKs = A.alloc([64, 512], BF16)
        b_mks = Buf()
        PT = [A.alloc([512], BF16) for _ in range(2)]
        b_pt = [Buf(), Buf()]
        ost = [A.alloc([128], BF16) for _ in range(4)]
        b_ost = [Buf() for _ in range(4)]
        rec = A.alloc([4], F32)
        b_rec = [Buf() for _ in range(4)]
        self.b_att = Buf()
        iters = [(l, h) for l in range(4) for h in range(16)]

        def loads(it):
            l, h = iters[it]
            k = it % 2
            n = 4 * (l + 1)
            Wk = self.WIN[l][:, OFF_K:OFF_K + 2048 * 512].rearrange("a (f c) -> a f c", c=512)
            Wv = self.WIN[l][:, OFF_V:OFF_V + 2048 * 512].rearrange("a (h p c) -> a h p c", h=16, p=128)
            R.dma("sp", KT[k][:, 0:n * 512].rearrange("p (a c) -> p a c", c=512),
                  Wk[:, h * 128:(h + 1) * 128, :].rearrange("a p c -> p a c"), [self.b_win[l]], [b_kv[k]])
            for a in range(n):
                R.dma("sp", VA[k][:, 4 * a:4 * a + 4, 0:128], Wv[a, h, :, :].rearrange("p (s d) -> p s d", d=128),
                      [self.b_win[l]], (), [b_kv[k]])
            R.dma("sp", QT[k], self.QT0[h * 128:(h + 1) * 128, l * 512:(l + 1) * 512], [self.b_qt0], (), [b_kv[k]])
        scnt = 0
        ocnt = 0
        loads(0)
        for it, (l, h) in enumerate(iters):
            k = it % 2
            n = 4 * (l + 1)
            if h == 0:
                for g0 in range(0, 4 * n, 16):
                    R.dma("sp", MKs[:, g0:g0 + 16, :], self.MK[l][g0:g0 + 16, :, :].rearrange("k s q -> s k q"), [self.b_mk],
                          [b_mks] if g0 == 0 else (), () if g0 == 0 else [b_mks])
            if it + 1 < len(iters):
                loads(it + 1)
            kown = 4 * (n - 1)
            for kk in range(0, 4 * n):
                si = scnt % 2
                scnt += 1
                S = self.ps[si]
                m = kk - kown
                qa = 128 * m if m > 0 else 0
                extra = [(self.identb, MKs[:, kk, qa:512], qa, 512, [self.b_c, b_mks])]
                if kk == kown - 1:
                    extra.append((self.identb, TB[:, h, 0:128], 0, 128, [self.b_c, b_tb]))
                if m >= 0:
                    w = min(256, 512 - 128 * m)
                    extra.append((self.identb, TB[:, h, 128:128 + w], 128 * m, 128 * m + w, [self.b_c, b_tb]))
                self.mm(S[:, qa:512], KT[k][:, kk * 128:(kk + 1) * 128], QT[k][:, qa:512], True, False, [b_kv[k]], [self.pb[si]])
                for ei, (lt_, rh_, c0, c1, rd_) in enumerate(extra):
                    self.mm(S[:, c0:c1], lt_, rh_, False, ei == len(extra) - 1, rd_, [self.pb[si]])
                self.act(PT[si][:, qa:512], S[:, qa:512], AF.Exp, [self.pb[si], self.b_c], [b_pt[si]], bias=self.t31[:, h:h + 1])
                for v in range(4):
                    if m > v:
                        continue
                    self.mm(self.ps[2 + v][:, 0:130], PT[si][:, v * 128:(v + 1) * 128], VA[k][:, kk, :],
                            kk == 0, kk == kown + v, [b_pt[si], b_kv[k]], [self.pb[2 + v]])
            for v in range(4):
                oi = ocnt % 4
                ocnt += 1
                O = self.ps[2 + v]
                R.op("dve", (lambda e, O=O, oi=oi: e.reciprocal(out=rec[:, oi:oi + 1], in_=O[:, 128:129])),
                     [self.pb[2 + v]], [b_rec[oi]])
                self.act(ost[oi], O[:, 0:128], AF.Copy, [self.pb[2 + v], b_rec[oi]], [b_ost[oi]], scale=rec[:, oi:oi + 1])
                row0 = l * 512 + v * 128
                R.dma("pool", self.ATT[row0:row0 + 128, h * 128:(h + 1) * 128], ost[oi], [b_ost[oi]], (), [self.b_att])
        R.barrier()


    def l1_moe(self):
        R, A = self.R, self.A
        A.reset()
        FE = self.FE
        NF = FE // 128
        hz_zb = []
        self.zero_fill(self.HZ, 8 * D, NLOC, hz_zb)
        gz_zb = Buf()
        R.dma("sp", self.GZ.rearrange("(p a) c -> p (a c)", p=128), self.zt[:, 0:2048].bitcast(F32), [self.b_zt], [gz_zb])
        b_hz = Buf()
        b_gz = Buf()

        def put_h(e):
            r = self.dynreg(e, "R8")
            return e.dma_start(out=self.HZ.rearrange("(s r) c -> s r c", s=8)[bass.ds(r, 1), :, :].rearrange("s (a p) c -> p (s a) c", p=128),
                               in_=self.HMT.rearrange("(a p) c -> p a c", p=128))
        self.dyn("pool", put_h, [self.b_hmt] + hz_zb, (), [b_hz])

        def put_g(e):
            r = self.dynreg(e, "R8")
            return e.dma_start(out=self.GZ.rearrange("(s r) c -> s r c", s=8)[bass.ds(r, 1), :, :].rearrange("s (p a) c -> p (s a c)", p=128),
                               in_=self.GT.rearrange("(p a) c -> p (a c)", p=128))
        self.dyn("sp", put_g, [self.b_gt, gz_zb], (), [b_gz])
        b_hg = Buf()
        self.allreduce8(self.HZ, self.HT8, self.HG, [b_hz] + hz_zb, b_hg)
        b_gg = Buf()
        self.allreduce8(self.GZ, self.GT8, self.GG, [b_gz, gz_zb], b_gg, esize=4)
        b_ew = Buf()
        for src, dst, rows, cols in ((self.ew1, self.EW1, D, FE), (self.ew3, self.EW3, D, FE), (self.ew2, self.EW2, FE, D)):
            step = max(128, (1 << 20) // cols // 128 * 128)
            for r0 in range(0, rows, step):
                nrow = min(step, rows - r0)
                R.dma("pool", dst[r0:r0 + nrow, :], src[r0:r0 + nrow, :], (), (), [b_ew])
        CAP = 192
        cst = A.alloc([16], F32)
        b_cst = Buf()
        R.dma("sp", cst, self.cst[:, 4992:5008], (), [b_cst])
        Uc = A.alloc([128], F32)
        iot = A.alloc([CAP], F32)
        ones = A.alloc([128], F32)
        R.dma("sp", Uc, self.cst[:, 0:128], (), (), [b_cst])
        R.dma("sp", iot, self.cst[:, 5120:5120 + CAP], (), (), [b_cst])
        Ub = A.alloc([128], BF16)
        onesb = A.alloc([128], BF16)
        avb = A.alloc([4], BF16)
        R.op("dve", lambda e: e.memset(onesb, 1.0), (), (), [b_cst])
        self.copy("dve", Ub, Uc, [b_cst], (), pw=[b_cst])
        STb = A.alloc([2, 512], BF16)
        gsl = A.alloc([2], F32)
        b_yh = Buf()
        xT = A.alloc([16, 512], BF16)
        b_xT = Buf()
        gts = A.alloc([4, 8], F32)
        gcol = A.alloc([4], F32)
        b_g = Buf()
        av = A.alloc([4], F32)
        pq = A.alloc([8], F32)
        off = A.alloc([4], F32)
        posm = A.alloc([4], F32)
        b_pos = Buf()
        Sb = A.alloc([4, CAP], BF16)
        Gf = A.alloc([4, CAP], F32)
        b_S = Buf()
        X = A.alloc([4, D], BF16)
        b_X = Buf()
        yh = X[:, 0:2, :]
        yl = X[:, 2:4, :]
        XcT = A.alloc([16, CAP], BF16)
        b_Xc = Buf()
        gT = A.alloc([NF, CAP], BF16)
        b_gT = Buf()
        Y = A.alloc([2, D], F32)
        b_Y = Buf()
        GTc = A.alloc([2, 512], F32)
        b_GT = Buf()
        w13 = [A.alloc([2, 16, 128], BF16) for _ in range(2)]
        b_w13 = [Buf(), Buf()]
        w2s = [A.alloc([NF, 256], BF16) for _ in range(2)]
        b_w2 = [Buf(), Buf()]
        sA = [A.alloc([CAP], BF16) for _ in range(2)]
        b_sA = [Buf(), Buf()]
        ot = [A.alloc([512], F32) for _ in range(2)]
        b_ot = [Buf() for _ in range(2)]
        self.b_mo = Buf()
        wc = 0
        pc = 0
        w2c = 0
        oc = 0
        chunks = [(0, 128), (128, CAP - 128)]
        for s_ in range(8):
            for tq in range(4):
                r0 = s_ * NLOC + tq * 512
                R.dma("sp", xT, self.HG[s_ * D:(s_ + 1) * D, tq * 512:(tq + 1) * 512].rearrange("(a p) t -> p a t", p=128), [b_hg], [b_xT])
                R.dma("sp", gts, self.GG[r0:r0 + 512, :].rearrange("(a p) e -> p a e", p=128), [b_gg], [b_g])
                for a in range(4):
                    self.tt("dve", gts[:, a, :], gts[:, a, :], cst[:, 0:8], ALU.mult, [b_cst], [b_g])
                R.op("dve", lambda e: e.reduce_sum(out=gcol, in_=gts, axis=mybir.AxisListType.X), [b_g], [b_g])
                self.ts("dve", av, gcol, 0.0, None, ALU.is_gt, None, [b_g], [b_pos])
                self.copy("dve", avb, av, [b_pos], [b_pos])
                self.mm(self.ps[6][:, 0:4], Ub, avb, True, True, [b_cst, b_pos], [self.pb[6]])
                self.mm(self.ps[6][:, 4:8], onesb, avb, True, True, [b_cst, b_pos], [self.pb[6]])
                self.copy("dve", pq, self.ps[6][:, 0:8], [self.pb[6]], [b_pos])
                self.memset("dve", off[:, 0:1], 0.0, [b_pos])
                self.copy("dve", off[:, 1:2], pq[:, 4:5], [b_pos], [b_pos])
                self.tt("dve", off[:, 2:3], off[:, 1:2], pq[:, 5:6], ALU.add, [b_pos], [b_pos])
                self.tt("dve", off[:, 3:4], off[:, 2:3], pq[:, 6:7], ALU.add, [b_pos], [b_pos])
                self.tt("dve", posm, pq[:, 0:4], off, ALU.add, [b_pos], [b_pos])
                self.tt("dve", posm, posm, av, ALU.mult, [b_pos], [b_pos])
                self.ts("dve", posm, posm, -1.0, None, ALU.add, None, [b_pos], [b_pos])
                for a in range(4):
                    self.ts("dve", Sb[:, a, :], iot, posm[:, a:a + 1], None, ALU.is_equal, None, [b_pos, b_cst],
                            [b_S] if a == 0 else (), pw=() if a == 0 else [b_S])
                for a in range(4):
                    self.ts("dve", Gf[:, a, :], Sb[:, a, :], gcol[:, a:a + 1], None, ALU.mult, None, [b_S, b_g], (), pw=[b_S])
                if MOE_STOP == 1:
                    continue
                for a in range(4):
                    for half in range(2):
                        pi = 6 + half
                        pv = self.ps[pi][:, :].bitcast(BF16)
                        for q in range(8):
                            dc = half * 8 + q
                            self.tr(pv[:, q * 128:(q + 1) * 128], xT[:, dc, a * 128:(a + 1) * 128], self.identb, [b_xT, self.b_c], [self.pb[pi]])
                        self.copy("act" if half else "dve", X[:, a, half * 1024:(half + 1) * 1024], pv, [self.pb[pi]],
                                  [b_X] if (a == 0 and half == 0) else (), pw=() if (a == 0 and half == 0) else [b_X])
                for d2 in range(8):
                    pi = 6 + (d2 % 2)
                    for q in range(2):
                        dc = d2 * 2 + q
                        for a in range(4):
                            self.mm(self.ps[pi][:, q * CAP:(q + 1) * CAP], X[:, a, dc * 128:(dc + 1) * 128], Sb[:, a, :], a == 0, a == 3,
                                    [b_X, b_S], [self.pb[pi]])
                    self.copy("act" if d2 % 2 else "dve", XcT[:, 2 * d2:2 * d2 + 2, :],
                              self.ps[pi][:, 0:2 * CAP].rearrange("p (a b) -> p a b", b=CAP), [self.pb[pi]],
                              [b_Xc] if d2 == 0 else (), pw=() if d2 == 0 else [b_Xc])
                if MOE_STOP == 2:
                    continue
                for f0 in range(0, FE, 128):
                    k = wc % 2
                    wc += 1
                    R.dma("sp", w13[k][:, 0, :, :], self.EW1[:, f0:f0 + 128].rearrange("(a p) n -> p a n", p=128), [b_ew], [b_w13[k]])
                    R.dma("sp", w13[k][:, 1, :, :], self.EW3[:, f0:f0 + 128].rearrange("(a p) n -> p a n", p=128), [b_ew], (), [b_w13[k]])
                    fidx = f0 // 128
                    pa = (pc % 2) * 2
                    sk = pc % 2
                    pc += 1
                    for dc in range(16):
                        self.mm(self.ps[pa][:, 0:CAP], w13[k][:, 0, dc, :], XcT[:, dc, :], dc == 0, dc == 15, [b_w13[k], b_Xc], [self.pb[pa]])
                    for dc in range(16):
                        self.mm(self.ps[pa + 1][:, 0:CAP], w13[k][:, 1, dc, :], XcT[:, dc, :], dc == 0, dc == 15, [b_w13[k], b_Xc], [self.pb[pa + 1]])
                    self.act(sA[sk], self.ps[pa][:, 0:CAP], AF.Silu, [self.pb[pa]], [b_sA[sk]])
                    self.tt("dve", gT[:, fidx, :], sA[sk], self.ps[pa + 1][:, 0:CAP], ALU.mult, [b_sA[sk], self.pb[pa + 1]],
                            [b_gT] if fidx == 0 else (), () if fidx == 0 else [b_gT])
                first_y = True
                for c0 in range(0, D, 256):
                    k2 = w2c % 2
                    w2c += 1
                    R.dma("sp", w2s[k2], self.EW2[:, c0:c0 + 256].rearrange("(a p) n -> p a n", p=128), [b_ew], [b_w2[k2]])
                    for ci, (cs, cn) in enumerate(chunks):
                        pi = 4 + (pc % 2)
                        pc += 1
                        for f in range(NF):
                            self.mm(self.ps[pi][0:cn, 0:256], gT[:, f, cs:cs + cn], w2s[k2][:, f, :], f == 0, f == NF - 1,
                                    [b_gT, b_w2[k2]], [self.pb[pi]])
                        self.copy("act" if ci else "dve", Y[0:cn, ci, c0:c0 + 256], self.ps[pi][0:cn, 0:256], [self.pb[pi]],
                                  [b_Y] if first_y else (), pw=() if first_y else [b_Y])
                        first_y = False
                if MOE_STOP == 3:
                    continue
                for a in range(4):
                    pi = 6 + (a % 2)
                    for ci, (cs, cn) in enumerate(chunks):
                        self.tr(self.ps[pi][0:cn, ci * 128:(ci + 1) * 128], Gf[:, a, cs:cs + cn], self.identf, [b_S, self.b_c], [self.pb[pi]])
                    for ci, (cs, cn) in enumerate(chunks):
                        self.copy("dve", GTc[0:cn, ci, a * 128:(a + 1) * 128], self.ps[pi][0:cn, ci * 128:(ci + 1) * 128], [self.pb[pi]],
                                  [b_GT] if (a == 0 and ci == 0) else (), pw=() if (a == 0 and ci == 0) else [b_GT])
                for ci, (cs, cn) in enumerate(chunks):
                    R.op("dve", (lambda e, ci=ci, cn=cn: e.reduce_sum(out=gsl[0:cn, ci:ci + 1], in_=GTc[0:cn, ci, :], axis=mybir.AxisListType.X)),
                         [b_GT], [b_yh] if ci == 0 else (), () if ci == 0 else [b_yh])
                    self.ts("dve", STb[0:cn, ci, :], GTc[0:cn, ci, :], 0.0, None, ALU.is_gt, None, [b_GT], (), pw=[b_yh])
                for ci, (cs, cn) in enumerate(chunks):
                    self.ts("dve", Y[0:cn, ci, :], Y[0:cn, ci, :], gsl[0:cn, ci:ci + 1], None, ALU.mult, None, [b_yh], [b_Y] if False else (), pw=[b_Y])
                for ci, (cs, cn) in enumerate(chunks):
                    self.copy("act", yh[0:cn, ci, :], Y[0:cn, ci, :], [b_Y, b_Xc], [b_X] if ci == 0 else (), pw=[b_yh] if ci == 0 else [b_yh, b_X])
                for ci, (cs, cn) in enumerate(chunks):
                    self.tt("dve", Y[0:cn, ci, :], Y[0:cn, ci, :], yh[0:cn, ci, :], ALU.subtract, [b_yh], (), pw=[b_Y])
                for ci, (cs, cn) in enumerate(chunks):
                    self.copy("act", yl[0:cn, ci, :], Y[0:cn, ci, :], [b_Y], (), pw=[b_yh, b_X])
                for a in range(4):
                    for cg in range(4):
                        pi = 4 + (pc % 2)
                        pc += 1
                        nmm = 0
                        for ci, (cs, cn) in enumerate(chunks):
                            for part in (yh, yl):
                                self.mm(self.ps[pi][:, :], STb[0:cn, ci, a * 128:(a + 1) * 128], part[0:cn, ci, cg * 512:(cg + 1) * 512],
                                        nmm == 0, nmm == 3, [b_yh, b_X], [self.pb[pi]])
                                nmm += 1
                        oi = oc % 2
                        oc += 1
                        self.copy("act" if oi else "dve", ot[oi], self.ps[pi][:, :], [self.pb[pi]], [b_ot[oi]])
                        R.dma("pool", self.MO[r0 + a * 128:r0 + (a + 1) * 128, cg * 512:(cg + 1) * 512], ot[oi], [b_ot[oi]], (), [self.b_mo])
        R.barrier()
        self.b_ms = Buf()
        self.allreduce8(self.MO, self.MT8, self.MS, [self.b_mo], self.b_ms, esize=4)
        self.b_ff = Buf()

        def get_ff(e):
            r = self.dynreg(e, "R8")
            return e.dma_start(out=self.FF.rearrange("(p a) c -> p (a c)", p=128),
                               in_=self.MS.rearrange("(s r) c -> s r c", s=8)[bass.ds(r, 1), :, :].rearrange("s (p a) c -> p (s a c)", p=128))
        self.dyn("sp", get_ff, [self.b_ms], [self.b_ff])
        R.barrier()

    def l1_final(self):
        R, A = self.R, self.A
        A.reset()
        gt = A.alloc([D], F32)
        bt = A.alloc([D], F32)
        b_gb = Buf()
        R.dma("sp", gt, self.vecs[6:7, :].partition_broadcast(128), (), (), [b_gb])
        R.dma("sp", bt, self.vecs[7:8, :].partition_broadcast(128), (), (), [b_gb])
        hm = [A.alloc([D], F32) for _ in range(2)]
        ff = [A.alloc([D], F32) for _ in range(2)]
        b_h = [Buf(), Buf()]
        b_f = [Buf(), Buf()]
        st6 = A.alloc([4, 6], F32)
        mv = A.alloc([2], F32)
        rs = A.alloc([4], F32)
        last = []
        for tt in range(16):
            k = tt % 2
            R.dma("sp", hm[k], self.HM[tt * 128:(tt + 1) * 128, :], [self.b_hm], [b_h[k]])
            R.dma("sp", ff[k], self.FF[tt * 128:(tt + 1) * 128, :], [self.b_ff], [b_f[k]])
            self.stt("dve", ff[k], hm[k], ALPHA, ff[k], ALU.mult, ALU.add, [b_h[k]], [b_f[k]])
            self.ln_inplace(ff[k], gt, bt, b_f[k], (st6, mv, rs), b_gb)
            last.append(R.dma("pool", self.y[tt * 128:(tt + 1) * 128, :], ff[k], [b_f[k]], ()))
        return last


def _rel_bucket_np(d):
    d = np.maximum(d, 0)
    exact = 16
    nf = np.maximum(d, 1).astype(np.float32)
    large = exact + (np.log(nf / exact) / math.log(128 / exact) * (32 - exact)).astype(np.int32)
    large = np.minimum(large, 31)
    return np.where(d < exact, d, large)


def _consts_for(j, core):
    c = np.zeros((128, CSTW), np.float32)
    p = np.arange(128)
    c[:, 0:128] = (p[:, None] <= p[None, :])
    g = np.arange(64)
    c[0:64, 128:192] = (g[:, None] < g[None, :])
    c[127, 192:320] = 1.0
    c[0:64, 320:448] = 1.0
    for l in range(4):
        n = 4 * (l + 1)
        i = TS[j][l]
        ws = i + 1 - n
        nvirt = 2 * max(0, -ws)
        for u in range(2):
            wb_own = 2 * (n - 1) + u
            row = np.zeros(32, np.float32)
            row[wb_own:] = -1e30
            row[:nvirt] = -1e30
            c[:, 448 + (2 * l + u) * 32:448 + (2 * l + u + 1) * 32] = row[None, :]
    for wb in range(32):
        c[wb, 832 + wb * 128:832 + (wb + 1) * 128] = 1.0
    for l in range(4):
        n = 4 * (l + 1)
        ws = TS[j][l] + 1 - n
        for a in range(max(0, -ws)):
            c[:, 4928 + l * 16 + a] = -1e30
    c[:, 4992 + core] = 1.0
    c[:, 5120:5376] = np.arange(256, dtype=np.float32)[None, :]
    return c


_CACHE = {}


def kernel(**inp):
    stop_after = inp.pop("_stop_after", None)
    x = np.asarray(inp["x"], np.float32)
    F0 = inp["ev_ffn_w1"].shape[-1]
    FE = inp["od_exp_w1"].shape[-1]
    key = (F0, FE, stop_after)
    if key not in _CACHE:
        pr = Prog(F0, FE, stop_after)
        pr.build()
        _CACHE[key] = pr
    pr = _CACHE[key]
    mats = {
        "w_in0": inp["ev_w_in"][0], "w_out0": inp["ev_w_out"][0], "w1": inp["ev_ffn_w1"][0], "w3": inp["ev_ffn_w3"][0],
        "w2": inp["ev_ffn_w2"][0], "w_in1": inp["od_w_in"][0], "w_uq": inp["od_w_uq"][0], "w_qidx": inp["od_w_qidx"][0],
        "w_uk": np.transpose(inp["od_w_uk"][0], (1, 0, 2)).reshape(512, D),
        "w_uv": np.transpose(inp["od_w_uv"][0], (1, 0, 2)).reshape(512, D),
        "w_out1": inp["od_w_out"][0], "router": inp["od_router"][0],
    }
    flat = np.zeros(pr.NR * 1024, np.float32)
    for name, K, N in pr.shapes:
        off = pr.lay[name][0]
        flat[off:off + K * N] = np.asarray(mats[name], np.float32).reshape(-1)
    flat = flat.reshape(pr.NR, 1024)
    vecs = np.zeros((16, D), np.float32)
    for i, nm in enumerate(["ev_ln1_g", "ev_ln1_b", "ev_ln2_g", "ev_ln2_b", "od_ln1_g", "od_ln1_b", "od_ln2_g", "od_ln2_b"]):
        vecs[i] = inp[nm][0]
    vecs[8, :512] = inp["od_q_norm_g"][0]
    vecs[9, :512] = inp["od_kv_norm_g"][0]
    table = np.asarray(inp["rel_table"], np.float32)
    tab31 = table[31:32, :].copy()
    s_l = np.arange(128)[:, None]
    yp = np.arange(384)[None, :]
    dd = yp - 128 - s_l
    bk = np.where(dd >= 0, _rel_bucket_np(dd), 31)
    tbraw = np.ascontiguousarray(np.transpose(table[bk, :], (0, 2, 1))).reshape(128, 16 * 384)
    in_maps = []
    for c in range(8):
        b, j = divmod(c, 4)
        xl = np.concatenate([x[b, t * 512:(t + 1) * 512] for t in TS[j]], 0)
        in_maps.append({
            "x_loc": np.ascontiguousarray(xl),
            "wsh": np.ascontiguousarray(flat[c * pr.NRS:(c + 1) * pr.NRS]),
            "vecs": vecs, "bfor": np.asarray(inp["ev_b_forget"], np.float32).reshape(1, 8),
            "tab31": tab31, "tbraw": tbraw, "cst": _consts_for(j, c),
            "ew1": np.ascontiguousarray(inp["od_exp_w1"][0, c]), "ew3": np.ascontiguousarray(inp["od_exp_w3"][0, c]),
            "ew2": np.ascontiguousarray(inp["od_exp_w2"][0, c]),
        })
    res = run_bass_kernel_spmd(pr.nc, in_maps, core_ids=list(range(8)))
    out = np.zeros((2, T, D), np.float32)
    for c in range(8):
        b, j = divmod(c, 4)
        yl = np.asarray(res.results[c]["y"], np.float32)
        for l, t in enumerate(TS[j]):
            out[b, t * 512:(t + 1) * 512] = yl[l * 512:(l + 1) * 512]
    return out
```
